# Optimizing a Trainium2 kernel written in Bass

```python
import jax, jax.numpy as jnp
from jax import lax
import numpy as np

D_MODEL = 2048
BATCH = 4
SEQ = 2048
DEPTH = 1
DEC_BATCH = 128
DEC_SEQ = 8
PAST_LEN = 16384
PAGE_SIZE = 128

D_MIX = 2 * D_MODEL
GATE_WIDTH = D_MIX // 2
GATE_HEADS = 16
GATE_HEAD_DIM = GATE_WIDTH // GATE_HEADS
CHUNK = 128
SSM_WIDTH = D_MIX - GATE_WIDTH
SSM_HEAD_DIM = 64
SSM_HEADS = SSM_WIDTH // SSM_HEAD_DIM
SSM_GROUPS = 4
SSM_STATE = 128
SSM_CONV = 4
SSM_CHUNK = 128
SSM_CONV_DIM = SSM_WIDTH + 2 * SSM_GROUPS * SSM_STATE
IN_DIM = 2 * GATE_WIDTH + SSM_WIDTH + SSM_CONV_DIM + SSM_HEADS
D_FF = 11 * D_MODEL // 4
FFN_CONV = 3
EPS = 1e-6

kernel_name = "hybrid_chunkmlp_ssd_convffn_step"


def rms_norm(x, w):
    xf = x.astype(jnp.float32)
    y = xf * lax.rsqrt(jnp.mean(xf * xf, axis=-1, keepdims=True) + EPS)
    return (y * w.astype(jnp.float32)).astype(x.dtype)


def layer_norm(x, g, b):
    xf = x.astype(jnp.float32)
    mu = jnp.mean(xf, axis=-1, keepdims=True)
    xc = xf - mu
    y = xc * lax.rsqrt(jnp.mean(xc * xc, axis=-1, keepdims=True) + EPS)
    return (y * g.astype(jnp.float32) + b.astype(jnp.float32)).astype(x.dtype)


def pad_seq(a, pad):
    return jnp.pad(a, [(0, 0), (0, pad)] + [(0, 0)] * (a.ndim - 2))


def causal_depthwise_conv(x, prev, w, b):
    k = w.shape[0]
    xp = jnp.concatenate([prev.astype(x.dtype), x], axis=1)
    out = lax.conv_general_dilated(xp, w[:, None, :].astype(x.dtype), (1,), "VALID",
                                   dimension_numbers=("NWC", "WIO", "NWC"),
                                   feature_group_count=x.shape[-1])
    return out + b.astype(x.dtype), xp[:, xp.shape[1] - (k - 1):]


def chunk_spatial_gate(a_u, a_v, ln_g, ln_b, w_s, b_s):
    bsz, L, _ = a_u.shape
    u = jax.nn.gelu(a_u, approximate=False)
    v = layer_norm(jax.nn.gelu(a_v, approximate=False), ln_g, ln_b)
    v = v.reshape(bsz, L, GATE_HEADS, GATE_HEAD_DIM)
    q = min(CHUNK, L)
    n = -(-L // q)
    vp = pad_seq(v, n * q - L).reshape(bsz, n, q, GATE_HEADS, GATE_HEAD_DIM)
    mask = np.tril(np.ones((q, q), dtype=bool))
    w = jnp.where(mask[None], w_s[:, :q, :q], 0).astype(v.dtype)
    s = jnp.einsum("hts,bnshd->bnthd", w, vp) + b_s[:, :q].T[None, None, :, :, None].astype(v.dtype)
    s = s.reshape(bsz, n * q, GATE_WIDTH)[:, :L]
    start = ((L - 1) // CHUNK) * CHUNK
    return u * s, v[:, start:]


def ssd_scan(x, dt, a, bm, cm, init_state):
    bsz, L, H, P = x.shape
    G, N = bm.shape[2], bm.shape[3]
    R = H // G
    q = min(SSM_CHUNK, L)
    nc = -(-L // q)
    pad = nc * q - L
    x, dt, bm, cm = pad_seq(x, pad), pad_seq(dt, pad), pad_seq(bm, pad), pad_seq(cm, pad)
    xdt = (x * dt[..., None]).reshape(bsz, nc, q, G, R, P)
    da_cs = jnp.cumsum((dt * a).reshape(bsz, nc, q, G, R), axis=2)
    bc = bm.reshape(bsz, nc, q, G, N)
    cc = cm.reshape(bsz, nc, q, G, N)
    seg = da_cs[:, :, :, None] - da_cs[:, :, None, :]
    mask = np.tril(np.ones((q, q), dtype=bool))[None, None, :, :, None, None]
    decay = jnp.exp(jnp.where(mask, seg, -jnp.inf))
    cb = jnp.einsum("bctgn,bcsgn->bctsg", cc, bc)
    y_diag = jnp.einsum("bctsgr,bcsgrp->bctgrp", cb[..., None] * decay, xdt)
    decay_states = jnp.exp(da_cs[:, :, -1:] - da_cs)
    states = jnp.einsum("bcsgn,bcsgr,bcsgrp->bcgrpn", bc, decay_states, xdt)
    init = init_state.reshape(bsz, 1, G, R, P, N)
    states = jnp.concatenate([init, states], axis=1)
    tot_cs = jnp.cumsum(jnp.pad(da_cs[:, :, -1], ((0, 0), (1, 0), (0, 0), (0, 0))), axis=1)
    seg_c = tot_cs[:, :, None] - tot_cs[:, None, :]
    mask_c = np.tril(np.ones((nc + 1, nc + 1), dtype=bool))[None, :, :, None, None]
    chunk_decay = jnp.exp(jnp.where(mask_c, seg_c, -jnp.inf))
    new_states = jnp.einsum("bzcgr,bcgrpn->bzgrpn", chunk_decay, states)
    states_in, final = new_states[:, :-1], new_states[:, -1]
    y_off = jnp.einsum("bctgn,bcgrpn,bctgr->bctgrp", cc, states_in, jnp.exp(da_cs))
    y = (y_diag + y_off).reshape(bsz, nc * q, H, P)[:, :L]
    return y, final.reshape(bsz, H, P, N)


def ssd_mixer(z, xbc, dt_raw, ssm_state, conv_state, conv_w, conv_b, dt_bias, a_log, d_skip, norm_w):
    bsz, L, _ = z.shape
    f32 = jnp.float32
    xbc, new_conv = causal_depthwise_conv(xbc, conv_state, conv_w, conv_b)
    xbc = jax.nn.silu(xbc)
    gn = SSM_GROUPS * SSM_STATE
    xs = xbc[..., :SSM_WIDTH].reshape(bsz, L, SSM_HEADS, SSM_HEAD_DIM).astype(f32)
    bm = xbc[..., SSM_WIDTH:SSM_WIDTH + gn].reshape(bsz, L, SSM_GROUPS, SSM_STATE).astype(f32)
    cm = xbc[..., SSM_WIDTH + gn:].reshape(bsz, L, SSM_GROUPS, SSM_STATE).astype(f32)
    dt = jax.nn.softplus(dt_raw.astype(f32) + dt_bias.astype(f32))
    a = -jnp.exp(a_log.astype(f32))
    y, new_state = ssd_scan(xs, dt, a, bm, cm, ssm_state.astype(f32))
    y = y + d_skip.astype(f32)[:, None] * xs
    y = y.reshape(bsz, L, SSM_WIDTH) * jax.nn.silu(z.astype(f32))
    yg = y.reshape(bsz, L, SSM_GROUPS, SSM_WIDTH // SSM_GROUPS)
    yg = yg * lax.rsqrt(jnp.mean(yg * yg, axis=-1, keepdims=True) + EPS)
    y = yg.reshape(bsz, L, SSM_WIDTH) * norm_w.astype(f32)
    return y.astype(z.dtype), new_state.astype(ssm_state.dtype), new_conv


def conv_ffn(h, conv_state, w_up, conv_w, conv_b, w_down):
    up = h @ w_up
    up, new_conv = causal_depthwise_conv(up, conv_state, conv_w, conv_b)
    g, u = jnp.split(up, 2, axis=-1)
    return (jax.nn.gelu(g, approximate=True) * u) @ w_down, new_conv


def decoder_layer(x, ssm_state, ssm_conv_state, ffn_conv_state, p):
    h = rms_norm(x, p["norm_mix_pre"])
    proj = h @ p["w_in"]
    o1 = GATE_WIDTH
    o2 = 2 * GATE_WIDTH
    o3 = o2 + SSM_WIDTH
    o4 = o3 + SSM_CONV_DIM
    a_u, a_v, z, xbc, dt_raw = proj[..., :o1], proj[..., o1:o2], proj[..., o2:o3], proj[..., o3:o4], proj[..., o4:]
    gate_out, chunk_v = chunk_spatial_gate(a_u, a_v, p["gate_ln_g"], p["gate_ln_b"], p["gate_w_s"], p["gate_b_s"])
    ssm_out, new_ssm, new_ssm_conv = ssd_mixer(z, xbc, dt_raw, ssm_state, ssm_conv_state,
                                              p["ssm_conv_w"], p["ssm_conv_b"], p["ssm_dt_bias"],
                                              p["ssm_a_log"], p["ssm_d"], p["ssm_norm_w"])
    mix = jnp.concatenate([gate_out, ssm_out], axis=-1) @ p["w_out"]
    x = x + rms_norm(mix, p["norm_mix_post"])
    f, new_ffn_conv = conv_ffn(rms_norm(x, p["norm_ffn_pre"]), ffn_conv_state,
                               p["ffn_w_up"], p["ffn_conv_w"], p["ffn_conv_b"], p["ffn_w_down"])
    x = x + rms_norm(f, p["norm_ffn_post"])
    return x, new_ssm, new_ssm_conv, new_ffn_conv, chunk_v


def setup_inputs(seed: int = 0) -> dict:
    key = jax.random.key(seed)
    ks = jax.random.split(key, 32)
    nrm = jax.random.normal
    f32 = jnp.float32
    dt0 = jnp.exp(jax.random.uniform(ks[12], (DEPTH, SSM_HEADS), f32, np.log(1e-3), np.log(1e-1)))
    return {
        "x_prompt": nrm(ks[0], (BATCH, SEQ, D_MODEL), f32),
        "x_sample": nrm(ks[1], (DEC_BATCH, DEC_SEQ, D_MODEL), f32),
        "state_ssm": 0.1 * nrm(ks[2], (DEPTH, DEC_BATCH, SSM_HEADS, SSM_HEAD_DIM, SSM_STATE), f32),
        "state_ssm_conv": nrm(ks[3], (DEPTH, DEC_BATCH, SSM_CONV - 1, SSM_CONV_DIM), f32),
        "state_ffn_conv": nrm(ks[4], (DEPTH, DEC_BATCH, FFN_CONV - 1, 2 * D_FF), f32),
        "norm_mix_pre": 1.0 + 0.1 * nrm(ks[5], (DEPTH, D_MODEL), f32),
        "w_in": nrm(ks[6], (DEPTH, D_MODEL, IN_DIM), f32) * D_MODEL ** -0.5,
        "gate_ln_g": 1.0 + 0.1 * nrm(ks[7], (DEPTH, GATE_WIDTH), f32),
        "gate_ln_b": 0.01 * nrm(ks[8], (DEPTH, GATE_WIDTH), f32),
        "gate_w_s": nrm(ks[9], (DEPTH, GATE_HEADS, CHUNK, CHUNK), f32) * CHUNK ** -0.5,
        "gate_b_s": 1.0 + 0.1 * nrm(ks[10], (DEPTH, GATE_HEADS, CHUNK), f32),
        "ssm_conv_w": nrm(ks[11], (DEPTH, SSM_CONV, SSM_CONV_DIM), f32) * SSM_CONV ** -0.5,
        "ssm_conv_b": 0.01 * nrm(ks[13], (DEPTH, SSM_CONV_DIM), f32),
        "ssm_dt_bias": dt0 + jnp.log(-jnp.expm1(-dt0)),
        "ssm_a_log": jnp.log(jax.random.uniform(ks[14], (DEPTH, SSM_HEADS), f32, 1.0, 16.0)),
        "ssm_d": 1.0 + 0.1 * nrm(ks[15], (DEPTH, SSM_HEADS), f32),
        "ssm_norm_w": 1.0 + 0.1 * nrm(ks[16], (DEPTH, SSM_WIDTH), f32),
        "w_out": nrm(ks[17], (DEPTH, D_MIX, D_MODEL), f32) * D_MIX ** -0.5,
        "norm_mix_post": 1.0 + 0.1 * nrm(ks[18], (DEPTH, D_MODEL), f32),
        "norm_ffn_pre": 1.0 + 0.1 * nrm(ks[19], (DEPTH, D_MODEL), f32),
        "ffn_w_up": nrm(ks[20], (DEPTH, D_MODEL, 2 * D_FF), f32) * D_MODEL ** -0.5,
        "ffn_conv_w": nrm(ks[21], (DEPTH, FFN_CONV, 2 * D_FF), f32) * FFN_CONV ** -0.5,
        "ffn_conv_b": 0.01 * nrm(ks[22], (DEPTH, 2 * D_FF), f32),
        "ffn_w_down": nrm(ks[23], (DEPTH, D_FF, D_MODEL), f32) * D_FF ** -0.5,
        "norm_ffn_post": 1.0 + 0.1 * nrm(ks[24], (DEPTH, D_MODEL), f32),
    }


def reference(x_prompt, x_sample, state_ssm, state_ssm_conv, state_ffn_conv,
              norm_mix_pre, w_in, gate_ln_g, gate_ln_b, gate_w_s, gate_b_s,
              ssm_conv_w, ssm_conv_b, ssm_dt_bias, ssm_a_log, ssm_d, ssm_norm_w,
              w_out, norm_mix_post, norm_ffn_pre, ffn_w_up, ffn_conv_w, ffn_conv_b,
              ffn_w_down, norm_ffn_post):
    bp = x_prompt.shape[0]
    yp, ys = x_prompt, x_sample
    sp_ssm, sp_sconv, sp_fconv, sp_v = [], [], [], []
    ss_ssm, ss_sconv, ss_fconv, ss_v = [], [], [], []
    for i in range(DEPTH):
        p = dict(norm_mix_pre=norm_mix_pre[i], w_in=w_in[i], gate_ln_g=gate_ln_g[i], gate_ln_b=gate_ln_b[i],
                 gate_w_s=gate_w_s[i], gate_b_s=gate_b_s[i], ssm_conv_w=ssm_conv_w[i], ssm_conv_b=ssm_conv_b[i],
                 ssm_dt_bias=ssm_dt_bias[i], ssm_a_log=ssm_a_log[i], ssm_d=ssm_d[i], ssm_norm_w=ssm_norm_w[i],
                 w_out=w_out[i], norm_mix_post=norm_mix_post[i], norm_ffn_pre=norm_ffn_pre[i],
                 ffn_w_up=ffn_w_up[i], ffn_conv_w=ffn_conv_w[i], ffn_conv_b=ffn_conv_b[i],
                 ffn_w_down=ffn_w_down[i], norm_ffn_post=norm_ffn_post[i])
        z_ssm = jnp.zeros((bp, SSM_HEADS, SSM_HEAD_DIM, SSM_STATE), state_ssm.dtype)
        z_sconv = jnp.zeros((bp, SSM_CONV - 1, SSM_CONV_DIM), x_prompt.dtype)
        z_fconv = jnp.zeros((bp, FFN_CONV - 1, 2 * D_FF), x_prompt.dtype)
        yp, a1, a2, a3, a4 = decoder_layer(yp, z_ssm, z_sconv, z_fconv, p)
        sp_ssm.append(a1); sp_sconv.append(a2); sp_fconv.append(a3); sp_v.append(a4)
        ys, b1, b2, b3, b4 = decoder_layer(ys, state_ssm[i], state_ssm_conv[i], state_ffn_conv[i], p)
        ss_ssm.append(b1); ss_sconv.append(b2); ss_fconv.append(b3); ss_v.append(b4)
    return (yp, ys,
            jnp.stack(sp_ssm), jnp.stack(sp_sconv), jnp.stack(sp_fconv), jnp.stack(sp_v),
            jnp.stack(ss_ssm), jnp.stack(ss_sconv), jnp.stack(ss_fconv), jnp.stack(ss_v))
```

```python
import numpy as np
import concourse.bass as bass
import concourse.mybir as mybir
from concourse.bass_utils import run_bass_kernel_spmd

F32 = mybir.dt.float32
BF16 = mybir.dt.bfloat16
AF = mybir.ActivationFunctionType
ALU = mybir.AluOpType

D = 2048
IN_DIM = 9248
DFF = 5632
EPS = 1e-6
NLITE = 7
NMAIN = 9
NEG = -30000.0
PENG = "pool"


def _esz(dt):
    s = str(dt)
    if "bfloat16" in s or "float16" in s:
        return 2
    return 4


class Op:
    __slots__ = ("eng", "fn", "chan", "deps", "signal", "sigval", "is_dma", "idx")

    def __init__(self, eng, fn, chan):
        self.eng = eng
        self.fn = fn
        self.chan = chan
        self.deps = {}
        self.signal = False
        self.sigval = 0
        self.is_dma = chan is not None


class Sched:
    ENGS = ("pe", "act", "dve", "pool", "sp")

    def __init__(self, nc):
        self.nc = nc
        self.ops = {e: [] for e in self.ENGS}
        self.track = {}
        self.chan_count = {}
        self.ignore = set()

    def region(self, ap):
        t = ap.tensor
        name = t.name
        if name in self.ignore:
            return None
        e = _esz(ap.dtype)
        shape = list(t.shape)
        rowb = 1
        for s in shape[1:]:
            rowb *= s
        rowb *= _esz(t.dtype)
        off = int(ap.offset) * e
        dims = list(ap.ap)
        p0 = off // rowb
        f0 = off % rowb
        npart = dims[0][1]
        lo = f0
        hi = f0
        for (st, cn) in dims[1:]:
            ext = st * (cn - 1) * e
            if ext < 0:
                lo += ext
            else:
                hi += ext
        hi += e
        if name.startswith("psum_"):
            lo = (lo // 2048) * 2048
            hi = ((hi + 2047) // 2048) * 2048
            return name, (0, 128, lo, hi)
        return name, (p0, p0 + npart, lo, hi)

    @staticmethod
    def _ov(a, b):
        return a[0] < b[1] and b[0] < a[1] and a[2] < b[3] and b[2] < a[3]

    @staticmethod
    def _contains(a, b):
        return a[0] <= b[0] and a[1] >= b[1] and a[2] <= b[2] and a[3] >= b[3]

    def add(self, eng, fn, reads=(), writes=(), chan=None):
        o = Op(eng, fn, chan)
        okey = chan if chan is not None else eng
        accs = []
        for ap in reads:
            r = self.region(ap)
            if r is not None:
                accs.append((False, r))
        for ap in writes:
            r = self.region(ap)
            if r is not None:
                accs.append((True, r))
        for (isw, (name, reg)) in accs:
            lst = self.track.get(name)
            if not lst:
                continue
            for ent in lst:
                (w2, reg2, key2, op2) = ent
                if not (isw or w2):
                    continue
                if not self._ov(reg, reg2):
                    continue
                if op2 is o:
                    continue
                need = True
                if (not op2.is_dma) and (not o.is_dma) and op2.eng == o.eng:
                    if o.eng == "pe":
                        need = False
                if need:
                    k = key2
                    prev = o.deps.get(k)
                    if prev is None or prev.idx < op2.idx:
                        o.deps[k] = op2
        o.idx = len(self.ops[eng])
        if o.is_dma:
            c = self.chan_count.get(chan, 0) + 1
            self.chan_count[chan] = c
            o.sigval = 16 * c
            o.idx = c
            o.signal = True
        for (isw, (name, reg)) in accs:
            lst = self.track.setdefault(name, [])
            if isw:
                lst[:] = [en for en in lst if not self._contains(reg, en[1])]
                lst.append((True, reg, okey, o))
            else:
                rep = False
                for i, en in enumerate(lst):
                    if (not en[0]) and en[2] == okey and en[1] == reg:
                        lst[i] = (False, reg, okey, o)
                        rep = True
                        break
                if not rep:
                    lst.append((False, reg, okey, o))
        for d in o.deps.values():
            d.signal = True
        self.ops[eng].append(o)
        return o

    def mark(self, label):
        if not hasattr(self, "marks"):
            self.marks = []
        self.marks.append((label, sum(1 for o in self.ops["pe"] if o.fn is not None)))

    def barrier(self, label=""):
        if not hasattr(self, "marks"):
            self.marks = []
        self.marks.append((label, sum(1 for o in self.ops["pe"] if o.fn is not None)))
        last = {}
        for e in self.ENGS:
            for o in reversed(self.ops[e]):
                if (not o.is_dma) and o.fn is not None:
                    last[e] = o
                    break
        lastdma = {}
        for e in self.ENGS:
            for o in self.ops[e]:
                if o.is_dma:
                    lastdma[o.chan] = o
        for e in self.ENGS:
            o = Op(e, None, None)
            o.idx = len(self.ops[e])
            for e2, l in last.items():
                if e2 != e or e != "pe":
                    o.deps[e2] = l
                    l.signal = True
            for c, l in lastdma.items():
                o.deps[c] = l
            self.ops[e].append(o)

    def emit(self, block, sems_ctx):
        nc = self.nc
        sem_eng = {e: sems_ctx["eng_" + e] for e in ("pe", "act", "dve", "pool")}
        sem_chan = {c: sems_ctx["ch_" + c] for c in self.chan_count}
        for e in ("pe", "act", "dve", "pool"):
            c = 0
            for o in self.ops[e]:
                if (not o.is_dma) and o.signal and o.fn is not None:
                    c += 1
                    o.sigval = c
                elif (not o.is_dma) and o.signal and o.fn is None:
                    o.sigval = c

        def run(e, engobj):
            waited = {}
            for o in self.ops[e]:
                for d in o.deps.values():
                    if d.is_dma:
                        sem = sem_chan[d.chan]
                        val = d.sigval
                        if d.chan == "const":
                            val = 16 * self.chan_count["const"]
                        key = "c" + d.chan
                    else:
                        sem = sem_eng[d.eng]
                        val = d.sigval
                        key = "e" + d.eng
                    if val <= 0:
                        continue
                    if waited.get(key, 0) < val:
                        engobj.wait_ge(sem, val)
                        waited[key] = val
                if o.fn is None:
                    continue
                ins = o.fn(engobj)
                if o.is_dma:
                    ins.then_inc(sem_chan[o.chan], 16)
                elif o.signal:
                    ins.then_inc(sem_eng[e], 1)

        @block.tensor
        def _(eng):
            run("pe", eng)

        @block.scalar
        def _(eng):
            run("act", eng)

        @block.vector
        def _(eng):
            run("dve", eng)

        @block.gpsimd
        def _(eng):
            run("pool", eng)

        @block.sync
        def _(eng):
            run("sp", eng)


def mk(base, free):
    d0 = list(base.ap)[0]
    return bass.AP(base.tensor, base.offset, [[d0[0], d0[1]]] + [[s, c] for (s, c) in free])


class KB:
    def __init__(self):
        nc = bass.Bass("TRN2", target_bir_lowering=False)
        self.nc = nc
        self.S = Sched(nc)
        self.din = {}
        self.dout = {}

    def inp(self, name, shape):
        t = self.nc.dram_tensor(name, list(shape), F32, kind="ExternalInput").ap()
        self.S.ignore.add(name)
        self.din[name] = t
        return t

    def outp(self, name, shape):
        t = self.nc.dram_tensor(name, list(shape), F32, kind="ExternalOutput").ap()
        self.S.ignore.add(name)
        self.dout[name] = t
        return t

    def mm(self, out, lhsT, rhs, start=True, stop=True):
        rd = [lhsT, rhs]
        return self.S.add("pe", lambda e: e.matmul(out, lhsT, rhs, start=start, stop=stop), rd, [out])

    def tr(self, out, in_, ident):
        return self.S.add("pe", lambda e: e.transpose(out, in_, ident), [in_, ident], [out])

    def act(self, out, in_, func, bias=None, scale=None, accum=None, eng="act"):
        rd = [in_]
        kw = {}
        if bias is not None:
            kw["bias"] = bias
            if not isinstance(bias, (int, float)):
                rd.append(bias)
        if scale is not None:
            kw["scale"] = scale
            if not isinstance(scale, (int, float)):
                rd.append(scale)
        wr = [out]
        if accum is not None:
            kw["accum_out"] = accum
            wr.append(accum)
        return self.S.add("act", lambda e: e.activation(out, in_, func, **kw), rd, wr)

    def tt(self, out, a, b, op, eng="dve"):
        return self.S.add(eng, lambda e: e.tensor_tensor(out, a, b, op), [a, b], [out])

    def ts(self, out, a, s1, s2, op0, op1=None, eng="dve"):
        rd = [a]
        for s in (s1, s2):
            if s is not None and not isinstance(s, (int, float)):
                rd.append(s)
        if op1 is None:
            return self.S.add(eng, lambda e: e.tensor_scalar(out, a, s1, None, op0), rd, [out])
        return self.S.add(eng, lambda e: e.tensor_scalar(out, a, s1, s2, op0, op1), rd, [out])

    def stt(self, out, a, s, b, op0, op1, eng="dve"):
        rd = [a, b]
        if not isinstance(s, (int, float)):
            rd.append(s)
        return self.S.add(eng, lambda e: e.scalar_tensor_tensor(out, a, s, b, op0, op1), rd, [out])

    def rsqrt(self, out, in_, scale, eps):
        self.act(out, in_, AF.Ln, bias=eps, scale=scale)
        self.act(out, out, AF.Exp, scale=-0.5)

    def cp(self, out, a, eng="dve"):
        if eng == "act":
            return self.S.add("act", lambda e: e.copy(out, a), [a], [out])
        return self.S.add(eng, lambda e: e.tensor_copy(out, a), [a], [out])

    def ms(self, out, val, eng="dve"):
        return self.S.add(eng, lambda e: e.memset(out, val), [], [out])

    def dma(self, out, in_, chan, eng="sp"):
        return self.S.add(eng, lambda e: e.dma_start(out=out, in_=in_), [in_], [out], chan=chan)


def build(dbg_stop=10 ** 9):
    K = KB()
    nc = K.nc
    S = K.S
    xpre = K.inp("xpre", [NLITE * 128, D])
    xmain = K.inp("xmain", [NMAIN * 128, D])
    xsamp = K.inp("xsamp", [128, D])
    flag = K.inp("flag", [128, 1])
    sstT = K.inp("sstT", [16, 128, 2048])
    sconvT = K.inp("sconvT", [128, 24, 48])
    fconvT = K.inp("fconvT", [128, 88, 32])
    w_in = K.inp("w_in", [D, IN_DIM])
    w_out = K.inp("w_out", [2 * D, D])
    w_up = K.inp("w_up", [D, 2 * DFF])
    w_down = K.inp("w_down", [DFF, D])
    vecs = K.inp("vecs", [128, 5, 16])
    nmo_d = K.inp("nmo", [128, D])
    nfo_d = K.inp("nfo", [128, D])
    wsTp_d = K.inp("wsTp", [128, 16, 128])
    wsTs_d = K.inp("wsTs", [128, 16, 128])
    bsp_d = K.inp("bsp", [128, D])
    bss_d = K.inp("bss", [128, D])
    scw_d = K.inp("scw", [128, 24, 5])
    fcw_d = K.inp("fcw", [128, 88, 4])
    h32_d = K.inp("h32", [128, 3, 32])
    cmat_d = K.inp("cmat", [128, 10, 128])

    ymain = K.outp("ymain", [NMAIN * 128, D])
    ysamp = K.outp("ysamp", [128, D])
    o_sst_p = K.outp("o_sst_p", [128, 2048])
    o_sconv_p = K.outp("o_sconv_p", [128, 24, 3])
    o_fconv_p = K.outp("o_fconv_p", [128, 88, 2])
    o_cv_p = K.outp("o_cv_p", [128, 16, 128])
    o_sst_s = K.outp("o_sst_s", [16, 128, 2048])
    o_sconv_s = K.outp("o_sconv_s", [128, 24, 48])
    o_fconv_s = K.outp("o_fconv_s", [128, 88, 32])
    o_cv_s = K.outp("o_cv_s", [128, 16, 128])
    xmid = nc.dram_tensor("xmid_scr", [(NMAIN + 1) * 128, D], F32, kind="Internal").ap()

    NSLOT = 2
    import contextlib

    with contextlib.ExitStack() as es:
        uid = [0]

        def un(name):
            uid[0] += 1
            return "t%d_%s" % (uid[0], name)

        def sb(name, shape, dt=F32):
            return es.enter_context(nc.sbuf_tensor(un(name), list(shape), dt))

        wslot = [sb("wslot%d" % i, [128, 8192], BF16) for i in range(NSLOT)]
        cm = sb("cm", [128, 10, 128])
        identb = sb("identb", [128, 128], BF16)
        Ub = {"p": sb("Ubp", [128, 128], BF16), "s": sb("Ubs", [128, 128], BF16)}
        Mnb = {"p": sb("Mnbp", [128, 128], BF16), "s": sb("Mnbs", [128, 128], BF16)}
        onesf = sb("onesf", [128, 128])
        onesb = sb("onesb", [128, 128], BF16)
        vec = sb("vec", [128, 5, 16])
        scw = sb("scw", [128, 24, 5])
        fcw = sb("fcw", [128, 88, 4])
        h32 = sb("h32", [128, 3, 32])
        arep = sb("arep", [128, 32])
        flg = sb("flg", [128, 1])
        wsT = {}
        St = sb("St", [128, 2048])
        Sb = sb("Sb", [128, 2048], BF16)
        hist_s = sb("hist_s", [128, 24, 3])
        hist_f = sb("hist_f", [128, 88, 2])
        hT = sb("hT", [128, 16, 512], BF16)
        BIG = {}
        DBGT = {}
        small = sb("small", [128, 64])
        ps = es.enter_context(nc.psum_tensor("psum_f", [128, 6, 512], F32))
        psb = es.enter_context(nc.psum_tensor("psum_b", [128, 2, 1024], BF16))

        identf = cm[:, 0, :]
        Umat = {"p": cm[:, 1, :], "s": cm[:, 2, :]}
        Tmat = {"p": cm[:, 3, :], "s": cm[:, 4, :]}
        Mneg = {"p": cm[:, 5, :], "s": cm[:, 6, :]}
        Msk = {"p": cm[:, 7, :], "s": cm[:, 8, :]}
        selj = cm[:, 9, 0:16]
        nmp = vec[:, 0, :]
        nfp = vec[:, 1, :]
        lng = vec[:, 2, :]
        lnb = vec[:, 3, :]
        snw = vec[:, 4, :]
        dtb = h32[:, 0, :]
        alog = h32[:, 1, :]
        dsk = h32[:, 2, :]

        def bank(i, n=512):
            return ps[:, i, 0:n]

        K.dma(cm[:, :, :], cmat_d[:, :, :], "const")
        K.dma(vec[:, :, :], vecs[:, :, :], "const")
        K.dma(scw[:, :, :], scw_d[:, :, :], "const")
        K.dma(fcw[:, :, :], fcw_d[:, :, :], "const")
        K.dma(h32[:, :, :], h32_d[:, :, :], "const")
        K.dma(flg[:, :], flag[:, :], "const")
        K.cp(identb[:, :], identf)
        for kd in ("p", "s"):
            K.cp(Ub[kd][:, :], Umat[kd])
            K.cp(Mnb[kd][:, :], Mneg[kd])
        K.ms(onesf[:, :], 1.0)
        K.ms(onesb[:, :], 1.0)
        K.act(arep[:, :], alog, AF.Exp)
        K.ts(arep[:, :], arep[:, :], -1.0, None, ALU.mult)
        K.ms(St[:, :], 0.0)
        K.ms(Sb[:, :], 0.0)
        K.ms(hist_s[:, :, :], 0.0)
        K.ms(hist_f[:, :, :], 0.0)

        wq = []

        def wdesc_cols(wd, c0, ncols, nk=16, r0=0):
            return ("c", wd, c0, ncols, nk, r0)

        class WS:
            def __init__(self):
                self.seq = []
                self.seq_phase = []
                self.plan_ph = 0
                self.exec_ph = 0
                self.issued = 0
                self.pos = 0
                self.slots = [(wslot[i], "b%d" % i) for i in range(NSLOT)]
                self.extras = []
                self.last = {}
                self.assigned = {}

            def plan_phase(self):
                self.plan_ph += 1

            def plan(self, d):
                self.seq.append(d)
                self.seq_phase.append(self.plan_ph)

            def begin_phase(self, extras=()):
                self.exec_ph += 1
                self.extras = [(t, "x%d" % i) for i, t in enumerate(extras)]

            def end_phase(self):
                while self.issued < len(self.seq) and self.issued - self.pos < 4:
                    if self.seq_phase[self.issued] == self.exec_ph:
                        break
                    if not self._issue(self.issued):
                        break
                    self.issued += 1
                self.extras = []

            def _issue(self, j):
                d = self.seq[j]
                cands = list(self.slots)
                if self.seq_phase[j] == self.exec_ph:
                    cands += self.extras
                best = None
                for (t, nm) in cands:
                    lb = self.last.get(nm, -1)
                    if lb < self.pos and (best is None or lb < best[2]):
                        best = (t, nm, lb)
                if best is None:
                    return False
                slot, nm, _ = best
                self.last[nm] = j
                self.assigned[j] = slot
                off = 0
                for pi, (wd, c0, ncols, nk, r0) in enumerate(d):
                    dst = slot[:, off:off + nk * ncols].rearrange("p (k n) -> p k n", k=nk)
                    src = wd[r0:r0 + nk * 128, c0:c0 + ncols].rearrange("(k p) n -> p k n", p=128)
                    K.dma(dst, src, "w%s_%d" % (nm, pi), eng="pool")
                    off += nk * ncols
                return True

            def get(self, d):
                i = self.pos
                assert self.seq[i] == d, (i, self.seq[i], d)
                assert self.seq_phase[i] == self.exec_ph, (i, self.seq_phase[i], self.exec_ph)
                while self.issued <= i:
                    ok = self._issue(self.issued)
                    assert ok
                    self.issued += 1
                while self.issued < len(self.seq) and self.issued - i <= 8:
                    if not self._issue(self.issued):
                        break
                    self.issued += 1
                self.pos += 1
                slot = self.assigned.pop(i)
                views = []
                off = 0
                for (wd, c0, ncols, nk, r0) in d:
                    views.append(slot[:, off:off + nk * ncols].rearrange("p (k n) -> p k n", k=nk))
                    off += nk * ncols
                return views

        ws = WS()

        def alloc_extras(pb, maxn):
            n = max(0, min(maxn, (nc.sbuf_bytes_remaining - 3072) // 16384))
            return [pb("wx%d" % i, [128, 8192], BF16) for i in range(n)]

        def D_in(c0, n):
            return [(w_in, c0, n, 16, 0)]

        def plan_lite():
            ws.plan_phase()
            for b in range(5):
                ws.plan(D_in(6144 + 512 * b, 512))
            ws.plan(D_in(9216, 32))

        def plan_full():
            ws.plan_phase()
            for b in range(4):
                ws.plan(D_in(2048 + 512 * b, 512))
            for b in range(4):
                ws.plan(D_in(512 * b, 512))
            ws.plan_phase()
            for b in range(4):
                ws.plan(D_in(4096 + 512 * b, 512))
            for b in range(6):
                ws.plan(D_in(6144 + 512 * b, 512))
            ws.plan(D_in(9216, 32))
            ws.plan_phase()
            for b in range(8):
                ws.plan([(w_out, 256 * b, 256, 32, 0)])
            ws.plan_phase()
            for b in range(22):
                ws.plan([(w_up, 256 * b, 256, 16, 0), (w_up, DFF + 256 * b, 256, 16, 0)])
            ws.plan_phase()
            for b in range(8):
                for kh in range(2):
                    ws.plan([(w_down, 256 * b, 256, 22, kh * 22 * 128)])

        def ptile(src, row, dst=None, xm_row=0):
            return dict(kind="p", src=src[row * 128:(row + 1) * 128, :], dst=dst, xm=xmid[xm_row * 128:(xm_row + 1) * 128, :])

        lite_groups = [[ptile(xpre, i) for i in range(0, 4)], [ptile(xpre, i) for i in range(4, 7)]]
        full_groups = [
            [ptile(xmain, i, ymain[i * 128:(i + 1) * 128, :], i) for i in range(0, 4)],
            [ptile(xmain, i, ymain[i * 128:(i + 1) * 128, :], i) for i in range(4, 8)],
            [ptile(xmain, i, ymain[i * 128:(i + 1) * 128, :], i) for i in range(8, 9)]
            + [dict(kind="s", src=xsamp[:, :], dst=ysamp[:, :], xm=xmid[9 * 128:10 * 128, :])],
        ]
        full_groups[2][0]["last"] = True
        for g in lite_groups:
            plan_lite()
        for g in full_groups:
            plan_full()

        cnt = {"x": 0, "o": 0}

        def rms_to_T(src_tile_ap, dstT, col0, wvec, xb, xsb, from_sb=None):
            junk = xsb
            ssq = small[:, 0:1]
            K.act(junk[:, :], xb[:, :], AF.Square, accum=ssq)
            rstd = small[:, 1:2]
            K.rsqrt(rstd, ssq, 1.0 / D, EPS)
            K.ts(xsb[:, :], xb[:, :], rstd, None, ALU.mult)
            for half in range(2):
                for c in range(8):
                    cc = half * 8 + c
                    K.tr(psb[:, half, c * 128:(c + 1) * 128], xsb[:, cc * 128:(cc + 1) * 128], identb[:, :])
                for c in range(8):
                    cc = half * 8 + c
                    K.act(dstT[:, cc, col0:col0 + 128], psb[:, half, c * 128:(c + 1) * 128], AF.Identity,
                          scale=wvec[:, cc:cc + 1])

        def phase0(tiles):
            with nc.sbuf_tensor(un("p0x0"), [128, D], F32) as x0, nc.sbuf_tensor(un("p0x1"), [128, D], F32) as x1, \
                    nc.sbuf_tensor(un("p0xs"), [128, D], BF16) as xsb:
                xbs = [x0, x1]
                for i, t in enumerate(tiles):
                    xb = xbs[i % 2]
                    K.dma(xb[:, :], t["src"], "x%d" % (i % 2))
                    rms_to_T(None, hT, i * 128, nmp, xb, xsb)
                S.barrier("phase0")

        def ssd_inproj(tiles, full, L):
            T = len(tiles)
            NT = 128 * T
            Tp = sum(1 for t in tiles if t["kind"] == "p")
            NTp = 128 * Tp
            has_s = Tp < T
            raw, raws, acc, cvo = L["raw"], L["raws"], L["acc"], L["cvo"]
            def mmpair(p, W):
                pp = p % 2
                for c in (2 * p, 2 * p + 1):
                    cc = c % 4
                    bk = bank(2 * pp + (c % 2), NT)
                    for k in range(16):
                        K.mm(bk, W[:, k, cc * 128:(cc + 1) * 128], hT[:, k, 0:NT], start=(k == 0), stop=(k == 15))

            def postpair(p):
                pp = p % 2
                cs_ = [2 * p, 2 * p + 1]
                for c in cs_:
                    r = raw[pp][c % 2]
                    bi = 2 * pp + (c % 2)
                    if NTp > 0:
                        K.cp(r[:, 0:3], hist_s[:, c, :])
                        K.cp(r[:, 3:3 + NTp], bank(bi, NTp), eng="act")
                        K.cp(hist_s[:, c, :], r[:, NTp:NTp + 3])
                    if has_s:
                        rs = raws[c % 2]
                        K.cp(rs[:, :, 0:3], L["sconv"][:, c, :].rearrange("p (j k) -> p j k", k=3))
                        K.cp(rs[:, :, 3:11], ps[:, bi, NTp:NTp + 128].rearrange("p (j t) -> p j t", t=8), eng="act")
                        K.cp(L["osc"][:, c, :].rearrange("p (j k) -> p j k", k=3), rs[:, :, 8:11])
                for kk in range(4):
                    for c in cs_:
                        r = raw[pp][c % 2]
                        if NTp > 0:
                            a = acc[pp][c % 2][:, 0:NTp]
                            if kk == 0:
                                K.ts(a, r[:, 0:NTp], scw[:, c, 0:1], scw[:, c, 4:5], ALU.mult, ALU.add)
                            else:
                                K.stt(a, r[:, kk:kk + NTp], scw[:, c, kk:kk + 1], a, ALU.mult, ALU.add)
                        if has_s:
                            rs = raws[c % 2]
                            a3 = acc[pp][c % 2][:, NTp:NTp + 128].rearrange("p (j t) -> p j t", t=8)
                            if kk == 0:
                                K.ts(a3, rs[:, :, 0:8], scw[:, c, 0:1], scw[:, c, 4:5], ALU.mult, ALU.add)
                            else:
                                K.stt(a3, rs[:, :, kk:kk + 8], scw[:, c, kk:kk + 1], a3, ALU.mult, ALU.add)
                for c in cs_:
                    av = acc[pp][c % 2][:, 0:NT]
                    if c < 20:
                        dstv = cvo[pp][c % 2][:, 0:NT] if c < 16 else L["BT"][:, c - 16, 0:NT]
                        K.act(dstv, av, AF.Silu)
                        hb = c % 2
                        for i in range(T):
                            K.tr(psb[:, hb, i * 128:(i + 1) * 128], dstv[:, i * 128:(i + 1) * 128], identb[:, :])
                        if c < 16:
                            dst3 = mk(L["xs"][:, 0, c * 128:(c + 1) * 128], [(2048, T), (1, 128)])
                        else:
                            dst3 = mk(L["Bt"][:, 0, (c - 16) * 128:(c - 15) * 128], [(512, T), (1, 128)])
                        K.cp(dst3, psb[:, hb, 0:NT].rearrange("p (i t) -> p i t", t=128))
                    else:
                        K.act(L["CT"][:, c - 20, 0:NT], av, AF.Silu)

            Wcur = None
            npair = 12 if full else 10
            for p in range(npair):
                if p % 2 == 0:
                    (Wcur,) = ws.get(D_in(6144 + 512 * (p // 2), 512))
                mmpair(p, Wcur)
                if p >= 1:
                    postpair(p - 1)
            postpair(npair - 1)
            (Wd,) = ws.get(D_in(9216, 32))
            for i in range(T):
                bk = ps[:, 2, 0:32]
                for k in range(16):
                    K.mm(bk, hT[:, k, i * 128:(i + 1) * 128], Wd[:, k, :], start=(k == 0), stop=(k == 15))
                K.tt(L["dtr"][:, i, :], bk, dtb, ALU.add)

        def ssd_scalars(tiles, L):
            T = len(tiles)
            Tp = sum(1 for t in tiles if t["kind"] == "p")
            sm = L["sm"]
            fl = lambda f: sm[:, f, :, :].rearrange("p a b -> p (a b)")
            e1, dt, da, ecs, ncs, dte, etot = (fl(6), fl(0), fl(1), fl(2), fl(3), fl(4), fl(5))
            K.act(e1, L["dtr"][:, 0:T, :].rearrange("p a b -> p (a b)"), AF.Exp)
            K.act(dt, e1, AF.Ln, bias=1.0)
            K.tt(sm[:, 1, :, :], sm[:, 0, :, :], mk(arep[:, :], [(0, T), (1, 32)]), ALU.mult)
            bc = ps[:, 0, :]
            n = 32 * T
            if Tp > 0:
                K.mm(bc[:, 0:32 * Tp], Umat["p"], da[:, 0:32 * Tp])
                K.mm(bc[:, 256:256 + 32 * Tp], Tmat["p"], da[:, 0:32 * Tp])
            if Tp < T:
                K.mm(bc[:, 32 * Tp:n], Umat["s"], da[:, 32 * Tp:n])
                K.mm(bc[:, 256 + 32 * Tp:256 + n], Tmat["s"], da[:, 32 * Tp:n])
            K.act(ncs, bc[:, 0:n], AF.Identity, scale=-1.0)
            K.act(ecs, bc[:, 0:n], AF.Exp)
            K.act(etot, bc[:, 256:256 + n], AF.Identity)
            K.tt(dte, etot, ncs, ALU.add)
            K.act(dte, dte, AF.Exp)
            K.act(etot, etot, AF.Exp)
            dah = L["dah"]
            hi = dah[:, 0, :, :].rearrange("p a b -> p (a b)")
            lo = dah[:, 1, :, :].rearrange("p a b -> p (a b)")
            K.cp(hi, da)
            K.tt(e1, da, hi, ALU.subtract)
            K.cp(lo, e1)

        def ssd_tile(i, t, full, L):
            kind = t["kind"]
            sm = L["sm"]
            dt = sm[:, 0, i, :]
            da = sm[:, 1, i, :]
            ecs = sm[:, 2, i, :]
            ncs = sm[:, 3, i, :]
            dte = sm[:, 4, i, :]
            etot = sm[:, 5, i, :]
            dah = L["dah"]
            xs_i = L["xs"][:, i, :]
            xs3 = xs_i.rearrange("p (h q) -> p h q", q=64)

            def b3(v):
                return mk(v, [(1, 32), (0, 64)])

            xdt, xdtd = L["xdt"], L["xdtd"]
            K.tt(xdt[:, :].rearrange("p (h q) -> p h q", q=64), xs3, b3(dt), ALU.mult, eng=PENG)
            K.tt(xdtd[:, :].rearrange("p (h q) -> p h q", q=64), xdt[:, :].rearrange("p (h q) -> p h q", q=64),
                 b3(dte), ALU.mult, eng=PENG)
            tc0 = i * 128
            ysb = L.get("ysb")
            if full:
                bcb = ps[:, 1, :]
                for g in range(4):
                    K.mm(bcb[:, g * 128:(g + 1) * 128], L["BT"][:, g, tc0:tc0 + 128], L["CT"][:, g, tc0:tc0 + 128])
                cbT = L["cbT"]
                K.cp(cbT[:, :], bcb, eng="act")
                if kind == "p":
                    for g in range(4):
                        bo = bank(2 + (g % 2))
                        K.mm(bo, L["CT"][:, g, tc0:tc0 + 128], Sb[:, g * 512:(g + 1) * 512])
                        K.tt(ysb[:, g * 512:(g + 1) * 512].rearrange("p (h q) -> p h q", q=64),
                             bo.rearrange("p (h q) -> p h q", q=64),
                             mk(ecs[:, g * 8:(g + 1) * 8], [(1, 8), (0, 64)]), ALU.mult)
                else:
                    sample_states(i, t, L, da, ecs)
                def stA(hq):
                    q = hq % 2
                    rh, rl = L["rhsU"][q]
                    K.tt(rh[:, :, :], mk(Ub[kind][:, :], [(0, 4), (1, 128)]),
                         mk(dah[:, 0, i, hq * 4:hq * 4 + 4], [(1, 4), (0, 128)]), ALU.mult)
                    K.tt(rl[:, :, :], mk(Ub[kind][:, :], [(0, 4), (1, 128)]),
                         mk(dah[:, 1, i, hq * 4:hq * 4 + 4], [(1, 4), (0, 128)]), ALU.mult)
                    bR = bank(4 + q)
                    K.mm(bR, onesb[:, :], rh[:, :, :].rearrange("p a b -> p (a b)"), start=True, stop=False)
                    K.mm(bR, onesb[:, :], rl[:, :, :].rearrange("p a b -> p (a b)"), start=False, stop=False)
                    K.mm(bR, identb[:, :], mk(Mnb[kind][:, :], [(0, 4), (1, 128)]), start=False, stop=True)

                def stB(hq):
                    q = hq % 2
                    bR = bank(4 + q)
                    df = L["Eh"][q]
                    K.tt(df[:, :, :], bR.rearrange("p (a b) -> p a b", b=128),
                         mk(ncs[:, hq * 4:hq * 4 + 4], [(1, 4), (0, 128)]), ALU.add)
                    E4 = L["E4"][q]
                    K.act(E4[:, :, :].rearrange("p a b -> p (a b)"), df[:, :, :].rearrange("p a b -> p (a b)"), AF.Exp)

                def stC(hq):
                    q = hq % 2
                    g = hq // 2
                    E4 = L["E4"][q]
                    L4 = L["Lh"][q]
                    K.tt(L4[:, :, :], E4[:, :, :], mk(cbT[:, g * 128:(g + 1) * 128], [(0, 4), (1, 128)]), ALU.mult)
                    for hh in range(4):
                        h = hq * 4 + hh
                        byd = ps[:, 2 + (g % 2), (h % 8) * 64:(h % 8) * 64 + 64]
                        K.mm(byd, L4[:, hh, :], xdt[:, h * 64:(h + 1) * 64])
                    if hq % 2 == 1:
                        K.tt(ysb[:, g * 512:(g + 1) * 512], ysb[:, g * 512:(g + 1) * 512], bank(2 + (g % 2)), ALU.add)

                for k in range(10):
                    if k < 8:
                        stA(k)
                    if 0 <= k - 1 < 8:
                        stB(k - 1)
                    if 0 <= k - 2 < 8:
                        stC(k - 2)
            if True:
                if kind == "p":
                    K.tt(St[:, :].rearrange("p (h q) -> p h q", q=64), St[:, :].rearrange("p (h q) -> p h q", q=64),
                         b3(etot), ALU.mult)
                    for g in range(4):
                        bu = bank(g % 2)
                        K.mm(bu, L["Bt"][:, i, g * 128:(g + 1) * 128], xdtd[:, g * 512:(g + 1) * 512])
                        K.tt(St[:, g * 512:(g + 1) * 512], St[:, g * 512:(g + 1) * 512], bu, ALU.add)
                    K.cp(Sb[:, :], St[:, :], eng="act")

            if full:
                tmp = L["tmp"]
                K.tt(tmp[:, :].rearrange("p (h q) -> p h q", q=64), xs3, b3(dsk), ALU.mult, eng=PENG)
                K.tt(ysb[:, :], ysb[:, :], tmp[:, :], ALU.add)
                K.tt(ysb[:, :], ysb[:, :], L["sz"][:, i, :], ALU.mult)
                ssq4 = sm[:, 7, i, 0:4]
                for g in range(4):
                    K.act(tmp[:, g * 512:(g + 1) * 512], ysb[:, g * 512:(g + 1) * 512], AF.Square,
                          accum=ssq4[:, g:g + 1])
                r4 = sm[:, 7, i, 4:8]
                K.rsqrt(r4, ssq4, 1.0 / 512, EPS)
                ynb = L["ynb"]
                for g in range(4):
                    K.act(ynb[:, g * 512:(g + 1) * 512], ysb[:, g * 512:(g + 1) * 512], AF.Identity, scale=r4[:, g:g + 1])
                for half in range(2):
                    for c in range(8):
                        cc = half * 8 + c
                        K.tr(psb[:, half, c * 128:(c + 1) * 128], ynb[:, cc * 128:(cc + 1) * 128], identb[:, :])
                    for c in range(8):
                        cc = half * 8 + c
                        K.act(BIG["mixT"][:, 16 + cc, tc0:tc0 + 128], psb[:, half, c * 128:(c + 1) * 128], AF.Identity,
                              scale=snw[:, cc:cc + 1])
        def sample_states(i, t, L, da, ecs):
            tc0 = i * 128
            sm = L["sm"]
            ysb = L.get("ysb")
            xdtd = L["xdtd"]
            daj = L["daj"]
            K.tt(daj[:, :, :], mk(da, [(0, 16), (1, 32)]), mk(selj, [(1, 16), (0, 32)]), ALU.mult)
            bE = bank(1)
            K.mm(bE, onesf[:, :], daj[:, :, :].rearrange("p a b -> p (a b)"))
            Ej = L["Ej"]
            K.act(Ej[:, :], bE, AF.Exp)
            yoT = ps[:, 2:6, :].rearrange("p a b -> p (a b)")
            for j0 in range(2):
                K.dma(L["sj"][j0][:, :], sstT[j0, :, :], "sj%d" % j0)
            for j in range(16):
                sj = L["sj"][j % 2]
                sjb = L["sjb"][j % 2]
                K.cp(sjb[:, :], sj[:, :], eng="act")
                for b in range(16):
                    g = b // 4
                    K.mm(yoT[:, b * 128 + j * 8:b * 128 + j * 8 + 8], sjb[:, b * 128:(b + 1) * 128],
                         L["CT"][:, g, tc0 + j * 8:tc0 + j * 8 + 8])
                Btm = L["Btm"][j % 2]
                K.ts(Btm[:, :], L["Bt"][:, i, :], selj[:, j:j + 1], None, ALU.mult)
                sn = L["sn"][j % 2]
                K.tt(sn[:, :].rearrange("p (h q) -> p h q", q=64), sj[:, :].rearrange("p (h q) -> p h q", q=64),
                     mk(Ej[:, j * 32:(j + 1) * 32], [(1, 32), (0, 64)]), ALU.mult)
                for g in range(4):
                    bu = bank(g % 2)
                    K.mm(bu, Btm[:, g * 128:(g + 1) * 128], xdtd[:, g * 512:(g + 1) * 512])
                    K.tt(sn[:, g * 512:(g + 1) * 512], sn[:, g * 512:(g + 1) * 512], bu, ALU.add)
                K.dma(o_sst_s[j, :, :], sn[:, :], "sn%d" % (j % 2))
                if j + 2 < 16:
                    K.dma(sj[:, :], sstT[j + 2, :, :], "sj%d" % (j % 2))
            yoS = L["tmp"]
            K.cp(yoS[:, :], yoT, eng="act")
            for b in range(16):
                bk = ps[:, 2 + (b // 4), (b % 4) * 128:(b % 4) * 128 + 128]
                K.mm(bk, yoS[:, b * 128:(b + 1) * 128], identf)
            for g in range(4):
                bo = bank(2 + g)
                K.cp(ysb[:, g * 512:(g + 1) * 512], bo, eng="act")
                K.tt(ysb[:, g * 512:(g + 1) * 512].rearrange("p (h q) -> p h q", q=64),
                     ysb[:, g * 512:(g + 1) * 512].rearrange("p (h q) -> p h q", q=64),
                     mk(ecs[:, g * 8:(g + 1) * 8], [(1, 8), (0, 64)]), ALU.mult)

        def ssd_phase(tiles, full):
            T = len(tiles)
            NT = 128 * T
            has_s = any(t["kind"] == "s" for t in tiles)
            with contextlib.ExitStack() as ph:
                def pb(name, shape, dt=F32):
                    return ph.enter_context(nc.sbuf_tensor(un(name), list(shape), dt))

                L = {}
                L["xs"] = pb("xs", [128, T, 2048], BF16)
                L["BT"] = pb("BT", [128, 4, 512], BF16)
                L["CT"] = pb("CT", [128, 4, 512], BF16)
                L["Bt"] = pb("Bt", [128, 4, 512], BF16)
                L["dtr"] = pb("dtr", [128, 4, 32])
                L["raw"] = [[pb("raw%d%d" % (pp, q), [128, 3 + 512]) for q in range(2)] for pp in range(2)]
                L["raws"] = [pb("raws%d" % q, [128, 16, 11]) for q in range(2)]
                L["acc"] = [[pb("acc%d%d" % (pp, q), [128, 512]) for q in range(2)] for pp in range(2)]
                L["cvo"] = [[pb("cvo%d%d" % (pp, q), [128, 512], BF16) for q in range(2)] for pp in range(2)]
                L["sm"] = pb("sm", [128, 8, T, 32])
                L["dah"] = pb("dah", [128, 2, T, 32], BF16)
                L["xdt"] = pb("xdt", [128, 2048], BF16)
                L["xdtd"] = pb("xdtd", [128, 2048], BF16)
                if full:
                    L["sz"] = pb("sz", [128, T, 2048], BF16)
                    L["ysb"] = pb("ysb", [128, 2048])
                    L["tmp"] = pb("tmp", [128, 2048], F32 if has_s else BF16)
                    L["ynb"] = L["xdtd"]
                    L["cbT"] = pb("cbT", [128, 512], BF16)
                    L["rhsU"] = [(pb("rhsUh%d" % q, [128, 4, 128], BF16), pb("rhsUl%d" % q, [128, 4, 128], BF16)) for q in range(2)]
                    L["Eh"] = [pb("Eh%d" % q, [128, 4, 128]) for q in range(2)]
                    L["E4"] = [pb("E4%d" % q, [128, 4, 128], BF16) for q in range(2)]
                    L["Lh"] = [pb("Lh%d" % q, [128, 4, 128], BF16) for q in range(2)]
                if has_s:
                    L["sconv"] = pb("sconv", [128, 24, 48])
                    L["osc"] = L["sconv"]
                    L["daj"] = pb("daj", [128, 16, 32])
                    L["Ej"] = pb("Ej", [128, 512])
                    L["sj"] = [pb("sj%d" % q, [128, 2048]) for q in range(2)]
                    L["sjb"] = [pb("sjb0", [128, 2048], BF16)] * 2
                    L["sn"] = L["sj"]
                    L["Btm"] = [pb("Btm%d" % q, [128, 512], BF16) for q in range(2)]
                    K.dma(L["sconv"][:, :, :], sconvT[:, :, :], "m1")
                ws.begin_phase(alloc_extras(pb, 4))
                if full:
                    for b in range(4):
                        (W,) = ws.get(D_in(4096 + 512 * b, 512))
                        for i in range(T):
                            bk = bank(2 + (i % 4))
                            for k in range(16):
                                K.mm(bk, hT[:, k, i * 128:(i + 1) * 128], W[:, k, :], start=(k == 0), stop=(k == 15))
                            K.act(L["sz"][:, i, b * 512:(b + 1) * 512], bk, AF.Silu)
                S.mark("  s.z")
                ssd_inproj(tiles, full, L)
                S.mark("  s.inproj")
                ssd_scalars(tiles, L)
                for i, t in enumerate(tiles):
                    ssd_tile(i, t, full, L)
                    S.mark("  s.tile")
                    if dbg_stop < 10 ** 8:
                        K.dma(ysamp[:, i * 224:(i + 1) * 224].rearrange("p (a b) -> p a b", b=32), L["sm"][:, 0:7, i, :], "dbg2")
                    if t.get("last"):
                        K.dma(o_sconv_p[:, :, :], hist_s[:, :, :], "m2")
                        K.dma(o_sst_p[:, :], St[:, :], "m3")
                if has_s:
                    K.dma(o_sconv_s[:, :, :], L["osc"][:, :, :], "m4")
                if dbg_stop < 10 ** 8:
                    K.dma(o_sst_p[:, :], St[:, :], "dbg1")
                    K.dma(ysamp[:, 1024:1024 + 32 * T], L["dtr"][:, 0:T, :].rearrange("p a b -> p (a b)"), "dbg3")
                    K.dma(ysamp[:, 384:384 + 512], L["Bt"][:, 0, :], "dbg4") if False else None
                ws.end_phase()
                S.barrier("ssd")

        STG = [None]

        def nrmf_stage(pb):
            if STG[0] is None:
                STG[0] = pb("nrmf", [128, 2048])
            return STG[0]

        def gate_phase(tiles):
            STG[0] = None
            T = len(tiles)
            NT = 128 * T
            Tp = sum(1 for t in tiles if t["kind"] == "p")
            NTp = 128 * Tp
            has_s = Tp < T
            with contextlib.ExitStack() as ph:
                def pb(name, shape, dt=F32):
                    return ph.enter_context(nc.sbuf_tensor(un(name), list(shape), dt))

                gv = pb("gv", [128, T, 2048], BF16)
                nrm = gv
                gsum = pb("gsum", [128, 4, 8])
                wsT["s"] = None
                wsT["p"] = pb("wsTp", [128, 16, 128], BF16)
                K.dma(nrmf_stage(pb)[:, :], wsTp_d.rearrange("p a b -> p (a b)"), "mwstp")
                K.tt(wsT["p"][:, :, :], mk(STG[0][:, :], [(128, 16), (1, 128)]),
                     mk(Msk["p"], [(0, 16), (1, 128)]), ALU.mult)
                if has_s:
                    wsT["s"] = pb("wsTs", [128, 16, 128], BF16)
                    K.dma(nrmf_stage(pb)[:, :], wsTs_d.rearrange("p a b -> p (a b)"), "mwsts")
                    K.tt(wsT["s"][:, :, :], mk(STG[0][:, :], [(128, 16), (1, 128)]),
                         mk(Msk["s"], [(0, 16), (1, 128)]), ALU.mult)
                st = pb("gst", [128, 8])
                gu = pb("gu", [128, 512])
                t2 = pb("t2", [128, 512])
                Qs = {"p": pb("Qsp", [128, 128]), "s": pb("Qss", [128, 128])}
                bsr = {"p": pb("bsrp", [128, 2048])}
                K.dma(bsr["p"][:, :], bsp_d[:, :], "m5")
                if has_s:
                    bsr["s"] = pb("bsrs", [128, 2048])
                    K.dma(bsr["s"][:, :], bss_d[:, :], "m6")
                junk = pb("gjunk", [128, 512], BF16)
                nrmf = nrmf_stage(pb)
                cvT = pb("cvT", [128, 16, 128])
                ws.begin_phase(alloc_extras(pb, 3))
                for b in range(4):
                    (W,) = ws.get(D_in(2048 + 512 * b, 512))
                    for i in range(T):
                        bk = bank(i % 4)
                        for k in range(16):
                            K.mm(bk, hT[:, k, i * 128:(i + 1) * 128], W[:, k, :], start=(k == 0), stop=(k == 15))
                        K.act(gv[:, i, b * 512:(b + 1) * 512], bk, AF.Gelu, accum=gsum[:, i, b:b + 1])
                        K.act(junk[:, :], gv[:, i, b * 512:(b + 1) * 512], AF.Square, accum=gsum[:, i, 4 + b:5 + b])
                for i, t in enumerate(tiles):
                    K.tt(st[:, 5:6], gsum[:, i, 4:5], gsum[:, i, 5:6], ALU.add)
                    K.tt(st[:, 6:7], gsum[:, i, 6:7], gsum[:, i, 7:8], ALU.add)
                    K.tt(st[:, 0:1], st[:, 5:6], st[:, 6:7], ALU.add)
                    K.tt(st[:, 1:2], gsum[:, i, 0:1], gsum[:, i, 1:2], ALU.add)
                    K.tt(st[:, 2:3], gsum[:, i, 2:3], gsum[:, i, 3:4], ALU.add)
                    K.tt(st[:, 1:2], st[:, 1:2], st[:, 2:3], ALU.add)
                    K.ts(st[:, 1:2], st[:, 1:2], 1.0 / D, None, ALU.mult)
                    K.tt(st[:, 2:3], st[:, 1:2], st[:, 1:2], ALU.mult)
                    K.stt(st[:, 3:4], st[:, 0:1], 1.0 / D, st[:, 2:3], ALU.mult, ALU.subtract)
                    K.rsqrt(st[:, 3:4], st[:, 3:4], 1.0, EPS)
                    K.stt(st[:, 4:5], st[:, 1:2], -1.0, st[:, 3:4], ALU.mult, ALU.mult)
                    if t.get("last") or t["kind"] == "s":
                        K.act(nrmf[:, :], gv[:, i, :], AF.Identity, bias=st[:, 4:5], scale=st[:, 3:4])
                        for h in range(16):
                            bk = ps[:, 4 + (h % 2), 0:128]
                            K.mm(bk, nrmf[:, h * 128:(h + 1) * 128], identf)
                            K.act(cvT[:, h, :], bk, AF.Identity, bias=lnb[:, h:h + 1], scale=lng[:, h:h + 1])
                        K.dma((o_cv_p if t["kind"] == "p" else o_cv_s)[:, :, :], cvT[:, :, :], "m7")
                    K.act(nrm[:, i, :], gv[:, i, :], AF.Identity, bias=st[:, 4:5], scale=st[:, 3:4])
                for b in range(4):
                    (W,) = ws.get(D_in(512 * b, 512))
                    for hh in range(4):
                        h = b * 4 + hh
                        bu = bank(4, NT)
                        for k in range(16):
                            K.mm(bu, W[:, k, hh * 128:(hh + 1) * 128], hT[:, k, 0:NT], start=(k == 0), stop=(k == 15))
                        K.act(gu[:, 0:NT], bu, AF.Gelu)
                        bs_ = bank(5, NT)
                        for i, t in enumerate(tiles):
                            K.mm(bs_[:, i * 128:(i + 1) * 128], nrm[:, i, h * 128:(h + 1) * 128], wsT[t["kind"]][:, h, :])
                        kinds = ["p"] + (["s"] if has_s else [])
                        for qi, kd in enumerate(kinds):
                            bq = ps[:, 0, qi * 128:(qi + 1) * 128]
                            K.mm(bq, onesb[:, :], wsT[kd][:, h, :])
                            K.stt(Qs[kd][:, :], bq, lnb[:, h:h + 1], bsr[kd][:, h * 128:(h + 1) * 128], ALU.mult, ALU.add)
                        if Tp > 0:
                            K.stt(t2[:, 0:NTp].rearrange("p (i t) -> p i t", t=128),
                                  bs_[:, 0:NTp].rearrange("p (i t) -> p i t", t=128), lng[:, h:h + 1],
                                  mk(Qs["p"][:, :], [(0, Tp), (1, 128)]), ALU.mult, ALU.add)
                        if has_s:
                            K.stt(t2[:, NTp:NT], bs_[:, NTp:NT], lng[:, h:h + 1], Qs["s"][:, :], ALU.mult, ALU.add)
                        K.tt(BIG["mixT"][:, h, 0:NT], t2[:, 0:NT], gu[:, 0:NT], ALU.mult)
                ws.end_phase()
                S.barrier("gate")

        def wout_phase(tiles):
            T = len(tiles)
            with contextlib.ExitStack() as ph:
                def pb(name, shape, dt=F32):
                    return ph.enter_context(nc.sbuf_tensor(un(name), list(shape), dt))

                mo = pb("mo", [128, T, 2048])
                with contextlib.ExitStack() as ph2:
                    def pb2(name, shape, dt=F32):
                        return ph2.enter_context(nc.sbuf_tensor(un(name), list(shape), dt))
                    ws.begin_phase(alloc_extras(pb2, 4))
                    for b in range(8):
                        (W,) = ws.get([(w_out, 256 * b, 256, 32, 0)])
                        for i in range(T):
                            bk = bank(i % 4, 256)
                            for k in range(32):
                                K.mm(bk, BIG["mixT"][:, k, i * 128:(i + 1) * 128], W[:, k, :], start=(k == 0), stop=(k == 31))
                            K.cp(mo[:, i, b * 256:(b + 1) * 256], bk, eng="act")
                    ws.end_phase()
                    S.barrier("wout_mm")
                nmo = pb("nmo_sb", [128, 2048])
                xb = [pb("p3x%d" % q, [128, 2048]) for q in range(2)]
                xm = [pb("p3m%d" % q, [128, 2048]) for q in range(2)]
                xsb = pb("p3xs", [128, 2048], BF16)
                K.dma(nmo[:, :], nmo_d[:, :], "m8")
                for i0 in range(min(2, T)):
                    K.dma(xb[i0][:, :], tiles[i0]["src"], "x%d" % i0)
                for i, t in enumerate(tiles):
                    x = xb[i % 2]
                    ssq = small[:, 2:3]
                    K.act(xsb[:, :], mo[:, i, :], AF.Square, accum=ssq)
                    rstd = small[:, 3:4]
                    K.rsqrt(rstd, ssq, 1.0 / D, EPS)
                    m = xm[i % 2]
                    K.stt(m[:, :], mo[:, i, :], rstd, nmo[:, :], ALU.mult, ALU.mult)
                    K.tt(m[:, :], m[:, :], x[:, :], ALU.add)
                    K.dma(t["xm"], m[:, :], "xm%d" % (i % 2))
                    if i + 2 < T:
                        K.dma(x[:, :], tiles[i + 2]["src"], "x%d" % (i % 2))
                    rms_to_T(None, hT, i * 128, nfp, m, xsb)
                S.barrier("wout")

        def up_phase(tiles):
            T = len(tiles)
            NT = 128 * T
            Tp = sum(1 for t in tiles if t["kind"] == "p")
            NTp = 128 * Tp
            has_s = Tp < T
            is_last = any(t.get("last") for t in tiles)
            with contextlib.ExitStack() as ph:
                def pb(name, shape, dt=F32):
                    return ph.enter_context(nc.sbuf_tensor(un(name), list(shape), dt))

                raw = [[pb("fr%d%d" % (q, z), [128, 2 + 512]) for z in range(2)] for q in range(2)]
                acc = [[pb("fa%d%d" % (q, z), [128, 512]) for z in range(2)] for q in range(2)]
                gg = [pb("fgg%d" % q, [128, 512]) for q in range(2)]
                if has_s:
                    raws = [[pb("frs%d%d" % (q, z), [128, 16, 10]) for z in range(2)] for q in range(2)]
                    fcv = pb("fcv", [128, 88, 32])
                    ofs = fcv
                    K.dma(fcv[:, :, :], fconvT[:, :, :], "m9")
                ws.begin_phase(alloc_extras(pb, 4))
                for b in range(22):
                    (Wg, Wu) = ws.get([(w_up, 256 * b, 256, 16, 0), (w_up, DFF + 256 * b, 256, 16, 0)])
                    for cc in range(2):
                        c = b * 2 + cc
                        q = c % 2
                        for z, Wz in enumerate((Wg, Wu)):
                            bk = bank(2 * q + z, NT)
                            for k in range(16):
                                K.mm(bk, Wz[:, k, cc * 128:(cc + 1) * 128], hT[:, k, 0:NT], start=(k == 0), stop=(k == 15))
                        for z in range(2):
                            ch = c + 44 * z
                            if NTp > 0:
                                r = raw[q][z]
                                K.cp(r[:, 0:2], hist_f[:, ch, :])
                                K.cp(r[:, 2:2 + NTp], bank(2 * q + z, NTp), eng="act")
                                K.cp(hist_f[:, ch, :], r[:, NTp:NTp + 2])
                            if has_s:
                                rs = raws[q][z]
                                K.cp(rs[:, :, 0:2], fcv[:, ch, :].rearrange("p (j k) -> p j k", k=2))
                                K.cp(rs[:, :, 2:10], ps[:, 2 * q + z, NTp:NTp + 128].rearrange("p (j t) -> p j t", t=8),
                                     eng="act")
                                K.cp(ofs[:, ch, :].rearrange("p (j k) -> p j k", k=2), rs[:, :, 8:10])
                        for kk in range(3):
                            for z in range(2):
                                ch = c + 44 * z
                                if NTp > 0:
                                    r = raw[q][z]
                                    a = acc[q][z][:, 0:NTp]
                                    if kk == 0:
                                        K.ts(a, r[:, 0:NTp], fcw[:, ch, 0:1], fcw[:, ch, 3:4], ALU.mult, ALU.add)
                                    else:
                                        K.stt(a, r[:, kk:kk + NTp], fcw[:, ch, kk:kk + 1], a, ALU.mult, ALU.add)
                                if has_s:
                                    rs = raws[q][z]
                                    a3 = acc[q][z][:, NTp:NTp + 128].rearrange("p (j t) -> p j t", t=8)
                                    if kk == 0:
                                        K.ts(a3, rs[:, :, 0:8], fcw[:, ch, 0:1], fcw[:, ch, 3:4], ALU.mult, ALU.add)
                                    else:
                                        K.stt(a3, rs[:, :, kk:kk + 8], fcw[:, ch, kk:kk + 1], a3, ALU.mult, ALU.add)
                        K.act(gg[q][:, 0:NT], acc[q][0][:, 0:NT], AF.Gelu_apprx_tanh)
                        K.tt(BIG["actT"][:, c, 0:NT], gg[q][:, 0:NT], acc[q][1][:, 0:NT], ALU.mult)
                if is_last:
                    K.dma(o_fconv_p[:, :, :], hist_f[:, :, :], "m10")
                if has_s:
                    K.dma(o_fconv_s[:, :, :], ofs[:, :, :], "m11")
                ws.end_phase()
                S.barrier("up")

        def down_phase(tiles):
            T = len(tiles)

            def bankx(i):
                if i < 6:
                    return ps[:, i, 0:256]
                return psb[:, i - 6, :].bitcast(F32)[:, 0:256]

            with contextlib.ExitStack() as ph:
                def pb(name, shape, dt=F32):
                    return ph.enter_context(nc.sbuf_tensor(un(name), list(shape), dt))

                fo = pb("fo", [128, T, 2048])
                with contextlib.ExitStack() as ph2:
                    def pb2(name, shape, dt=F32):
                        return ph2.enter_context(nc.sbuf_tensor(un(name), list(shape), dt))
                    ws.begin_phase(alloc_extras(pb2, 4))
                    for b in range(8):
                        base = (b % 2) * 4 if T <= 4 else 0
                        for kh in range(2):
                            (W,) = ws.get([(w_down, 256 * b, 256, 22, kh * 22 * 128)])
                            for i in range(T):
                                bk = bankx(base + i)
                                for k in range(22):
                                    K.mm(bk, BIG["actT"][:, kh * 22 + k, i * 128:(i + 1) * 128], W[:, k, :],
                                         start=(kh == 0 and k == 0), stop=(kh == 1 and k == 21))
                        for i in range(T):
                            K.cp(fo[:, i, b * 256:(b + 1) * 256], bankx(base + i), eng="act")
                    ws.end_phase()
                    S.barrier("down_mm")
                nfo = pb("nfo_sb", [128, 2048])
                xm = [pb("p5m%d" % q, [128, 2048]) for q in range(2)]
                yo = [pb("p5y%d" % q, [128, 2048]) for q in range(2)]
                K.dma(nfo[:, :], nfo_d[:, :], "m12")
                for i0 in range(min(2, T)):
                    K.dma(xm[i0][:, :], tiles[i0]["xm"], "xr%d" % i0)
                for i, t in enumerate(tiles):
                    m = xm[i % 2]
                    ssq = small[:, 4:5]
                    y = yo[i % 2]
                    K.act(y[:, :], fo[:, i, :], AF.Square, accum=ssq)
                    rstd = small[:, 5:6]
                    K.rsqrt(rstd, ssq, 1.0 / D, EPS)
                    K.stt(y[:, :], fo[:, i, :], rstd, nfo[:, :], ALU.mult, ALU.mult)
                    K.tt(y[:, :], y[:, :], m[:, :], ALU.add)
                    K.dma(t["dst"], y[:, :], "yo%d" % (i % 2))
                    if i + 2 < T:
                        K.dma(m[:, :], tiles[i + 2]["xm"], "xr%d" % (i % 2))
                S.barrier("down")

        prog = []
        prog.append(lambda: S.barrier())
        for g in lite_groups:
            prog.append(lambda g=g: phase0(g))
            prog.append(lambda g=g: ssd_phase(g, full=False))

        def _maskstate():
            K.ts(St[:, :], St[:, :], flg[:, 0:1], None, ALU.mult)
            K.cp(Sb[:, :], St[:, :], eng="act")
        prog.append(_maskstate)
        for g in full_groups:
            Tg = len(g)
            prog.append(lambda g=g: phase0(g))

            def _mix(g=g, Tg=Tg):
                with nc.sbuf_tensor(un("mixT"), [128, 32, 128 * Tg], BF16) as mixT:
                    BIG["mixT"] = mixT
                    for fn in (gate_phase, lambda gg: ssd_phase(gg, full=True), wout_phase):
                        if DBG["n"] <= 0:
                            break
                        DBG["n"] -= 1
                        fn(g)

            def _ffn(g=g, Tg=Tg):
                with nc.sbuf_tensor(un("actT"), [128, 44, 128 * Tg], BF16) as actT:
                    BIG["actT"] = actT
                    for fn in (up_phase, down_phase):
                        if DBG["n"] <= 0:
                            break
                        DBG["n"] -= 1
                        fn(g)
            prog.append(_mix)
            prog.append(_ffn)
        DBG = {"n": dbg_stop}
        for fn in prog:
            if DBG["n"] <= 0:
                break
            if fn.__name__ in ("_mix", "_ffn"):
                fn()
            else:
                DBG["n"] -= 1
                fn()
        S.barrier()

        sem_names = ["eng_pe", "eng_act", "eng_dve", "eng_pool"] + ["ch_" + c for c in S.chan_count]
        sems = {}
        for n in sem_names:
            sems[n] = es.enter_context(nc.semaphore(n))
        with nc.Block() as block:
            S.emit(block, sems)
    return nc


_NC_CACHE = {}


def _consts():
    t = np.arange(128)
    ident = np.eye(128, dtype=np.float32)
    Up = (t[:, None] <= t[None, :]).astype(np.float32)
    same = (t[:, None] // 8) == (t[None, :] // 8)
    Us = (Up.astype(bool) & same).astype(np.float32)
    Tp = np.ones((128, 128), np.float32)
    Ts = same.astype(np.float32)
    mskp = Up.copy()
    msks = Us.copy()
    Mnp = np.where(mskp > 0, 0.0, NEG).astype(np.float32)
    Mns = np.where(msks > 0, 0.0, NEG).astype(np.float32)
    selj = np.zeros((128, 128), np.float32)
    selj[t, t // 8] = 1.0
    return np.stack([ident, Up, Us, Tp, Ts, Mnp, Mns, mskp, msks, selj], axis=1).astype(np.float32)


def kernel(x_prompt, x_sample, state_ssm, state_ssm_conv, state_ffn_conv,
           norm_mix_pre, w_in, gate_ln_g, gate_ln_b, gate_w_s, gate_b_s,
           ssm_conv_w, ssm_conv_b, ssm_dt_bias, ssm_a_log, ssm_d, ssm_norm_w,
           w_out, norm_mix_post, norm_ffn_pre, ffn_w_up, ffn_conv_w, ffn_conv_b,
           ffn_w_down, norm_ffn_post, _runner=None):
    f = np.float32
    A = lambda a: np.ascontiguousarray(np.asarray(a, dtype=f))
    x_prompt = A(x_prompt); x_sample = A(x_sample)

    def fm(v, n):
        return A(np.asarray(v, f).reshape(n, 128).T)

    vecs = np.stack([fm(norm_mix_pre[0], 16), fm(norm_ffn_pre[0], 16), fm(gate_ln_g[0], 16),
                     fm(gate_ln_b[0], 16), fm(ssm_norm_w[0], 16)], axis=1)
    rep = lambda v: A(np.broadcast_to(np.asarray(v, f).reshape(1, -1), (128, np.asarray(v).size)))
    ws = np.asarray(gate_w_s[0], f)
    wsTp = A(ws.transpose(2, 0, 1))
    wsTs = np.zeros((128, 16, 128), f)
    blk = ws[:, :8, :8].transpose(2, 0, 1)
    for j in range(16):
        wsTs[8 * j:8 * j + 8, :, 8 * j:8 * j + 8] = blk
    bs = np.asarray(gate_b_s[0], f)
    bsp = rep(bs.reshape(-1))
    bss = rep(np.tile(bs[:, :8], (1, 16)).reshape(-1))
    scw = np.concatenate([np.asarray(ssm_conv_w[0], f), np.asarray(ssm_conv_b[0], f)[None]], axis=0)
    scw = A(scw.reshape(5, 24, 128).transpose(2, 1, 0))
    fcw = np.concatenate([np.asarray(ffn_conv_w[0], f), np.asarray(ffn_conv_b[0], f)[None]], axis=0)
    fcw = A(fcw.reshape(4, 88, 128).transpose(2, 1, 0))
    h32 = np.stack([rep(ssm_dt_bias[0]), rep(ssm_a_log[0]), rep(ssm_d[0])], axis=1)
    cmat = _consts()
    common = dict(w_in=A(w_in[0]), w_out=A(w_out[0]), w_up=A(ffn_w_up[0]), w_down=A(ffn_w_down[0]),
                  vecs=A(vecs), nmo=rep(norm_mix_post[0]), nfo=rep(norm_ffn_post[0]), wsTp=wsTp, wsTs=wsTs,
                  bsp=bsp, bss=bss, scw=scw, fcw=fcw, h32=A(h32), cmat=cmat)
    in_maps = []
    for core in range(8):
        b = core // 2
        second = core % 2
        m = dict(common)
        if second == 0:
            m["xpre"] = np.zeros((NLITE * 128, D), f)
            m["xmain"] = A(x_prompt[b, 0:NMAIN * 128])
            m["flag"] = np.zeros((128, 1), f)
        else:
            m["xpre"] = A(x_prompt[b, 0:NLITE * 128])
            m["xmain"] = A(x_prompt[b, NLITE * 128:2048])
            m["flag"] = np.ones((128, 1), f)
        js = slice(16 * core, 16 * core + 16)
        m["xsamp"] = A(x_sample[js].reshape(128, D))
        m["sstT"] = A(np.asarray(state_ssm[0, js], f).reshape(16, 2048, 128).transpose(0, 2, 1))
        m["sconvT"] = A(np.asarray(state_ssm_conv[0, js], f).reshape(48, 24, 128).transpose(2, 1, 0))
        m["fconvT"] = A(np.asarray(state_ffn_conv[0, js], f).reshape(32, 88, 128).transpose(2, 1, 0))
        in_maps.append(m)
    if _runner is not None:
        R = _runner(in_maps)
    else:
        if "nc" not in _NC_CACHE:
            _NC_CACHE["nc"] = build()
        nc = _NC_CACHE["nc"]
        res = run_bass_kernel_spmd(nc, in_maps, core_ids=list(range(8)))
        R = res.results
    y_p = np.zeros((4, 2048, D), f)
    y_s = np.zeros((128, 8, D), f)
    sst_p = np.zeros((1, 4, 32, 64, 128), f)
    sconv_p = np.zeros((1, 4, 3, 3072), f)
    fconv_p = np.zeros((1, 4, 2, 11264), f)
    cv_p = np.zeros((1, 4, 128, 16, 128), f)
    sst_s = np.zeros((1, 128, 32, 64, 128), f)
    sconv_s = np.zeros((1, 128, 3, 3072), f)
    fconv_s = np.zeros((1, 128, 2, 11264), f)
    cv_s = np.zeros((1, 128, 8, 16, 128), f)
    for core in range(8):
        b = core // 2
        second = core % 2
        r = R[core]
        ym = np.asarray(r["ymain"]).reshape(NMAIN * 128, D)
        if second == 0:
            y_p[b, 0:NMAIN * 128] = ym
        else:
            y_p[b, NMAIN * 128:2048] = ym[(NMAIN * 128 - (2048 - NMAIN * 128)):]
            sst_p[0, b] = np.asarray(r["o_sst_p"]).reshape(128, 2048).T.reshape(32, 64, 128)
            sconv_p[0, b] = np.asarray(r["o_sconv_p"]).reshape(128, 24, 3).transpose(2, 1, 0).reshape(3, 3072)
            fconv_p[0, b] = np.asarray(r["o_fconv_p"]).reshape(128, 88, 2).transpose(2, 1, 0).reshape(2, 11264)
            cv_p[0, b] = np.asarray(r["o_cv_p"]).reshape(128, 16, 128).transpose(2, 1, 0)
        js = slice(16 * core, 16 * core + 16)
        y_s[js] = np.asarray(r["ysamp"]).reshape(16, 8, D)
        sst_s[0, js] = np.asarray(r["o_sst_s"]).reshape(16, 128, 2048).transpose(0, 2, 1).reshape(16, 32, 64, 128)
        sconv_s[0, js] = np.asarray(r["o_sconv_s"]).reshape(128, 24, 16, 3).transpose(2, 3, 1, 0).reshape(16, 3, 3072)
        fconv_s[0, js] = np.asarray(r["o_fconv_s"]).reshape(128, 88, 16, 2).transpose(2, 3, 1, 0).reshape(16, 2, 11264)
        cv_s[0, js] = np.asarray(r["o_cv_s"]).reshape(128, 16, 16, 8).transpose(2, 3, 1, 0)
    return (y_p, y_s, sst_p, sconv_p, fconv_p, cv_p, sst_s, sconv_s, fconv_s, cv_s)
```

```python
import numpy as np
import concourse.bass as bass
import concourse.mybir as mybir
from concourse.bass_utils import run_bass_kernel_spmd

F32 = mybir.dt.float32
BF16 = mybir.dt.bfloat16
AF = mybir.ActivationFunctionType
ALU = mybir.AluOpType

D = 2048
IN_DIM = 9248
DFF = 5632
EPS = 1e-6
NLITE = 7
NMAIN = 9
NEG = -30000.0
PENG = "pool"


def _esz(dt):
    s = str(dt)
    if "bfloat16" in s or "float16" in s:
        return 2
    return 4


class Op:
    __slots__ = ("eng", "fn", "chan", "deps", "signal", "sigval", "is_dma", "idx")

    def __init__(self, eng, fn, chan):
        self.eng = eng
        self.fn = fn
        self.chan = chan
        self.deps = {}
        self.signal = False
        self.sigval = 0
        self.is_dma = chan is not None


class Sched:
    ENGS = ("pe", "act", "dve", "pool", "sp")

    def __init__(self, nc):
        self.nc = nc
        self.ops = {e: [] for e in self.ENGS}
        self.track = {}
        self.chan_count = {}
        self.ignore = set()

    def region(self, ap):
        t = ap.tensor
        name = t.name
        if name in self.ignore:
            return None
        e = _esz(ap.dtype)
        shape = list(t.shape)
        rowb = 1
        for s in shape[1:]:
            rowb *= s
        rowb *= _esz(t.dtype)
        off = int(ap.offset) * e
        dims = list(ap.ap)
        p0 = off // rowb
        f0 = off % rowb
        npart = dims[0][1]
        lo = f0
        hi = f0
        for (st, cn) in dims[1:]:
            ext = st * (cn - 1) * e
            if ext < 0:
                lo += ext
            else:
                hi += ext
        hi += e
        if name.startswith("psum_"):
            lo = (lo // 2048) * 2048
            hi = ((hi + 2047) // 2048) * 2048
            return name, (0, 128, lo, hi)
        return name, (p0, p0 + npart, lo, hi)

    @staticmethod
    def _ov(a, b):
        return a[0] < b[1] and b[0] < a[1] and a[2] < b[3] and b[2] < a[3]

    @staticmethod
    def _contains(a, b):
        return a[0] <= b[0] and a[1] >= b[1] and a[2] <= b[2] and a[3] >= b[3]

    def add(self, eng, fn, reads=(), writes=(), chan=None):
        o = Op(eng, fn, chan)
        okey = chan if chan is not None else eng
        accs = []
        for ap in reads:
            r = self.region(ap)
            if r is not None:
                accs.append((False, r))
        for ap in writes:
            r = self.region(ap)
            if r is not None:
                accs.append((True, r))
        for (isw, (name, reg)) in accs:
            lst = self.track.get(name)
            if not lst:
                continue
            for ent in lst:
                (w2, reg2, key2, op2) = ent
                if not (isw or w2):
                    continue
                if not self._ov(reg, reg2):
                    continue
                if op2 is o:
                    continue
                need = True
                if (not op2.is_dma) and (not o.is_dma) and op2.eng == o.eng:
                    if o.eng == "pe":
                        need = False
                if need:
                    k = key2
                    prev = o.deps.get(k)
                    if prev is None or prev.idx < op2.idx:
                        o.deps[k] = op2
        o.idx = len(self.ops[eng])
        if o.is_dma:
            c = self.chan_count.get(chan, 0) + 1
            self.chan_count[chan] = c
            o.sigval = 16 * c
            o.idx = c
            o.signal = True
        for (isw, (name, reg)) in accs:
            lst = self.track.setdefault(name, [])
            if isw:
                lst[:] = [en for en in lst if not self._contains(reg, en[1])]
                lst.append((True, reg, okey, o))
            else:
                rep = False
                for i, en in enumerate(lst):
                    if (not en[0]) and en[2] == okey and en[1] == reg:
                        lst[i] = (False, reg, okey, o)
                        rep = True
                        break
                if not rep:
                    lst.append((False, reg, okey, o))
        for d in o.deps.values():
            d.signal = True
        self.ops[eng].append(o)
        return o

    def mark(self, label):
        if not hasattr(self, "marks"):
            self.marks = []
        self.marks.append((label, sum(1 for o in self.ops["pe"] if o.fn is not None)))

    def barrier(self, label=""):
        if not hasattr(self, "marks"):
            self.marks = []
        self.marks.append((label, sum(1 for o in self.ops["pe"] if o.fn is not None)))
        last = {}
        for e in self.ENGS:
            for o in reversed(self.ops[e]):
                if (not o.is_dma) and o.fn is not None:
                    last[e] = o
                    break
        lastdma = {}
        for e in self.ENGS:
            for o in self.ops[e]:
                if o.is_dma:
                    lastdma[o.chan] = o
        for e in self.ENGS:
            o = Op(e, None, None)
            o.idx = len(self.ops[e])
            for e2, l in last.items():
                if e2 != e or e != "pe":
                    o.deps[e2] = l
                    l.signal = True
            for c, l in lastdma.items():
                o.deps[c] = l
            self.ops[e].append(o)

    def emit(self, block, sems_ctx):
        nc = self.nc
        sem_eng = {e: sems_ctx["eng_" + e] for e in ("pe", "act", "dve", "pool")}
        sem_chan = {c: sems_ctx["ch_" + c] for c in self.chan_count}
        for e in ("pe", "act", "dve", "pool"):
            c = 0
            for o in self.ops[e]:
                if (not o.is_dma) and o.signal and o.fn is not None:
                    c += 1
                    o.sigval = c
                elif (not o.is_dma) and o.signal and o.fn is None:
                    o.sigval = c

        def run(e, engobj):
            waited = {}
            for o in self.ops[e]:
                for d in o.deps.values():
                    if d.is_dma:
                        sem = sem_chan[d.chan]
                        val = d.sigval
                        if d.chan == "const":
                            val = 16 * self.chan_count["const"]
                        key = "c" + d.chan
                    else:
                        sem = sem_eng[d.eng]
                        val = d.sigval
                        key = "e" + d.eng
                    if val <= 0:
                        continue
                    if waited.get(key, 0) < val:
                        engobj.wait_ge(sem, val)
                        waited[key] = val
                if o.fn is None:
                    continue
                ins = o.fn(engobj)
                if o.is_dma:
                    ins.then_inc(sem_chan[o.chan], 16)
                elif o.signal:
                    ins.then_inc(sem_eng[e], 1)

        @block.tensor
        def _(eng):
            run("pe", eng)

        @block.scalar
        def _(eng):
            run("act", eng)

        @block.vector
        def _(eng):
            run("dve", eng)

        @block.gpsimd
        def _(eng):
            run("pool", eng)

        @block.sync
        def _(eng):
            run("sp", eng)


def mk(base, free):
    d0 = list(base.ap)[0]
    return bass.AP(base.tensor, base.offset, [[d0[0], d0[1]]] + [[s, c] for (s, c) in free])


class KB:
    def __init__(self):
        nc = bass.Bass("TRN2", target_bir_lowering=False)
        self.nc = nc
        self.S = Sched(nc)
        self.din = {}
        self.dout = {}

    def inp(self, name, shape):
        t = self.nc.dram_tensor(name, list(shape), F32, kind="ExternalInput").ap()
        self.S.ignore.add(name)
        self.din[name] = t
        return t

    def outp(self, name, shape):
        t = self.nc.dram_tensor(name, list(shape), F32, kind="ExternalOutput").ap()
        self.S.ignore.add(name)
        self.dout[name] = t
        return t

    def mm(self, out, lhsT, rhs, start=True, stop=True):
        rd = [lhsT, rhs]
        return self.S.add("pe", lambda e: e.matmul(out, lhsT, rhs, start=start, stop=stop), rd, [out])

    def tr(self, out, in_, ident):
        return self.S.add("pe", lambda e: e.transpose(out, in_, ident), [in_, ident], [out])

    def act(self, out, in_, func, bias=None, scale=None, accum=None, eng="act"):
        rd = [in_]
        kw = {}
        if bias is not None:
            kw["bias"] = bias
            if not isinstance(bias, (int, float)):
                rd.append(bias)
        if scale is not None:
            kw["scale"] = scale
            if not isinstance(scale, (int, float)):
                rd.append(scale)
        wr = [out]
        if accum is not None:
            kw["accum_out"] = accum
            wr.append(accum)
        return self.S.add("act", lambda e: e.activation(out, in_, func, **kw), rd, wr)

    def tt(self, out, a, b, op, eng="dve"):
        return self.S.add(eng, lambda e: e.tensor_tensor(out, a, b, op), [a, b], [out])

    def ts(self, out, a, s1, s2, op0, op1=None, eng="dve"):
        rd = [a]
        for s in (s1, s2):
            if s is not None and not isinstance(s, (int, float)):
                rd.append(s)
        if op1 is None:
            return self.S.add(eng, lambda e: e.tensor_scalar(out, a, s1, None, op0), rd, [out])
        return self.S.add(eng, lambda e: e.tensor_scalar(out, a, s1, s2, op0, op1), rd, [out])

    def stt(self, out, a, s, b, op0, op1, eng="dve"):
        rd = [a, b]
        if not isinstance(s, (int, float)):
            rd.append(s)
        return self.S.add(eng, lambda e: e.scalar_tensor_tensor(out, a, s, b, op0, op1), rd, [out])

    def rsqrt(self, out, in_, scale, eps):
        self.act(out, in_, AF.Ln, bias=eps, scale=scale)
        self.act(out, out, AF.Exp, scale=-0.5)

    def cp(self, out, a, eng="dve"):
        if eng == "act":
            return self.S.add("act", lambda e: e.copy(out, a), [a], [out])
        return self.S.add(eng, lambda e: e.tensor_copy(out, a), [a], [out])

    def ms(self, out, val, eng="dve"):
        return self.S.add(eng, lambda e: e.memset(out, val), [], [out])

    def dma(self, out, in_, chan, eng="sp"):
        return self.S.add(eng, lambda e: e.dma_start(out=out, in_=in_), [in_], [out], chan=chan)


def build(dbg_stop=10 ** 9):
    K = KB()
    nc = K.nc
    S = K.S
    xpre = K.inp("xpre", [NLITE * 128, D])
    xmain = K.inp("xmain", [NMAIN * 128, D])
    xsamp = K.inp("xsamp", [128, D])
    flag = K.inp("flag", [128, 1])
    sstT = K.inp("sstT", [16, 128, 2048])
    sconvT = K.inp("sconvT", [128, 24, 48])
    fconvT = K.inp("fconvT", [128, 88, 32])
    w_in = K.inp("w_in", [D, IN_DIM])
    w_out = K.inp("w_out", [2 * D, D])
    w_up = K.inp("w_up", [D, 2 * DFF])
    w_down = K.inp("w_down", [DFF, D])
    vecs = K.inp("vecs", [128, 5, 16])
    nmo_d = K.inp("nmo", [128, D])
    nfo_d = K.inp("nfo", [128, D])
    wsTp_d = K.inp("wsTp", [128, 16, 128])
    wsTs_d = K.inp("wsTs", [128, 16, 128])
    bsp_d = K.inp("bsp", [128, D])
    bss_d = K.inp("bss", [128, D])
    scw_d = K.inp("scw", [128, 24, 5])
    fcw_d = K.inp("fcw", [128, 88, 4])
    h32_d = K.inp("h32", [128, 3, 32])
    cmat_d = K.inp("cmat", [128, 10, 128])

    ymain = K.outp("ymain", [NMAIN * 128, D])
    ysamp = K.outp("ysamp", [128, D])
    o_sst_p = K.outp("o_sst_p", [128, 2048])
    o_sconv_p = K.outp("o_sconv_p", [128, 24, 3])
    o_fconv_p = K.outp("o_fconv_p", [128, 88, 2])
    o_cv_p = K.outp("o_cv_p", [128, 16, 128])
    o_sst_s = K.outp("o_sst_s", [16, 128, 2048])
    o_sconv_s = K.outp("o_sconv_s", [128, 24, 48])
    o_fconv_s = K.outp("o_fconv_s", [128, 88, 32])
    o_cv_s = K.outp("o_cv_s", [128, 16, 128])
    xmid = nc.dram_tensor("xmid_scr", [(NMAIN + 1) * 128, D], F32, kind="Internal").ap()

    NSLOT = 2
    import contextlib

    with contextlib.ExitStack() as es:
        uid = [0]

        def un(name):
            uid[0] += 1
            return "t%d_%s" % (uid[0], name)

        def sb(name, shape, dt=F32):
            return es.enter_context(nc.sbuf_tensor(un(name), list(shape), dt))

        wslot = [sb("wslot%d" % i, [128, 8192], BF16) for i in range(NSLOT)]
        cm = sb("cm", [128, 10, 128])
        identb = sb("identb", [128, 128], BF16)
        Ub = {"p": sb("Ubp", [128, 128], BF16), "s": sb("Ubs", [128, 128], BF16)}
        Mnb = {"p": sb("Mnbp", [128, 128], BF16), "s": sb("Mnbs", [128, 128], BF16)}
        onesf = sb("onesf", [128, 128])
        onesb = sb("onesb", [128, 128], BF16)
        vec = sb("vec", [128, 5, 16])
        scw = sb("scw", [128, 24, 5])
        fcw = sb("fcw", [128, 88, 4])
        h32 = sb("h32", [128, 3, 32])
        arep = sb("arep", [128, 32])
        flg = sb("flg", [128, 1])
        wsT = {}
        St = sb("St", [128, 2048])
        Sb = sb("Sb", [128, 2048], BF16)
        hist_s = sb("hist_s", [128, 24, 3])
        hist_f = sb("hist_f", [128, 88, 2])
        hT = sb("hT", [128, 16, 512], BF16)
        BIG = {}
        DBGT = {}
        small = sb("small", [128, 64])
        ps = es.enter_context(nc.psum_tensor("psum_f", [128, 6, 512], F32))
        psb = es.enter_context(nc.psum_tensor("psum_b", [128, 2, 1024], BF16))

        identf = cm[:, 0, :]
        Umat = {"p": cm[:, 1, :], "s": cm[:, 2, :]}
        Tmat = {"p": cm[:, 3, :], "s": cm[:, 4, :]}
        Mneg = {"p": cm[:, 5, :], "s": cm[:, 6, :]}
        Msk = {"p": cm[:, 7, :], "s": cm[:, 8, :]}
        selj = cm[:, 9, 0:16]
        nmp = vec[:, 0, :]
        nfp = vec[:, 1, :]
        lng = vec[:, 2, :]
        lnb = vec[:, 3, :]
        snw = vec[:, 4, :]
        dtb = h32[:, 0, :]
        alog = h32[:, 1, :]
        dsk = h32[:, 2, :]

        def bank(i, n=512):
            return ps[:, i, 0:n]

        K.dma(cm[:, :, :], cmat_d[:, :, :], "const")
        K.dma(vec[:, :, :], vecs[:, :, :], "const")
        K.dma(scw[:, :, :], scw_d[:, :, :], "const")
        K.dma(fcw[:, :, :], fcw_d[:, :, :], "const")
        K.dma(h32[:, :, :], h32_d[:, :, :], "const")
        K.dma(flg[:, :], flag[:, :], "const")
        K.cp(identb[:, :], identf)
        for kd in ("p", "s"):
            K.cp(Ub[kd][:, :], Umat[kd])
            K.cp(Mnb[kd][:, :], Mneg[kd])
        K.ms(onesf[:, :], 1.0)
        K.ms(onesb[:, :], 1.0)
        K.act(arep[:, :], alog, AF.Exp)
        K.ts(arep[:, :], arep[:, :], -1.0, None, ALU.mult)
        K.ms(St[:, :], 0.0)
        K.ms(Sb[:, :], 0.0)
        K.ms(hist_s[:, :, :], 0.0)
        K.ms(hist_f[:, :, :], 0.0)

        wq = []

        def wdesc_cols(wd, c0, ncols, nk=16, r0=0):
            return ("c", wd, c0, ncols, nk, r0)

        class WS:
            def __init__(self):
                self.seq = []
                self.seq_phase = []
                self.plan_ph = 0
                self.exec_ph = 0
                self.issued = 0
                self.pos = 0
                self.slots = [(wslot[i], "b%d" % i) for i in range(NSLOT)]
                self.extras = []
                self.last = {}
                self.assigned = {}

            def plan_phase(self):
                self.plan_ph += 1

            def plan(self, d):
                self.seq.append(d)
                self.seq_phase.append(self.plan_ph)

            def begin_phase(self, extras=()):
                self.exec_ph += 1
                self.extras = [(t, "x%d" % i) for i, t in enumerate(extras)]

            def end_phase(self):
                while self.issued < len(self.seq) and self.issued - self.pos < 4:
                    if self.seq_phase[self.issued] == self.exec_ph:
                        break
                    if not self._issue(self.issued):
                        break
                    self.issued += 1
                self.extras = []

            def _issue(self, j):
                d = self.seq[j]
                cands = list(self.slots)
                if self.seq_phase[j] == self.exec_ph:
                    cands += self.extras
                best = None
                for (t, nm) in cands:
                    lb = self.last.get(nm, -1)
                    if lb < self.pos and (best is None or lb < best[2]):
                        best = (t, nm, lb)
                if best is None:
                    return False
                slot, nm, _ = best
                self.last[nm] = j
                self.assigned[j] = slot
                off = 0
                for pi, (wd, c0, ncols, nk, r0) in enumerate(d):
                    dst = slot[:, off:off + nk * ncols].rearrange("p (k n) -> p k n", k=nk)
                    src = wd[r0:r0 + nk * 128, c0:c0 + ncols].rearrange("(k p) n -> p k n", p=128)
                    K.dma(dst, src, "w%s_%d" % (nm, pi), eng="pool")
                    off += nk * ncols
                return True

            def get(self, d):
                i = self.pos
                assert self.seq[i] == d, (i, self.seq[i], d)
                assert self.seq_phase[i] == self.exec_ph, (i, self.seq_phase[i], self.exec_ph)
                while self.issued <= i:
                    ok = self._issue(self.issued)
                    assert ok
                    self.issued += 1
                while self.issued < len(self.seq) and self.issued - i <= 8:
                    if not self._issue(self.issued):
                        break
                    self.issued += 1
                self.pos += 1
                slot = self.assigned.pop(i)
                views = []
                off = 0
                for (wd, c0, ncols, nk, r0) in d:
                    views.append(slot[:, off:off + nk * ncols].rearrange("p (k n) -> p k n", k=nk))
                    off += nk * ncols
                return views

        ws = WS()

        def alloc_extras(pb, maxn):
            n = max(0, min(maxn, (nc.sbuf_bytes_remaining - 3072) // 16384))
            return [pb("wx%d" % i, [128, 8192], BF16) for i in range(n)]

        def D_in(c0, n):
            return [(w_in, c0, n, 16, 0)]

        def plan_lite():
            ws.plan_phase()
            for b in range(5):
                ws.plan(D_in(6144 + 512 * b, 512))
            ws.plan(D_in(9216, 32))

        def plan_full():
            ws.plan_phase()
            for b in range(4):
                ws.plan(D_in(2048 + 512 * b, 512))
            for b in range(4):
                ws.plan(D_in(512 * b, 512))
            ws.plan_phase()
            for b in range(4):
                ws.plan(D_in(4096 + 512 * b, 512))
            for b in range(6):
                ws.plan(D_in(6144 + 512 * b, 512))
            ws.plan(D_in(9216, 32))
            ws.plan_phase()
            for b in range(8):
                ws.plan([(w_out, 256 * b, 256, 32, 0)])
            ws.plan_phase()
            for b in range(22):
                ws.plan([(w_up, 256 * b, 256, 16, 0), (w_up, DFF + 256 * b, 256, 16, 0)])
            ws.plan_phase()
            for b in range(8):
                for kh in range(2):
                    ws.plan([(w_down, 256 * b, 256, 22, kh * 22 * 128)])

        def ptile(src, row, dst=None, xm_row=0):
            return dict(kind="p", src=src[row * 128:(row + 1) * 128, :], dst=dst, xm=xmid[xm_row * 128:(xm_row + 1) * 128, :])

        lite_groups = [[ptile(xpre, i) for i in range(0, 4)], [ptile(xpre, i) for i in range(4, 7)]]
        full_groups = [
            [ptile(xmain, i, ymain[i * 128:(i + 1) * 128, :], i) for i in range(0, 4)],
            [ptile(xmain, i, ymain[i * 128:(i + 1) * 128, :], i) for i in range(4, 8)],
            [ptile(xmain, i, ymain[i * 128:(i + 1) * 128, :], i) for i in range(8, 9)]
            + [dict(kind="s", src=xsamp[:, :], dst=ysamp[:, :], xm=xmid[9 * 128:10 * 128, :])],
        ]
        full_groups[2][0]["last"] = True
        for g in lite_groups:
            plan_lite()
        for g in full_groups:
            plan_full()

        cnt = {"x": 0, "o": 0}

        def rms_to_T(src_tile_ap, dstT, col0, wvec, xb, xsb, from_sb=None):
            junk = xsb
            ssq = small[:, 0:1]
            K.act(junk[:, :], xb[:, :], AF.Square, accum=ssq)
            rstd = small[:, 1:2]
            K.rsqrt(rstd, ssq, 1.0 / D, EPS)
            K.ts(xsb[:, :], xb[:, :], rstd, None, ALU.mult)
            for half in range(2):
                for c in range(8):
                    cc = half * 8 + c
                    K.tr(psb[:, half, c * 128:(c + 1) * 128], xsb[:, cc * 128:(cc + 1) * 128], identb[:, :])
                for c in range(8):
                    cc = half * 8 + c
                    K.act(dstT[:, cc, col0:col0 + 128], psb[:, half, c * 128:(c + 1) * 128], AF.Identity,
                          scale=wvec[:, cc:cc + 1])

        def phase0(tiles):
            with nc.sbuf_tensor(un("p0x0"), [128, D], F32) as x0, nc.sbuf_tensor(un("p0x1"), [128, D], F32) as x1, \
                    nc.sbuf_tensor(un("p0xs"), [128, D], BF16) as xsb:
                xbs = [x0, x1]
                for i, t in enumerate(tiles):
                    xb = xbs[i % 2]
                    K.dma(xb[:, :], t["src"], "x%d" % (i % 2))
                    rms_to_T(None, hT, i * 128, nmp, xb, xsb)
                S.barrier("phase0")

        def ssd_inproj(tiles, full, L):
            T = len(tiles)
            NT = 128 * T
            Tp = sum(1 for t in tiles if t["kind"] == "p")
            NTp = 128 * Tp
            has_s = Tp < T
            raw, raws, acc, cvo = L["raw"], L["raws"], L["acc"], L["cvo"]
            def mmpair(p, W):
                pp = p % 2
                for c in (2 * p, 2 * p + 1):
                    cc = c % 4
                    bk = bank(2 * pp + (c % 2), NT)
                    for k in range(16):
                        K.mm(bk, W[:, k, cc * 128:(cc + 1) * 128], hT[:, k, 0:NT], start=(k == 0), stop=(k == 15))

            def postpair(p):
                pp = p % 2
                cs_ = [2 * p, 2 * p + 1]
                for c in cs_:
                    r = raw[pp][c % 2]
                    bi = 2 * pp + (c % 2)
                    if NTp > 0:
                        K.cp(r[:, 0:3], hist_s[:, c, :])
                        K.cp(r[:, 3:3 + NTp], bank(bi, NTp), eng="act")
                        K.cp(hist_s[:, c, :], r[:, NTp:NTp + 3])
                    if has_s:
                        rs = raws[c % 2]
                        K.cp(rs[:, :, 0:3], L["sconv"][:, c, :].rearrange("p (j k) -> p j k", k=3))
                        K.cp(rs[:, :, 3:11], ps[:, bi, NTp:NTp + 128].rearrange("p (j t) -> p j t", t=8), eng="act")
                        K.cp(L["osc"][:, c, :].rearrange("p (j k) -> p j k", k=3), rs[:, :, 8:11])
                for kk in range(4):
                    for c in cs_:
                        r = raw[pp][c % 2]
                        if NTp > 0:
                            a = acc[pp][c % 2][:, 0:NTp]
                            if kk == 0:
                                K.ts(a, r[:, 0:NTp], scw[:, c, 0:1], scw[:, c, 4:5], ALU.mult, ALU.add)
                            else:
                                K.stt(a, r[:, kk:kk + NTp], scw[:, c, kk:kk + 1], a, ALU.mult, ALU.add)
                        if has_s:
                            rs = raws[c % 2]
                            a3 = acc[pp][c % 2][:, NTp:NTp + 128].rearrange("p (j t) -> p j t", t=8)
                            if kk == 0:
                                K.ts(a3, rs[:, :, 0:8], scw[:, c, 0:1], scw[:, c, 4:5], ALU.mult, ALU.add)
                            else:
                                K.stt(a3, rs[:, :, kk:kk + 8], scw[:, c, kk:kk + 1], a3, ALU.mult, ALU.add)
                for c in cs_:
                    av = acc[pp][c % 2][:, 0:NT]
                    if c < 20:
                        dstv = cvo[pp][c % 2][:, 0:NT] if c < 16 else L["BT"][:, c - 16, 0:NT]
                        K.act(dstv, av, AF.Silu)
                        hb = c % 2
                        for i in range(T):
                            K.tr(psb[:, hb, i * 128:(i + 1) * 128], dstv[:, i * 128:(i + 1) * 128], identb[:, :])
                        if c < 16:
                            dst3 = mk(L["xs"][:, 0, c * 128:(c + 1) * 128], [(2048, T), (1, 128)])
                        else:
                            dst3 = mk(L["Bt"][:, 0, (c - 16) * 128:(c - 15) * 128], [(512, T), (1, 128)])
                        K.cp(dst3, psb[:, hb, 0:NT].rearrange("p (i t) -> p i t", t=128))
                    else:
                        K.act(L["CT"][:, c - 20, 0:NT], av, AF.Silu)

            Wcur = None
            npair = 12 if full else 10
            for p in range(npair):
                if p % 2 == 0:
                    (Wcur,) = ws.get(D_in(6144 + 512 * (p // 2), 512))
                mmpair(p, Wcur)
                if p >= 1:
                    postpair(p - 1)
            postpair(npair - 1)
            (Wd,) = ws.get(D_in(9216, 32))
            for i in range(T):
                bk = ps[:, 2, 0:32]
                for k in range(16):
                    K.mm(bk, hT[:, k, i * 128:(i + 1) * 128], Wd[:, k, :], start=(k == 0), stop=(k == 15))
                K.tt(L["dtr"][:, i, :], bk, dtb, ALU.add)

        def ssd_scalars(tiles, L):
            T = len(tiles)
            Tp = sum(1 for t in tiles if t["kind"] == "p")
            sm = L["sm"]
            fl = lambda f: sm[:, f, :, :].rearrange("p a b -> p (a b)")
            e1, dt, da, ecs, ncs, dte, etot = (fl(6), fl(0), fl(1), fl(2), fl(3), fl(4), fl(5))
            K.act(e1, L["dtr"][:, 0:T, :].rearrange("p a b -> p (a b)"), AF.Exp)
            K.act(dt, e1, AF.Ln, bias=1.0)
            K.tt(sm[:, 1, :, :], sm[:, 0, :, :], mk(arep[:, :], [(0, T), (1, 32)]), ALU.mult)
            bc = ps[:, 0, :]
            n = 32 * T
            if Tp > 0:
                K.mm(bc[:, 0:32 * Tp], Umat["p"], da[:, 0:32 * Tp])
                K.mm(bc[:, 256:256 + 32 * Tp], Tmat["p"], da[:, 0:32 * Tp])
            if Tp < T:
                K.mm(bc[:, 32 * Tp:n], Umat["s"], da[:, 32 * Tp:n])
                K.mm(bc[:, 256 + 32 * Tp:256 + n], Tmat["s"], da[:, 32 * Tp:n])
            K.act(ncs, bc[:, 0:n], AF.Identity, scale=-1.0)
            K.act(ecs, bc[:, 0:n], AF.Exp)
            K.act(etot, bc[:, 256:256 + n], AF.Identity)
            K.tt(dte, etot, ncs, ALU.add)
            K.act(dte, dte, AF.Exp)
            K.act(etot, etot, AF.Exp)
            dah = L["dah"]
            hi = dah[:, 0, :, :].rearrange("p a b -> p (a b)")
            lo = dah[:, 1, :, :].rearrange("p a b -> p (a b)")
            K.cp(hi, da)
            K.tt(e1, da, hi, ALU.subtract)
            K.cp(lo, e1)

        def ssd_tile(i, t, full, L):
            kind = t["kind"]
            sm = L["sm"]
            dt = sm[:, 0, i, :]
            da = sm[:, 1, i, :]
            ecs = sm[:, 2, i, :]
            ncs = sm[:, 3, i, :]
            dte = sm[:, 4, i, :]
            etot = sm[:, 5, i, :]
            dah = L["dah"]
            xs_i = L["xs"][:, i, :]
            xs3 = xs_i.rearrange("p (h q) -> p h q", q=64)

            def b3(v):
                return mk(v, [(1, 32), (0, 64)])

            xdt, xdtd = L["xdt"], L["xdtd"]
            K.tt(xdt[:, :].rearrange("p (h q) -> p h q", q=64), xs3, b3(dt), ALU.mult, eng=PENG)
            K.tt(xdtd[:, :].rearrange("p (h q) -> p h q", q=64), xdt[:, :].rearrange("p (h q) -> p h q", q=64),
                 b3(dte), ALU.mult, eng=PENG)
            tc0 = i * 128
            ysb = L.get("ysb")
            if full:
                bcb = ps[:, 1, :]
                for g in range(4):
                    K.mm(bcb[:, g * 128:(g + 1) * 128], L["BT"][:, g, tc0:tc0 + 128], L["CT"][:, g, tc0:tc0 + 128])
                cbT = L["cbT"]
                K.cp(cbT[:, :], bcb, eng="act")
                if kind == "p":
                    for g in range(4):
                        bo = bank(2 + (g % 2))
                        K.mm(bo, L["CT"][:, g, tc0:tc0 + 128], Sb[:, g * 512:(g + 1) * 512])
                        K.tt(ysb[:, g * 512:(g + 1) * 512].rearrange("p (h q) -> p h q", q=64),
                             bo.rearrange("p (h q) -> p h q", q=64),
                             mk(ecs[:, g * 8:(g + 1) * 8], [(1, 8), (0, 64)]), ALU.mult)
                else:
                    sample_states(i, t, L, da, ecs)
                def stA(hq):
                    q = hq % 2
                    rh, rl = L["rhsU"][q]
                    K.tt(rh[:, :, :], mk(Ub[kind][:, :], [(0, 4), (1, 128)]),
                         mk(dah[:, 0, i, hq * 4:hq * 4 + 4], [(1, 4), (0, 128)]), ALU.mult, eng=PENG)
                    K.tt(rl[:, :, :], mk(Ub[kind][:, :], [(0, 4), (1, 128)]),
                         mk(dah[:, 1, i, hq * 4:hq * 4 + 4], [(1, 4), (0, 128)]), ALU.mult, eng=PENG)
                    bR = bank(4 + q)
                    K.mm(bR, onesb[:, :], rh[:, :, :].rearrange("p a b -> p (a b)"), start=True, stop=False)
                    K.mm(bR, onesb[:, :], rl[:, :, :].rearrange("p a b -> p (a b)"), start=False, stop=False)
                    K.mm(bR, identb[:, :], mk(Mnb[kind][:, :], [(0, 4), (1, 128)]), start=False, stop=True)

                def stB(hq):
                    q = hq % 2
                    bR = bank(4 + q)
                    df = L["Eh"][q]
                    K.tt(df[:, :, :], bR.rearrange("p (a b) -> p a b", b=128),
                         mk(ncs[:, hq * 4:hq * 4 + 4], [(1, 4), (0, 128)]), ALU.add)
                    E4 = L["E4"][q]
                    K.act(E4[:, :, :].rearrange("p a b -> p (a b)"), df[:, :, :].rearrange("p a b -> p (a b)"), AF.Exp)

                def stC(hq):
                    q = hq % 2
                    g = hq // 2
                    E4 = L["E4"][q]
                    L4 = L["Lh"][q]
                    K.tt(L4[:, :, :], E4[:, :, :], mk(cbT[:, g * 128:(g + 1) * 128], [(0, 4), (1, 128)]), ALU.mult)
                    for hh in range(4):
                        h = hq * 4 + hh
                        byd = ps[:, 2 + (g % 2), (h % 8) * 64:(h % 8) * 64 + 64]
                        K.mm(byd, L4[:, hh, :], xdt[:, h * 64:(h + 1) * 64])
                    if hq % 2 == 1:
                        K.tt(ysb[:, g * 512:(g + 1) * 512], ysb[:, g * 512:(g + 1) * 512], bank(2 + (g % 2)), ALU.add)

                for k in range(10):
                    if k < 8:
                        stA(k)
                    if 0 <= k - 1 < 8:
                        stB(k - 1)
                    if 0 <= k - 2 < 8:
                        stC(k - 2)
            if True:
                if kind == "p":
                    K.tt(St[:, :].rearrange("p (h q) -> p h q", q=64), St[:, :].rearrange("p (h q) -> p h q", q=64),
                         b3(etot), ALU.mult)
                    for g in range(4):
                        bu = bank(g % 2)
                        K.mm(bu, L["Bt"][:, i, g * 128:(g + 1) * 128], xdtd[:, g * 512:(g + 1) * 512])
                        K.tt(St[:, g * 512:(g + 1) * 512], St[:, g * 512:(g + 1) * 512], bu, ALU.add)
                    K.cp(Sb[:, :], St[:, :], eng="act")

            if full:
                tmp = L["tmp"]
                K.tt(tmp[:, :].rearrange("p (h q) -> p h q", q=64), xs3, b3(dsk), ALU.mult, eng=PENG)
                K.tt(ysb[:, :], ysb[:, :], tmp[:, :], ALU.add, eng=PENG)
                K.tt(ysb[:, :], ysb[:, :], L["sz"][:, i, :], ALU.mult, eng=PENG)
                ssq4 = sm[:, 7, i, 0:4]
                for g in range(4):
                    K.act(tmp[:, g * 512:(g + 1) * 512], ysb[:, g * 512:(g + 1) * 512], AF.Square,
                          accum=ssq4[:, g:g + 1])
                r4 = sm[:, 7, i, 4:8]
                K.rsqrt(r4, ssq4, 1.0 / 512, EPS)
                ynb = L["ynb"]
                for g in range(4):
                    K.act(ynb[:, g * 512:(g + 1) * 512], ysb[:, g * 512:(g + 1) * 512], AF.Identity, scale=r4[:, g:g + 1])
                for half in range(2):
                    for c in range(8):
                        cc = half * 8 + c
                        K.tr(psb[:, half, c * 128:(c + 1) * 128], ynb[:, cc * 128:(cc + 1) * 128], identb[:, :])
                    for c in range(8):
                        cc = half * 8 + c
                        K.act(BIG["mixT"][:, 16 + cc, tc0:tc0 + 128], psb[:, half, c * 128:(c + 1) * 128], AF.Identity,
                              scale=snw[:, cc:cc + 1])
        def sample_states(i, t, L, da, ecs):
            tc0 = i * 128
            sm = L["sm"]
            ysb = L.get("ysb")
            xdtd = L["xdtd"]
            daj = L["daj"]
            K.tt(daj[:, :, :], mk(da, [(0, 16), (1, 32)]), mk(selj, [(1, 16), (0, 32)]), ALU.mult)
            bE = bank(1)
            K.mm(bE, onesf[:, :], daj[:, :, :].rearrange("p a b -> p (a b)"))
            Ej = L["Ej"]
            K.act(Ej[:, :], bE, AF.Exp)
            yoT = ps[:, 2:6, :].rearrange("p a b -> p (a b)")
            for j0 in range(2):
                K.dma(L["sj"][j0][:, :], sstT[j0, :, :], "sj%d" % j0)
            for j in range(16):
                sj = L["sj"][j % 2]
                sjb = L["sjb"][j % 2]
                K.cp(sjb[:, :], sj[:, :], eng="act")
                for b in range(16):
                    g = b // 4
                    K.mm(yoT[:, b * 128 + j * 8:b * 128 + j * 8 + 8], sjb[:, b * 128:(b + 1) * 128],
                         L["CT"][:, g, tc0 + j * 8:tc0 + j * 8 + 8])
                Btm = L["Btm"][j % 2]
                K.ts(Btm[:, :], L["Bt"][:, i, :], selj[:, j:j + 1], None, ALU.mult)
                sn = L["sn"][j % 2]
                K.tt(sn[:, :].rearrange("p (h q) -> p h q", q=64), sj[:, :].rearrange("p (h q) -> p h q", q=64),
                     mk(Ej[:, j * 32:(j + 1) * 32], [(1, 32), (0, 64)]), ALU.mult)
                for g in range(4):
                    bu = bank(g % 2)
                    K.mm(bu, Btm[:, g * 128:(g + 1) * 128], xdtd[:, g * 512:(g + 1) * 512])
                    K.tt(sn[:, g * 512:(g + 1) * 512], sn[:, g * 512:(g + 1) * 512], bu, ALU.add)
                K.dma(o_sst_s[j, :, :], sn[:, :], "sn%d" % (j % 2))
                if j + 2 < 16:
                    K.dma(sj[:, :], sstT[j + 2, :, :], "sj%d" % (j % 2))
            yoS = L["tmp"]
            K.cp(yoS[:, :], yoT, eng="act")
            for b in range(16):
                bk = ps[:, 2 + (b // 4), (b % 4) * 128:(b % 4) * 128 + 128]
                K.mm(bk, yoS[:, b * 128:(b + 1) * 128], identf)
            for g in range(4):
                bo = bank(2 + g)
                K.cp(ysb[:, g * 512:(g + 1) * 512], bo, eng="act")
                K.tt(ysb[:, g * 512:(g + 1) * 512].rearrange("p (h q) -> p h q", q=64),
                     ysb[:, g * 512:(g + 1) * 512].rearrange("p (h q) -> p h q", q=64),
                     mk(ecs[:, g * 8:(g + 1) * 8], [(1, 8), (0, 64)]), ALU.mult)

        def ssd_phase(tiles, full):
            T = len(tiles)
            NT = 128 * T
            has_s = any(t["kind"] == "s" for t in tiles)
            with contextlib.ExitStack() as ph:
                def pb(name, shape, dt=F32):
                    return ph.enter_context(nc.sbuf_tensor(un(name), list(shape), dt))

                L = {}
                L["xs"] = pb("xs", [128, T, 2048], BF16)
                L["BT"] = pb("BT", [128, 4, 512], BF16)
                L["CT"] = pb("CT", [128, 4, 512], BF16)
                L["Bt"] = pb("Bt", [128, 4, 512], BF16)
                L["dtr"] = pb("dtr", [128, 4, 32])
                L["raw"] = [[pb("raw%d%d" % (pp, q), [128, 3 + 512]) for q in range(2)] for pp in range(2)]
                L["raws"] = [pb("raws%d" % q, [128, 16, 11]) for q in range(2)]
                L["acc"] = [[pb("acc%d%d" % (pp, q), [128, 512]) for q in range(2)] for pp in range(2)]
                L["cvo"] = [[pb("cvo%d%d" % (pp, q), [128, 512], BF16) for q in range(2)] for pp in range(2)]
                L["sm"] = pb("sm", [128, 8, T, 32])
                L["dah"] = pb("dah", [128, 2, T, 32], BF16)
                L["xdt"] = pb("xdt", [128, 2048], BF16)
                L["xdtd"] = pb("xdtd", [128, 2048], BF16)
                if full:
                    L["sz"] = pb("sz", [128, T, 2048], BF16)
                    L["ysb"] = pb("ysb", [128, 2048])
                    L["tmp"] = pb("tmp", [128, 2048], F32 if has_s else BF16)
                    L["ynb"] = L["xdtd"]
                    L["cbT"] = pb("cbT", [128, 512], BF16)
                    L["rhsU"] = [(pb("rhsUh%d" % q, [128, 4, 128], BF16), pb("rhsUl%d" % q, [128, 4, 128], BF16)) for q in range(2)]
                    L["Eh"] = [pb("Eh%d" % q, [128, 4, 128]) for q in range(2)]
                    L["E4"] = [pb("E4%d" % q, [128, 4, 128], BF16) for q in range(2)]
                    L["Lh"] = [pb("Lh%d" % q, [128, 4, 128], BF16) for q in range(2)]
                if has_s:
                    L["sconv"] = pb("sconv", [128, 24, 48])
                    L["osc"] = L["sconv"]
                    L["daj"] = pb("daj", [128, 16, 32])
                    L["Ej"] = pb("Ej", [128, 512])
                    L["sj"] = [pb("sj%d" % q, [128, 2048]) for q in range(2)]
                    L["sjb"] = [pb("sjb0", [128, 2048], BF16)] * 2
                    L["sn"] = L["sj"]
                    L["Btm"] = [pb("Btm%d" % q, [128, 512], BF16) for q in range(2)]
                    K.dma(L["sconv"][:, :, :], sconvT[:, :, :], "m1")
                ws.begin_phase(alloc_extras(pb, 4))
                if full:
                    for b in range(4):
                        (W,) = ws.get(D_in(4096 + 512 * b, 512))
                        for i in range(T):
                            bk = bank(2 + (i % 4))
                            for k in range(16):
                                K.mm(bk, hT[:, k, i * 128:(i + 1) * 128], W[:, k, :], start=(k == 0), stop=(k == 15))
                            K.act(L["sz"][:, i, b * 512:(b + 1) * 512], bk, AF.Silu)
                S.mark("  s.z")
                ssd_inproj(tiles, full, L)
                S.mark("  s.inproj")
                ssd_scalars(tiles, L)
                for i, t in enumerate(tiles):
                    ssd_tile(i, t, full, L)
                    S.mark("  s.tile")
                    if dbg_stop < 10 ** 8:
                        K.dma(ysamp[:, i * 224:(i + 1) * 224].rearrange("p (a b) -> p a b", b=32), L["sm"][:, 0:7, i, :], "dbg2")
                    if t.get("last"):
                        K.dma(o_sconv_p[:, :, :], hist_s[:, :, :], "m2")
                        K.dma(o_sst_p[:, :], St[:, :], "m3")
                if has_s:
                    K.dma(o_sconv_s[:, :, :], L["osc"][:, :, :], "m4")
                if dbg_stop < 10 ** 8:
                    K.dma(o_sst_p[:, :], St[:, :], "dbg1")
                    K.dma(ysamp[:, 1024:1024 + 32 * T], L["dtr"][:, 0:T, :].rearrange("p a b -> p (a b)"), "dbg3")
                    K.dma(ysamp[:, 384:384 + 512], L["Bt"][:, 0, :], "dbg4") if False else None
                ws.end_phase()
                S.barrier("ssd")

        STG = [None]

        def nrmf_stage(pb):
            if STG[0] is None:
                STG[0] = pb("nrmf", [128, 2048])
            return STG[0]

        def gate_phase(tiles):
            STG[0] = None
            T = len(tiles)
            NT = 128 * T
            Tp = sum(1 for t in tiles if t["kind"] == "p")
            NTp = 128 * Tp
            has_s = Tp < T
            with contextlib.ExitStack() as ph:
                def pb(name, shape, dt=F32):
                    return ph.enter_context(nc.sbuf_tensor(un(name), list(shape), dt))

                gv = pb("gv", [128, T, 2048], BF16)
                nrm = gv
                gsum = pb("gsum", [128, 4, 8])
                wsT["s"] = None
                wsT["p"] = pb("wsTp", [128, 16, 128], BF16)
                K.dma(nrmf_stage(pb)[:, :], wsTp_d.rearrange("p a b -> p (a b)"), "mwstp")
                K.tt(wsT["p"][:, :, :], mk(STG[0][:, :], [(128, 16), (1, 128)]),
                     mk(Msk["p"], [(0, 16), (1, 128)]), ALU.mult)
                if has_s:
                    wsT["s"] = pb("wsTs", [128, 16, 128], BF16)
                    K.dma(nrmf_stage(pb)[:, :], wsTs_d.rearrange("p a b -> p (a b)"), "mwsts")
                    K.tt(wsT["s"][:, :, :], mk(STG[0][:, :], [(128, 16), (1, 128)]),
                         mk(Msk["s"], [(0, 16), (1, 128)]), ALU.mult)
                st = pb("gst", [128, 8])
                gu = pb("gu", [128, 512])
                t2 = pb("t2", [128, 512])
                Qs = {"p": pb("Qsp", [128, 128]), "s": pb("Qss", [128, 128])}
                bsr = {"p": pb("bsrp", [128, 2048])}
                K.dma(bsr["p"][:, :], bsp_d[:, :], "m5")
                if has_s:
                    bsr["s"] = pb("bsrs", [128, 2048])
                    K.dma(bsr["s"][:, :], bss_d[:, :], "m6")
                junk = pb("gjunk", [128, 512], BF16)
                nrmf = nrmf_stage(pb)
                cvT = pb("cvT", [128, 16, 128])
                ws.begin_phase(alloc_extras(pb, 3))
                for b in range(4):
                    (W,) = ws.get(D_in(2048 + 512 * b, 512))
                    for i in range(T):
                        bk = bank(i % 4)
                        for k in range(16):
                            K.mm(bk, hT[:, k, i * 128:(i + 1) * 128], W[:, k, :], start=(k == 0), stop=(k == 15))
                        K.act(gv[:, i, b * 512:(b + 1) * 512], bk, AF.Gelu, accum=gsum[:, i, b:b + 1])
                        K.act(junk[:, :], gv[:, i, b * 512:(b + 1) * 512], AF.Square, accum=gsum[:, i, 4 + b:5 + b])
                for i, t in enumerate(tiles):
                    K.tt(st[:, 5:6], gsum[:, i, 4:5], gsum[:, i, 5:6], ALU.add)
                    K.tt(st[:, 6:7], gsum[:, i, 6:7], gsum[:, i, 7:8], ALU.add)
                    K.tt(st[:, 0:1], st[:, 5:6], st[:, 6:7], ALU.add)
                    K.tt(st[:, 1:2], gsum[:, i, 0:1], gsum[:, i, 1:2], ALU.add)
                    K.tt(st[:, 2:3], gsum[:, i, 2:3], gsum[:, i, 3:4], ALU.add)
                    K.tt(st[:, 1:2], st[:, 1:2], st[:, 2:3], ALU.add)
                    K.ts(st[:, 1:2], st[:, 1:2], 1.0 / D, None, ALU.mult)
                    K.tt(st[:, 2:3], st[:, 1:2], st[:, 1:2], ALU.mult)
                    K.stt(st[:, 3:4], st[:, 0:1], 1.0 / D, st[:, 2:3], ALU.mult, ALU.subtract)
                    K.rsqrt(st[:, 3:4], st[:, 3:4], 1.0, EPS)
                    K.stt(st[:, 4:5], st[:, 1:2], -1.0, st[:, 3:4], ALU.mult, ALU.mult)
                    if t.get("last") or t["kind"] == "s":
                        K.act(nrmf[:, :], gv[:, i, :], AF.Identity, bias=st[:, 4:5], scale=st[:, 3:4])
                        for h in range(16):
                            bk = ps[:, 4 + (h % 2), 0:128]
                            K.mm(bk, nrmf[:, h * 128:(h + 1) * 128], identf)
                            K.act(cvT[:, h, :], bk, AF.Identity, bias=lnb[:, h:h + 1], scale=lng[:, h:h + 1])
                        K.dma((o_cv_p if t["kind"] == "p" else o_cv_s)[:, :, :], cvT[:, :, :], "m7")
                    K.act(nrm[:, i, :], gv[:, i, :], AF.Identity, bias=st[:, 4:5], scale=st[:, 3:4])
                for b in range(4):
                    (W,) = ws.get(D_in(512 * b, 512))
                    for hh in range(4):
                        h = b * 4 + hh
                        bu = bank(4, NT)
                        for k in range(16):
                            K.mm(bu, W[:, k, hh * 128:(hh + 1) * 128], hT[:, k, 0:NT], start=(k == 0), stop=(k == 15))
                        K.act(gu[:, 0:NT], bu, AF.Gelu)
                        bs_ = bank(5, NT)
                        for i, t in enumerate(tiles):
                            K.mm(bs_[:, i * 128:(i + 1) * 128], nrm[:, i, h * 128:(h + 1) * 128], wsT[t["kind"]][:, h, :])
                        kinds = ["p"] + (["s"] if has_s else [])
                        for qi, kd in enumerate(kinds):
                            bq = ps[:, 0, qi * 128:(qi + 1) * 128]
                            K.mm(bq, onesb[:, :], wsT[kd][:, h, :])
                            K.stt(Qs[kd][:, :], bq, lnb[:, h:h + 1], bsr[kd][:, h * 128:(h + 1) * 128], ALU.mult, ALU.add)
                        if Tp > 0:
                            K.stt(t2[:, 0:NTp].rearrange("p (i t) -> p i t", t=128),
                                  bs_[:, 0:NTp].rearrange("p (i t) -> p i t", t=128), lng[:, h:h + 1],
                                  mk(Qs["p"][:, :], [(0, Tp), (1, 128)]), ALU.mult, ALU.add)
                        if has_s:
                            K.stt(t2[:, NTp:NT], bs_[:, NTp:NT], lng[:, h:h + 1], Qs["s"][:, :], ALU.mult, ALU.add)
                        K.tt(BIG["mixT"][:, h, 0:NT], t2[:, 0:NT], gu[:, 0:NT], ALU.mult)
                ws.end_phase()
                S.barrier("gate")

        def wout_phase(tiles):
            T = len(tiles)
            with contextlib.ExitStack() as ph:
                def pb(name, shape, dt=F32):
                    return ph.enter_context(nc.sbuf_tensor(un(name), list(shape), dt))

                mo = pb("mo", [128, T, 2048])
                with contextlib.ExitStack() as ph2:
                    def pb2(name, shape, dt=F32):
                        return ph2.enter_context(nc.sbuf_tensor(un(name), list(shape), dt))
                    ws.begin_phase(alloc_extras(pb2, 4))
                    for b in range(8):
                        (W,) = ws.get([(w_out, 256 * b, 256, 32, 0)])
                        for i in range(T):
                            bk = bank(i % 4, 256)
                            for k in range(32):
                                K.mm(bk, BIG["mixT"][:, k, i * 128:(i + 1) * 128], W[:, k, :], start=(k == 0), stop=(k == 31))
                            K.cp(mo[:, i, b * 256:(b + 1) * 256], bk, eng="act")
                    ws.end_phase()
                    S.barrier("wout_mm")
                nmo = pb("nmo_sb", [128, 2048])
                xb = [pb("p3x%d" % q, [128, 2048]) for q in range(2)]
                xm = [pb("p3m%d" % q, [128, 2048]) for q in range(2)]
                xsb = pb("p3xs", [128, 2048], BF16)
                K.dma(nmo[:, :], nmo_d[:, :], "m8")
                for i0 in range(min(2, T)):
                    K.dma(xb[i0][:, :], tiles[i0]["src"], "x%d" % i0)
                for i, t in enumerate(tiles):
                    x = xb[i % 2]
                    ssq = small[:, 2:3]
                    K.act(xsb[:, :], mo[:, i, :], AF.Square, accum=ssq)
                    rstd = small[:, 3:4]
                    K.rsqrt(rstd, ssq, 1.0 / D, EPS)
                    m = xm[i % 2]
                    K.stt(m[:, :], mo[:, i, :], rstd, nmo[:, :], ALU.mult, ALU.mult)
                    K.tt(m[:, :], m[:, :], x[:, :], ALU.add)
                    K.dma(t["xm"], m[:, :], "xm%d" % (i % 2))
                    if i + 2 < T:
                        K.dma(x[:, :], tiles[i + 2]["src"], "x%d" % (i % 2))
                    rms_to_T(None, hT, i * 128, nfp, m, xsb)
                S.barrier("wout")

        def up_phase(tiles):
            T = len(tiles)
            NT = 128 * T
            Tp = sum(1 for t in tiles if t["kind"] == "p")
            NTp = 128 * Tp
            has_s = Tp < T
            is_last = any(t.get("last") for t in tiles)
            with contextlib.ExitStack() as ph:
                def pb(name, shape, dt=F32):
                    return ph.enter_context(nc.sbuf_tensor(un(name), list(shape), dt))

                raw = [[pb("fr%d%d" % (q, z), [128, 2 + 512]) for z in range(2)] for q in range(2)]
                acc = [[pb("fa%d%d" % (q, z), [128, 512]) for z in range(2)] for q in range(2)]
                gg = [pb("fgg%d" % q, [128, 512]) for q in range(2)]
                if has_s:
                    raws = [[pb("frs%d%d" % (q, z), [128, 16, 10]) for z in range(2)] for q in range(2)]
                    fcv = pb("fcv", [128, 88, 32])
                    ofs = fcv
                    K.dma(fcv[:, :, :], fconvT[:, :, :], "m9")
                ws.begin_phase(alloc_extras(pb, 4))
                for b in range(22):
                    (Wg, Wu) = ws.get([(w_up, 256 * b, 256, 16, 0), (w_up, DFF + 256 * b, 256, 16, 0)])
                    for cc in range(2):
                        c = b * 2 + cc
                        q = c % 2
                        for z, Wz in enumerate((Wg, Wu)):
                            bk = bank(2 * q + z, NT)
                            for k in range(16):
                                K.mm(bk, Wz[:, k, cc * 128:(cc + 1) * 128], hT[:, k, 0:NT], start=(k == 0), stop=(k == 15))
                        for z in range(2):
                            ch = c + 44 * z
                            if NTp > 0:
                                r = raw[q][z]
                                K.cp(r[:, 0:2], hist_f[:, ch, :])
                                K.cp(r[:, 2:2 + NTp], bank(2 * q + z, NTp), eng="act")
                                K.cp(hist_f[:, ch, :], r[:, NTp:NTp + 2])
                            if has_s:
                                rs = raws[q][z]
                                K.cp(rs[:, :, 0:2], fcv[:, ch, :].rearrange("p (j k) -> p j k", k=2))
                                K.cp(rs[:, :, 2:10], ps[:, 2 * q + z, NTp:NTp + 128].rearrange("p (j t) -> p j t", t=8),
                                     eng="act")
                                K.cp(ofs[:, ch, :].rearrange("p (j k) -> p j k", k=2), rs[:, :, 8:10])
                        for kk in range(3):
                            for z in range(2):
                                ch = c + 44 * z
                                if NTp > 0:
                                    r = raw[q][z]
                                    a = acc[q][z][:, 0:NTp]
                                    if kk == 0:
                                        K.ts(a, r[:, 0:NTp], fcw[:, ch, 0:1], fcw[:, ch, 3:4], ALU.mult, ALU.add)
                                    else:
                                        K.stt(a, r[:, kk:kk + NTp], fcw[:, ch, kk:kk + 1], a, ALU.mult, ALU.add)
                                if has_s:
                                    rs = raws[q][z]
                                    a3 = acc[q][z][:, NTp:NTp + 128].rearrange("p (j t) -> p j t", t=8)
                                    if kk == 0:
                                        K.ts(a3, rs[:, :, 0:8], fcw[:, ch, 0:1], fcw[:, ch, 3:4], ALU.mult, ALU.add)
                                    else:
                                        K.stt(a3, rs[:, :, kk:kk + 8], fcw[:, ch, kk:kk + 1], a3, ALU.mult, ALU.add)
                        K.act(gg[q][:, 0:NT], acc[q][0][:, 0:NT], AF.Gelu_apprx_tanh)
                        K.tt(BIG["actT"][:, c, 0:NT], gg[q][:, 0:NT], acc[q][1][:, 0:NT], ALU.mult)
                if is_last:
                    K.dma(o_fconv_p[:, :, :], hist_f[:, :, :], "m10")
                if has_s:
                    K.dma(o_fconv_s[:, :, :], ofs[:, :, :], "m11")
                ws.end_phase()
                S.barrier("up")

        def down_phase(tiles):
            T = len(tiles)

            def bankx(i):
                if i < 6:
                    return ps[:, i, 0:256]
                return psb[:, i - 6, :].bitcast(F32)[:, 0:256]

            with contextlib.ExitStack() as ph:
                def pb(name, shape, dt=F32):
                    return ph.enter_context(nc.sbuf_tensor(un(name), list(shape), dt))

                fo = pb("fo", [128, T, 2048])
                with contextlib.ExitStack() as ph2:
                    def pb2(name, shape, dt=F32):
                        return ph2.enter_context(nc.sbuf_tensor(un(name), list(shape), dt))
                    ws.begin_phase(alloc_extras(pb2, 4))
                    for b in range(8):
                        base = (b % 2) * 4 if T <= 4 else 0
                        for kh in range(2):
                            (W,) = ws.get([(w_down, 256 * b, 256, 22, kh * 22 * 128)])
                            for i in range(T):
                                bk = bankx(base + i)
                                for k in range(22):
                                    K.mm(bk, BIG["actT"][:, kh * 22 + k, i * 128:(i + 1) * 128], W[:, k, :],
                                         start=(kh == 0 and k == 0), stop=(kh == 1 and k == 21))
                        for i in range(T):
                            K.cp(fo[:, i, b * 256:(b + 1) * 256], bankx(base + i), eng="act")
                    ws.end_phase()
                    S.barrier("down_mm")
                nfo = pb("nfo_sb", [128, 2048])
                xm = [pb("p5m%d" % q, [128, 2048]) for q in range(2)]
                yo = [pb("p5y%d" % q, [128, 2048]) for q in range(2)]
                K.dma(nfo[:, :], nfo_d[:, :], "m12")
                for i0 in range(min(2, T)):
                    K.dma(xm[i0][:, :], tiles[i0]["xm"], "xr%d" % i0)
                for i, t in enumerate(tiles):
                    m = xm[i % 2]
                    ssq = small[:, 4:5]
                    y = yo[i % 2]
                    K.act(y[:, :], fo[:, i, :], AF.Square, accum=ssq)
                    rstd = small[:, 5:6]
                    K.rsqrt(rstd, ssq, 1.0 / D, EPS)
                    K.stt(y[:, :], fo[:, i, :], rstd, nfo[:, :], ALU.mult, ALU.mult)
                    K.tt(y[:, :], y[:, :], m[:, :], ALU.add)
                    K.dma(t["dst"], y[:, :], "yo%d" % (i % 2))
                    if i + 2 < T:
                        K.dma(m[:, :], tiles[i + 2]["xm"], "xr%d" % (i % 2))
                S.barrier("down")

        prog = []
        prog.append(lambda: S.barrier())
        for g in lite_groups:
            prog.append(lambda g=g: phase0(g))
            prog.append(lambda g=g: ssd_phase(g, full=False))

        def _maskstate():
            K.ts(St[:, :], St[:, :], flg[:, 0:1], None, ALU.mult)
            K.cp(Sb[:, :], St[:, :], eng="act")
        prog.append(_maskstate)
        for g in full_groups:
            Tg = len(g)
            prog.append(lambda g=g: phase0(g))

            def _mix(g=g, Tg=Tg):
                with nc.sbuf_tensor(un("mixT"), [128, 32, 128 * Tg], BF16) as mixT:
                    BIG["mixT"] = mixT
                    for fn in (gate_phase, lambda gg: ssd_phase(gg, full=True), wout_phase):
                        if DBG["n"] <= 0:
                            break
                        DBG["n"] -= 1
                        fn(g)

            def _ffn(g=g, Tg=Tg):
                with nc.sbuf_tensor(un("actT"), [128, 44, 128 * Tg], BF16) as actT:
                    BIG["actT"] = actT
                    for fn in (up_phase, down_phase):
                        if DBG["n"] <= 0:
                            break
                        DBG["n"] -= 1
                        fn(g)
            prog.append(_mix)
            prog.append(_ffn)
        DBG = {"n": dbg_stop}
        for fn in prog:
            if DBG["n"] <= 0:
                break
            if fn.__name__ in ("_mix", "_ffn"):
                fn()
            else:
                DBG["n"] -= 1
                fn()
        S.barrier()

        sem_names = ["eng_pe", "eng_act", "eng_dve", "eng_pool"] + ["ch_" + c for c in S.chan_count]
        sems = {}
        for n in sem_names:
            sems[n] = es.enter_context(nc.semaphore(n))
        with nc.Block() as block:
            S.emit(block, sems)
    return nc


_NC_CACHE = {}


def _consts():
    t = np.arange(128)
    ident = np.eye(128, dtype=np.float32)
    Up = (t[:, None] <= t[None, :]).astype(np.float32)
    same = (t[:, None] // 8) == (t[None, :] // 8)
    Us = (Up.astype(bool) & same).astype(np.float32)
    Tp = np.ones((128, 128), np.float32)
    Ts = same.astype(np.float32)
    mskp = Up.copy()
    msks = Us.copy()
    Mnp = np.where(mskp > 0, 0.0, NEG).astype(np.float32)
    Mns = np.where(msks > 0, 0.0, NEG).astype(np.float32)
    selj = np.zeros((128, 128), np.float32)
    selj[t, t // 8] = 1.0
    return np.stack([ident, Up, Us, Tp, Ts, Mnp, Mns, mskp, msks, selj], axis=1).astype(np.float32)


def kernel(x_prompt, x_sample, state_ssm, state_ssm_conv, state_ffn_conv,
           norm_mix_pre, w_in, gate_ln_g, gate_ln_b, gate_w_s, gate_b_s,
           ssm_conv_w, ssm_conv_b, ssm_dt_bias, ssm_a_log, ssm_d, ssm_norm_w,
           w_out, norm_mix_post, norm_ffn_pre, ffn_w_up, ffn_conv_w, ffn_conv_b,
           ffn_w_down, norm_ffn_post, _runner=None):
    f = np.float32
    A = lambda a: np.ascontiguousarray(np.asarray(a, dtype=f))
    x_prompt = A(x_prompt); x_sample = A(x_sample)

    def fm(v, n):
        return A(np.asarray(v, f).reshape(n, 128).T)

    vecs = np.stack([fm(norm_mix_pre[0], 16), fm(norm_ffn_pre[0], 16), fm(gate_ln_g[0], 16),
                     fm(gate_ln_b[0], 16), fm(ssm_norm_w[0], 16)], axis=1)
    rep = lambda v: A(np.broadcast_to(np.asarray(v, f).reshape(1, -1), (128, np.asarray(v).size)))
    ws = np.asarray(gate_w_s[0], f)
    wsTp = A(ws.transpose(2, 0, 1))
    wsTs = np.zeros((128, 16, 128), f)
    blk = ws[:, :8, :8].transpose(2, 0, 1)
    for j in range(16):
        wsTs[8 * j:8 * j + 8, :, 8 * j:8 * j + 8] = blk
    bs = np.asarray(gate_b_s[0], f)
    bsp = rep(bs.reshape(-1))
    bss = rep(np.tile(bs[:, :8], (1, 16)).reshape(-1))
    scw = np.concatenate([np.asarray(ssm_conv_w[0], f), np.asarray(ssm_conv_b[0], f)[None]], axis=0)
    scw = A(scw.reshape(5, 24, 128).transpose(2, 1, 0))
    fcw = np.concatenate([np.asarray(ffn_conv_w[0], f), np.asarray(ffn_conv_b[0], f)[None]], axis=0)
    fcw = A(fcw.reshape(4, 88, 128).transpose(2, 1, 0))
    h32 = np.stack([rep(ssm_dt_bias[0]), rep(ssm_a_log[0]), rep(ssm_d[0])], axis=1)
    cmat = _consts()
    common = dict(w_in=A(w_in[0]), w_out=A(w_out[0]), w_up=A(ffn_w_up[0]), w_down=A(ffn_w_down[0]),
                  vecs=A(vecs), nmo=rep(norm_mix_post[0]), nfo=rep(norm_ffn_post[0]), wsTp=wsTp, wsTs=wsTs,
                  bsp=bsp, bss=bss, scw=scw, fcw=fcw, h32=A(h32), cmat=cmat)
    in_maps = []
    for core in range(8):
        b = core // 2
        second = core % 2
        m = dict(common)
        if second == 0:
            m["xpre"] = np.zeros((NLITE * 128, D), f)
            m["xmain"] = A(x_prompt[b, 0:NMAIN * 128])
            m["flag"] = np.zeros((128, 1), f)
        else:
            m["xpre"] = A(x_prompt[b, 0:NLITE * 128])
            m["xmain"] = A(x_prompt[b, NLITE * 128:2048])
            m["flag"] = np.ones((128, 1), f)
        js = slice(16 * core, 16 * core + 16)
        m["xsamp"] = A(x_sample[js].reshape(128, D))
        m["sstT"] = A(np.asarray(state_ssm[0, js], f).reshape(16, 2048, 128).transpose(0, 2, 1))
        m["sconvT"] = A(np.asarray(state_ssm_conv[0, js], f).reshape(48, 24, 128).transpose(2, 1, 0))
        m["fconvT"] = A(np.asarray(state_ffn_conv[0, js], f).reshape(32, 88, 128).transpose(2, 1, 0))
        in_maps.append(m)
    if _runner is not None:
        R = _runner(in_maps)
    else:
        if "nc" not in _NC_CACHE:
            _NC_CACHE["nc"] = build()
        nc = _NC_CACHE["nc"]
        res = run_bass_kernel_spmd(nc, in_maps, core_ids=list(range(8)))
        R = res.results
    y_p = np.zeros((4, 2048, D), f)
    y_s = np.zeros((128, 8, D), f)
    sst_p = np.zeros((1, 4, 32, 64, 128), f)
    sconv_p = np.zeros((1, 4, 3, 3072), f)
    fconv_p = np.zeros((1, 4, 2, 11264), f)
    cv_p = np.zeros((1, 4, 128, 16, 128), f)
    sst_s = np.zeros((1, 128, 32, 64, 128), f)
    sconv_s = np.zeros((1, 128, 3, 3072), f)
    fconv_s = np.zeros((1, 128, 2, 11264), f)
    cv_s = np.zeros((1, 128, 8, 16, 128), f)
    for core in range(8):
        b = core // 2
        second = core % 2
        r = R[core]
        ym = np.asarray(r["ymain"]).reshape(NMAIN * 128, D)
        if second == 0:
            y_p[b, 0:NMAIN * 128] = ym
        else:
            y_p[b, NMAIN * 128:2048] = ym[(NMAIN * 128 - (2048 - NMAIN * 128)):]
            sst_p[0, b] = np.asarray(r["o_sst_p"]).reshape(128, 2048).T.reshape(32, 64, 128)
            sconv_p[0, b] = np.asarray(r["o_sconv_p"]).reshape(128, 24, 3).transpose(2, 1, 0).reshape(3, 3072)
            fconv_p[0, b] = np.asarray(r["o_fconv_p"]).reshape(128, 88, 2).transpose(2, 1, 0).reshape(2, 11264)
            cv_p[0, b] = np.asarray(r["o_cv_p"]).reshape(128, 16, 128).transpose(2, 1, 0)
        js = slice(16 * core, 16 * core + 16)
        y_s[js] = np.asarray(r["ysamp"]).reshape(16, 8, D)
        sst_s[0, js] = np.asarray(r["o_sst_s"]).reshape(16, 128, 2048).transpose(0, 2, 1).reshape(16, 32, 64, 128)
        sconv_s[0, js] = np.asarray(r["o_sconv_s"]).reshape(128, 24, 16, 3).transpose(2, 3, 1, 0).reshape(16, 3, 3072)
        fconv_s[0, js] = np.asarray(r["o_fconv_s"]).reshape(128, 88, 16, 2).transpose(2, 3, 1, 0).reshape(16, 2, 11264)
        cv_s[0, js] = np.asarray(r["o_cv_s"]).reshape(128, 16, 16, 8).transpose(2, 3, 1, 0)
    return (y_p, y_s, sst_p, sconv_p, fconv_p, cv_p, sst_s, sconv_s, fconv_s, cv_s)
```

```python
import numpy as np
import concourse.bass as bass
import concourse.mybir as mybir
from concourse.bass_utils import run_bass_kernel_spmd

F32 = mybir.dt.float32
BF16 = mybir.dt.bfloat16
AF = mybir.ActivationFunctionType
ALU = mybir.AluOpType

D = 2048
IN_DIM = 9248
DFF = 5632
EPS = 1e-6
NLITE = 7
NMAIN = 9
NEG = -30000.0
PENG = "pool"


def _esz(dt):
    s = str(dt)
    if "bfloat16" in s or "float16" in s:
        return 2
    return 4


class Op:
    __slots__ = ("eng", "fn", "chan", "deps", "signal", "sigval", "is_dma", "idx")

    def __init__(self, eng, fn, chan):
        self.eng = eng
        self.fn = fn
        self.chan = chan
        self.deps = {}
        self.signal = False
        self.sigval = 0
        self.is_dma = chan is not None


class Sched:
    ENGS = ("pe", "act", "dve", "pool", "sp")

    def __init__(self, nc):
        self.nc = nc
        self.ops = {e: [] for e in self.ENGS}
        self.track = {}
        self.chan_count = {}
        self.ignore = set()

    def region(self, ap):
        t = ap.tensor
        name = t.name
        if name in self.ignore:
            return None
        e = _esz(ap.dtype)
        shape = list(t.shape)
        rowb = 1
        for s in shape[1:]:
            rowb *= s
        rowb *= _esz(t.dtype)
        off = int(ap.offset) * e
        dims = list(ap.ap)
        p0 = off // rowb
        f0 = off % rowb
        npart = dims[0][1]
        lo = f0
        hi = f0
        for (st, cn) in dims[1:]:
            ext = st * (cn - 1) * e
            if ext < 0:
                lo += ext
            else:
                hi += ext
        hi += e
        if name.startswith("psum_"):
            lo = (lo // 2048) * 2048
            hi = ((hi + 2047) // 2048) * 2048
            return name, (0, 128, lo, hi)
        return name, (p0, p0 + npart, lo, hi)

    @staticmethod
    def _ov(a, b):
        return a[0] < b[1] and b[0] < a[1] and a[2] < b[3] and b[2] < a[3]

    @staticmethod
    def _contains(a, b):
        return a[0] <= b[0] and a[1] >= b[1] and a[2] <= b[2] and a[3] >= b[3]

    def add(self, eng, fn, reads=(), writes=(), chan=None):
        o = Op(eng, fn, chan)
        okey = chan if chan is not None else eng
        accs = []
        for ap in reads:
            r = self.region(ap)
            if r is not None:
                accs.append((False, r))
        for ap in writes:
            r = self.region(ap)
            if r is not None:
                accs.append((True, r))
        for (isw, (name, reg)) in accs:
            lst = self.track.get(name)
            if not lst:
                continue
            for ent in lst:
                (w2, reg2, key2, op2) = ent
                if not (isw or w2):
                    continue
                if not self._ov(reg, reg2):
                    continue
                if op2 is o:
                    continue
                need = True
                if (not op2.is_dma) and (not o.is_dma) and op2.eng == o.eng:
                    if o.eng == "pe":
                        need = False
                if need:
                    k = key2
                    prev = o.deps.get(k)
                    if prev is None or prev.idx < op2.idx:
                        o.deps[k] = op2
        o.idx = len(self.ops[eng])
        if o.is_dma:
            c = self.chan_count.get(chan, 0) + 1
            self.chan_count[chan] = c
            o.sigval = 16 * c
            o.idx = c
            o.signal = True
        for (isw, (name, reg)) in accs:
            lst = self.track.setdefault(name, [])
            if isw:
                lst[:] = [en for en in lst if not self._contains(reg, en[1])]
                lst.append((True, reg, okey, o))
            else:
                rep = False
                for i, en in enumerate(lst):
                    if (not en[0]) and en[2] == okey and en[1] == reg:
                        lst[i] = (False, reg, okey, o)
                        rep = True
                        break
                if not rep:
                    lst.append((False, reg, okey, o))
        for d in o.deps.values():
            d.signal = True
        self.ops[eng].append(o)
        return o

    def mark(self, label):
        if not hasattr(self, "marks"):
            self.marks = []
        self.marks.append((label, sum(1 for o in self.ops["pe"] if o.fn is not None)))

    def barrier(self, label=""):
        if not hasattr(self, "marks"):
            self.marks = []
        self.marks.append((label, sum(1 for o in self.ops["pe"] if o.fn is not None)))
        last = {}
        for e in self.ENGS:
            for o in reversed(self.ops[e]):
                if (not o.is_dma) and o.fn is not None:
                    last[e] = o
                    break
        lastdma = {}
        for e in self.ENGS:
            for o in self.ops[e]:
                if o.is_dma:
                    lastdma[o.chan] = o
        for e in self.ENGS:
            o = Op(e, None, None)
            o.idx = len(self.ops[e])
            for e2, l in last.items():
                if e2 != e or e != "pe":
                    o.deps[e2] = l
                    l.signal = True
            for c, l in lastdma.items():
                o.deps[c] = l
            self.ops[e].append(o)

    def emit(self, block, sems_ctx):
        nc = self.nc
        sem_eng = {e: sems_ctx["eng_" + e] for e in ("pe", "act", "dve", "pool")}
        sem_chan = {c: sems_ctx["ch_" + c] for c in self.chan_count}
        for e in ("pe", "act", "dve", "pool"):
            c = 0
            for o in self.ops[e]:
                if (not o.is_dma) and o.signal and o.fn is not None:
                    c += 1
                    o.sigval = c
                elif (not o.is_dma) and o.signal and o.fn is None:
                    o.sigval = c

        def run(e, engobj):
            waited = {}
            for o in self.ops[e]:
                for d in o.deps.values():
                    if d.is_dma:
                        sem = sem_chan[d.chan]
                        val = d.sigval
                        if d.chan == "const":
                            val = 16 * self.chan_count["const"]
                        key = "c" + d.chan
                    else:
                        sem = sem_eng[d.eng]
                        val = d.sigval
                        key = "e" + d.eng
                    if val <= 0:
                        continue
                    if waited.get(key, 0) < val:
                        engobj.wait_ge(sem, val)
                        waited[key] = val
                if o.fn is None:
                    continue
                ins = o.fn(engobj)
                if o.is_dma:
                    ins.then_inc(sem_chan[o.chan], 16)
                elif o.signal:
                    ins.then_inc(sem_eng[e], 1)

        @block.tensor
        def _(eng):
            run("pe", eng)

        @block.scalar
        def _(eng):
            run("act", eng)

        @block.vector
        def _(eng):
            run("dve", eng)

        @block.gpsimd
        def _(eng):
            run("pool", eng)

        @block.sync
        def _(eng):
            run("sp", eng)


def mk(base, free):
    d0 = list(base.ap)[0]
    return bass.AP(base.tensor, base.offset, [[d0[0], d0[1]]] + [[s, c] for (s, c) in free])


class KB:
    def __init__(self):
        nc = bass.Bass("TRN2", target_bir_lowering=False)
        self.nc = nc
        self.S = Sched(nc)
        self.din = {}
        self.dout = {}

    def inp(self, name, shape):
        t = self.nc.dram_tensor(name, list(shape), F32, kind="ExternalInput").ap()
        self.S.ignore.add(name)
        self.din[name] = t
        return t

    def outp(self, name, shape):
        t = self.nc.dram_tensor(name, list(shape), F32, kind="ExternalOutput").ap()
        self.S.ignore.add(name)
        self.dout[name] = t
        return t

    def mm(self, out, lhsT, rhs, start=True, stop=True):
        rd = [lhsT, rhs]
        return self.S.add("pe", lambda e: e.matmul(out, lhsT, rhs, start=start, stop=stop), rd, [out])

    def tr(self, out, in_, ident):
        return self.S.add("pe", lambda e: e.transpose(out, in_, ident), [in_, ident], [out])

    def act(self, out, in_, func, bias=None, scale=None, accum=None, eng="act"):
        rd = [in_]
        kw = {}
        if bias is not None:
            kw["bias"] = bias
            if not isinstance(bias, (int, float)):
                rd.append(bias)
        if scale is not None:
            kw["scale"] = scale
            if not isinstance(scale, (int, float)):
                rd.append(scale)
        wr = [out]
        if accum is not None:
            kw["accum_out"] = accum
            wr.append(accum)
        return self.S.add("act", lambda e: e.activation(out, in_, func, **kw), rd, wr)

    def tt(self, out, a, b, op, eng="dve"):
        return self.S.add(eng, lambda e: e.tensor_tensor(out, a, b, op), [a, b], [out])

    def ts(self, out, a, s1, s2, op0, op1=None, eng="dve"):
        rd = [a]
        for s in (s1, s2):
            if s is not None and not isinstance(s, (int, float)):
                rd.append(s)
        if op1 is None:
            return self.S.add(eng, lambda e: e.tensor_scalar(out, a, s1, None, op0), rd, [out])
        return self.S.add(eng, lambda e: e.tensor_scalar(out, a, s1, s2, op0, op1), rd, [out])

    def stt(self, out, a, s, b, op0, op1, eng="dve"):
        rd = [a, b]
        if not isinstance(s, (int, float)):
            rd.append(s)
        return self.S.add(eng, lambda e: e.scalar_tensor_tensor(out, a, s, b, op0, op1), rd, [out])

    def rsqrt(self, out, in_, scale, eps):
        self.act(out, in_, AF.Ln, bias=eps, scale=scale)
        self.act(out, out, AF.Exp, scale=-0.5)

    def cp(self, out, a, eng="dve"):
        if eng == "act":
            return self.S.add("act", lambda e: e.copy(out, a), [a], [out])
        return self.S.add(eng, lambda e: e.tensor_copy(out, a), [a], [out])

    def ms(self, out, val, eng="dve"):
        return self.S.add(eng, lambda e: e.memset(out, val), [], [out])

    def dma(self, out, in_, chan, eng="sp"):
        return self.S.add(eng, lambda e: e.dma_start(out=out, in_=in_), [in_], [out], chan=chan)


def build(dbg_stop=10 ** 9):
    K = KB()
    nc = K.nc
    S = K.S
    xpre = K.inp("xpre", [NLITE * 128, D])
    xmain = K.inp("xmain", [NMAIN * 128, D])
    xsamp = K.inp("xsamp", [128, D])
    flag = K.inp("flag", [128, 1])
    sstT = K.inp("sstT", [16, 128, 2048])
    sconvT = K.inp("sconvT", [128, 24, 48])
    fconvT = K.inp("fconvT", [128, 88, 32])
    w_in = K.inp("w_in", [D, IN_DIM])
    w_out = K.inp("w_out", [2 * D, D])
    w_up = K.inp("w_up", [D, 2 * DFF])
    w_down = K.inp("w_down", [DFF, D])
    vecs = K.inp("vecs", [128, 5, 16])
    nmo_d = K.inp("nmo", [128, D])
    nfo_d = K.inp("nfo", [128, D])
    wsTp_d = K.inp("wsTp", [128, 16, 128])
    wsTs_d = K.inp("wsTs", [128, 16, 128])
    bsp_d = K.inp("bsp", [128, D])
    bss_d = K.inp("bss", [128, D])
    scw_d = K.inp("scw", [128, 24, 5])
    fcw_d = K.inp("fcw", [128, 88, 4])
    h32_d = K.inp("h32", [128, 3, 32])
    cmat_d = K.inp("cmat", [128, 10, 128])

    ymain = K.outp("ymain", [NMAIN * 128, D])
    ysamp = K.outp("ysamp", [128, D])
    o_sst_p = K.outp("o_sst_p", [128, 2048])
    o_sconv_p = K.outp("o_sconv_p", [128, 24, 3])
    o_fconv_p = K.outp("o_fconv_p", [128, 88, 2])
    o_cv_p = K.outp("o_cv_p", [128, 16, 128])
    o_sst_s = K.outp("o_sst_s", [16, 128, 2048])
    o_sconv_s = K.outp("o_sconv_s", [128, 24, 48])
    o_fconv_s = K.outp("o_fconv_s", [128, 88, 32])
    o_cv_s = K.outp("o_cv_s", [128, 16, 128])
    xmid = nc.dram_tensor("xmid_scr", [(NMAIN + 1) * 128, D], F32, kind="Internal").ap()

    NSLOT = 2
    import contextlib

    with contextlib.ExitStack() as es:
        uid = [0]

        def un(name):
            uid[0] += 1
            return "t%d_%s" % (uid[0], name)

        def sb(name, shape, dt=F32):
            return es.enter_context(nc.sbuf_tensor(un(name), list(shape), dt))

        wslot = [sb("wslot%d" % i, [128, 8192], BF16) for i in range(NSLOT)]
        cm = sb("cm", [128, 10, 128])
        identb = sb("identb", [128, 128], BF16)
        Ub = {"p": sb("Ubp", [128, 128], BF16), "s": sb("Ubs", [128, 128], BF16)}
        Mnb = {"p": sb("Mnbp", [128, 128], BF16), "s": sb("Mnbs", [128, 128], BF16)}
        onesf = sb("onesf", [128, 128])
        onesb = sb("onesb", [128, 128], BF16)
        vec = sb("vec", [128, 5, 16])
        scw = sb("scw", [128, 24, 5])
        fcw = sb("fcw", [128, 88, 4])
        h32 = sb("h32", [128, 3, 32])
        arep = sb("arep", [128, 32])
        flg = sb("flg", [128, 1])
        wsT = {}
        St = sb("St", [128, 2048])
        Sb = sb("Sb", [128, 2048], BF16)
        hist_s = sb("hist_s", [128, 24, 3])
        hist_f = sb("hist_f", [128, 88, 2])
        hT = sb("hT", [128, 16, 512], BF16)
        BIG = {}
        DBGT = {}
        small = sb("small", [128, 64])
        ps = es.enter_context(nc.psum_tensor("psum_f", [128, 6, 512], F32))
        psb = es.enter_context(nc.psum_tensor("psum_b", [128, 2, 1024], BF16))

        identf = cm[:, 0, :]
        Umat = {"p": cm[:, 1, :], "s": cm[:, 2, :]}
        Tmat = {"p": cm[:, 3, :], "s": cm[:, 4, :]}
        Mneg = {"p": cm[:, 5, :], "s": cm[:, 6, :]}
        Msk = {"p": cm[:, 7, :], "s": cm[:, 8, :]}
        selj = cm[:, 9, 0:16]
        nmp = vec[:, 0, :]
        nfp = vec[:, 1, :]
        lng = vec[:, 2, :]
        lnb = vec[:, 3, :]
        snw = vec[:, 4, :]
        dtb = h32[:, 0, :]
        alog = h32[:, 1, :]
        dsk = h32[:, 2, :]

        def bank(i, n=512):
            return ps[:, i, 0:n]

        K.dma(cm[:, :, :], cmat_d[:, :, :], "const")
        K.dma(vec[:, :, :], vecs[:, :, :], "const")
        K.dma(scw[:, :, :], scw_d[:, :, :], "const")
        K.dma(fcw[:, :, :], fcw_d[:, :, :], "const")
        K.dma(h32[:, :, :], h32_d[:, :, :], "const")
        K.dma(flg[:, :], flag[:, :], "const")
        K.cp(identb[:, :], identf)
        for kd in ("p", "s"):
            K.cp(Ub[kd][:, :], Umat[kd])
            K.cp(Mnb[kd][:, :], Mneg[kd])
        K.ms(onesf[:, :], 1.0)
        K.ms(onesb[:, :], 1.0)
        K.act(arep[:, :], alog, AF.Exp)
        K.ts(arep[:, :], arep[:, :], -1.0, None, ALU.mult)
        K.ms(St[:, :], 0.0)
        K.ms(Sb[:, :], 0.0)
        K.ms(hist_s[:, :, :], 0.0)
        K.ms(hist_f[:, :, :], 0.0)

        wq = []

        def wdesc_cols(wd, c0, ncols, nk=16, r0=0):
            return ("c", wd, c0, ncols, nk, r0)

        class WS:
            def __init__(self):
                self.seq = []
                self.seq_phase = []
                self.plan_ph = 0
                self.exec_ph = 0
                self.issued = 0
                self.pos = 0
                self.slots = [(wslot[i], "b%d" % i) for i in range(NSLOT)]
                self.extras = []
                self.last = {}
                self.assigned = {}

            def plan_phase(self):
                self.plan_ph += 1

            def plan(self, d):
                self.seq.append(d)
                self.seq_phase.append(self.plan_ph)

            def begin_phase(self, extras=()):
                self.exec_ph += 1
                self.extras = [(t, "x%d" % i) for i, t in enumerate(extras)]

            def end_phase(self):
                while self.issued < len(self.seq) and self.issued - self.pos < 4:
                    if self.seq_phase[self.issued] == self.exec_ph:
                        break
                    if not self._issue(self.issued):
                        break
                    self.issued += 1
                self.extras = []

            def _issue(self, j):
                d = self.seq[j]
                cands = list(self.slots)
                if self.seq_phase[j] == self.exec_ph:
                    cands += self.extras
                best = None
                for (t, nm) in cands:
                    lb = self.last.get(nm, -1)
                    if lb < self.pos and (best is None or lb < best[2]):
                        best = (t, nm, lb)
                if best is None:
                    return False
                slot, nm, _ = best
                self.last[nm] = j
                self.assigned[j] = slot
                off = 0
                for pi, (wd, c0, ncols, nk, r0) in enumerate(d):
                    dst = slot[:, off:off + nk * ncols].rearrange("p (k n) -> p k n", k=nk)
                    src = wd[r0:r0 + nk * 128, c0:c0 + ncols].rearrange("(k p) n -> p k n", p=128)
                    K.dma(dst, src, "w%s_%d" % (nm, pi), eng="pool")
                    off += nk * ncols
                return True

            def get(self, d):
                i = self.pos
                assert self.seq[i] == d, (i, self.seq[i], d)
                assert self.seq_phase[i] == self.exec_ph, (i, self.seq_phase[i], self.exec_ph)
                while self.issued <= i:
                    ok = self._issue(self.issued)
                    assert ok
                    self.issued += 1
                while self.issued < len(self.seq) and self.issued - i <= 8:
                    if not self._issue(self.issued):
                        break
                    self.issued += 1
                self.pos += 1
                slot = self.assigned.pop(i)
                views = []
                off = 0
                for (wd, c0, ncols, nk, r0) in d:
                    views.append(slot[:, off:off + nk * ncols].rearrange("p (k n) -> p k n", k=nk))
                    off += nk * ncols
                return views

        ws = WS()

        def alloc_extras(pb, maxn):
            n = max(0, min(maxn, (nc.sbuf_bytes_remaining - 3072) // 16384))
            return [pb("wx%d" % i, [128, 8192], BF16) for i in range(n)]

        def D_in(c0, n):
            return [(w_in, c0, n, 16, 0)]

        def plan_lite():
            ws.plan_phase()
            for b in range(5):
                ws.plan(D_in(6144 + 512 * b, 512))
            ws.plan(D_in(9216, 32))

        def plan_full():
            ws.plan_phase()
            for b in range(4):
                ws.plan(D_in(2048 + 512 * b, 512))
            for b in range(4):
                ws.plan(D_in(512 * b, 512))
            ws.plan_phase()
            for b in range(4):
                ws.plan(D_in(4096 + 512 * b, 512))
            for b in range(6):
                ws.plan(D_in(6144 + 512 * b, 512))
            ws.plan(D_in(9216, 32))
            ws.plan_phase()
            for b in range(8):
                ws.plan([(w_out, 256 * b, 256, 32, 0)])
            ws.plan_phase()
            for b in range(22):
                ws.plan([(w_up, 256 * b, 256, 16, 0), (w_up, DFF + 256 * b, 256, 16, 0)])
            ws.plan_phase()
            for b in range(8):
                for kh in range(2):
                    ws.plan([(w_down, 256 * b, 256, 22, kh * 22 * 128)])

        def ptile(src, row, dst=None, xm_row=0):
            return dict(kind="p", src=src[row * 128:(row + 1) * 128, :], dst=dst, xm=xmid[xm_row * 128:(xm_row + 1) * 128, :])

        lite_groups = [[ptile(xpre, i) for i in range(0, 4)], [ptile(xpre, i) for i in range(4, 7)]]
        full_groups = [
            [ptile(xmain, i, ymain[i * 128:(i + 1) * 128, :], i) for i in range(0, 4)],
            [ptile(xmain, i, ymain[i * 128:(i + 1) * 128, :], i) for i in range(4, 8)],
            [ptile(xmain, i, ymain[i * 128:(i + 1) * 128, :], i) for i in range(8, 9)]
            + [dict(kind="s", src=xsamp[:, :], dst=ysamp[:, :], xm=xmid[9 * 128:10 * 128, :])],
        ]
        full_groups[2][0]["last"] = True
        for g in lite_groups:
            plan_lite()
        for g in full_groups:
            plan_full()

        cnt = {"x": 0, "o": 0}

        def rms_to_T(src_tile_ap, dstT, col0, wvec, xb, xsb, from_sb=None):
            junk = xsb
            ssq = small[:, 0:1]
            K.act(junk[:, :], xb[:, :], AF.Square, accum=ssq)
            rstd = small[:, 1:2]
            K.rsqrt(rstd, ssq, 1.0 / D, EPS)
            K.ts(xsb[:, :], xb[:, :], rstd, None, ALU.mult)
            for half in range(2):
                for c in range(8):
                    cc = half * 8 + c
                    K.tr(psb[:, half, c * 128:(c + 1) * 128], xsb[:, cc * 128:(cc + 1) * 128], identb[:, :])
                K.tt(dstT[:, half * 8:half * 8 + 8, col0:col0 + 128],
                     psb[:, half, :].rearrange("p (c t) -> p c t", t=128),
                     mk(wvec[:, half * 8:half * 8 + 8], [(1, 8), (0, 128)]), ALU.mult)

        def phase0(tiles):
            with nc.sbuf_tensor(un("p0x0"), [128, D], F32) as x0, nc.sbuf_tensor(un("p0x1"), [128, D], F32) as x1, \
                    nc.sbuf_tensor(un("p0xs"), [128, D], BF16) as xsb:
                xbs = [x0, x1]
                for i, t in enumerate(tiles):
                    xb = xbs[i % 2]
                    K.dma(xb[:, :], t["src"], "x%d" % (i % 2))
                    rms_to_T(None, hT, i * 128, nmp, xb, xsb)
                S.barrier("phase0")

        def ssd_inproj(tiles, full, L):
            T = len(tiles)
            NT = 128 * T
            Tp = sum(1 for t in tiles if t["kind"] == "p")
            NTp = 128 * Tp
            has_s = Tp < T
            raw, raws, acc, cvo = L["raw"], L["raws"], L["acc"], L["cvo"]
            def mmpair(p, W):
                pp = p % 2
                for c in (2 * p, 2 * p + 1):
                    cc = c % 4
                    bk = bank(2 * pp + (c % 2), NT)
                    for k in range(16):
                        K.mm(bk, W[:, k, cc * 128:(cc + 1) * 128], hT[:, k, 0:NT], start=(k == 0), stop=(k == 15))

            def postpair(p):
                pp = p % 2
                cs_ = [2 * p, 2 * p + 1]
                for c in cs_:
                    r = raw[pp][c % 2]
                    bi = 2 * pp + (c % 2)
                    if NTp > 0:
                        K.cp(r[:, 0:3], hist_s[:, c, :])
                        K.cp(r[:, 3:3 + NTp], bank(bi, NTp), eng="act")
                        K.cp(hist_s[:, c, :], r[:, NTp:NTp + 3])
                    if has_s:
                        rs = raws[c % 2]
                        K.cp(rs[:, :, 0:3], L["sconv"][:, c, :].rearrange("p (j k) -> p j k", k=3))
                        K.cp(rs[:, :, 3:11], ps[:, bi, NTp:NTp + 128].rearrange("p (j t) -> p j t", t=8), eng="act")
                        K.cp(L["osc"][:, c, :].rearrange("p (j k) -> p j k", k=3), rs[:, :, 8:11])
                for kk in range(4):
                    for c in cs_:
                        r = raw[pp][c % 2]
                        if NTp > 0:
                            a = acc[pp][c % 2][:, 0:NTp]
                            if kk == 0:
                                K.ts(a, r[:, 0:NTp], scw[:, c, 0:1], scw[:, c, 4:5], ALU.mult, ALU.add)
                            else:
                                K.stt(a, r[:, kk:kk + NTp], scw[:, c, kk:kk + 1], a, ALU.mult, ALU.add)
                        if has_s:
                            rs = raws[c % 2]
                            a3 = acc[pp][c % 2][:, NTp:NTp + 128].rearrange("p (j t) -> p j t", t=8)
                            if kk == 0:
                                K.ts(a3, rs[:, :, 0:8], scw[:, c, 0:1], scw[:, c, 4:5], ALU.mult, ALU.add)
                            else:
                                K.stt(a3, rs[:, :, kk:kk + 8], scw[:, c, kk:kk + 1], a3, ALU.mult, ALU.add)
                for c in cs_:
                    av = acc[pp][c % 2][:, 0:NT]
                    if c < 20:
                        dstv = cvo[pp][c % 2][:, 0:NT] if c < 16 else L["BT"][:, c - 16, 0:NT]
                        K.act(dstv, av, AF.Silu)
                        hb = c % 2
                        for i in range(T):
                            K.tr(psb[:, hb, i * 128:(i + 1) * 128], dstv[:, i * 128:(i + 1) * 128], identb[:, :])
                        if c < 16:
                            dst3 = mk(L["xs"][:, 0, c * 128:(c + 1) * 128], [(2048, T), (1, 128)])
                        else:
                            dst3 = mk(L["Bt"][:, 0, (c - 16) * 128:(c - 15) * 128], [(512, T), (1, 128)])
                        K.cp(dst3, psb[:, hb, 0:NT].rearrange("p (i t) -> p i t", t=128))
                    else:
                        K.act(L["CT"][:, c - 20, 0:NT], av, AF.Silu)

            Wcur = None
            npair = 12 if full else 10
            for p in range(npair):
                if p % 2 == 0:
                    (Wcur,) = ws.get(D_in(6144 + 512 * (p // 2), 512))
                mmpair(p, Wcur)
                if p >= 1:
                    postpair(p - 1)
            postpair(npair - 1)
            (Wd,) = ws.get(D_in(9216, 32))
            for i in range(T):
                bk = ps[:, 2, 0:32]
                for k in range(16):
                    K.mm(bk, hT[:, k, i * 128:(i + 1) * 128], Wd[:, k, :], start=(k == 0), stop=(k == 15))
                K.tt(L["dtr"][:, i, :], bk, dtb, ALU.add)

        def ssd_scalars(tiles, L):
            T = len(tiles)
            Tp = sum(1 for t in tiles if t["kind"] == "p")
            sm = L["sm"]
            fl = lambda f: sm[:, f, :, :].rearrange("p a b -> p (a b)")
            e1, dt, da, ecs, ncs, dte, etot = (fl(6), fl(0), fl(1), fl(2), fl(3), fl(4), fl(5))
            K.act(e1, L["dtr"][:, 0:T, :].rearrange("p a b -> p (a b)"), AF.Exp)
            K.act(dt, e1, AF.Ln, bias=1.0)
            K.tt(sm[:, 1, :, :], sm[:, 0, :, :], mk(arep[:, :], [(0, T), (1, 32)]), ALU.mult)
            bc = ps[:, 0, :]
            n = 32 * T
            if Tp > 0:
                K.mm(bc[:, 0:32 * Tp], Umat["p"], da[:, 0:32 * Tp])
                K.mm(bc[:, 256:256 + 32 * Tp], Tmat["p"], da[:, 0:32 * Tp])
            if Tp < T:
                K.mm(bc[:, 32 * Tp:n], Umat["s"], da[:, 32 * Tp:n])
                K.mm(bc[:, 256 + 32 * Tp:256 + n], Tmat["s"], da[:, 32 * Tp:n])
            K.act(ncs, bc[:, 0:n], AF.Identity, scale=-1.0)
            K.act(ecs, bc[:, 0:n], AF.Exp)
            K.act(etot, bc[:, 256:256 + n], AF.Identity)
            K.tt(dte, etot, ncs, ALU.add)
            K.act(dte, dte, AF.Exp)
            K.act(etot, etot, AF.Exp)
            dah = L["dah"]
            hi = dah[:, 0, :, :].rearrange("p a b -> p (a b)")
            lo = dah[:, 1, :, :].rearrange("p a b -> p (a b)")
            K.cp(hi, da)
            K.tt(e1, da, hi, ALU.subtract)
            K.cp(lo, e1)

        def ssd_tile(i, t, full, L):
            kind = t["kind"]
            sm = L["sm"]
            dt = sm[:, 0, i, :]
            da = sm[:, 1, i, :]
            ecs = sm[:, 2, i, :]
            ncs = sm[:, 3, i, :]
            dte = sm[:, 4, i, :]
            etot = sm[:, 5, i, :]
            dah = L["dah"]
            xs_i = L["xs"][:, i, :]
            xs3 = xs_i.rearrange("p (h q) -> p h q", q=64)

            def b3(v):
                return mk(v, [(1, 32), (0, 64)])

            xdt, xdtd = L["xdt"], L["xdtd"]
            K.tt(xdt[:, :].rearrange("p (h q) -> p h q", q=64), xs3, b3(dt), ALU.mult, eng=PENG)
            K.tt(xdtd[:, :].rearrange("p (h q) -> p h q", q=64), xdt[:, :].rearrange("p (h q) -> p h q", q=64),
                 b3(dte), ALU.mult, eng=PENG)
            tc0 = i * 128
            ysb = L.get("ysb")
            if full:
                bcb = ps[:, 1, :]
                for g in range(4):
                    K.mm(bcb[:, g * 128:(g + 1) * 128], L["BT"][:, g, tc0:tc0 + 128], L["CT"][:, g, tc0:tc0 + 128])
                cbT = L["cbT"]
                K.cp(cbT[:, :], bcb, eng="act")
                if kind == "p":
                    for g in range(4):
                        bo = bank(2 + (g % 2))
                        K.mm(bo, L["CT"][:, g, tc0:tc0 + 128], Sb[:, g * 512:(g + 1) * 512])
                        K.tt(ysb[:, g * 512:(g + 1) * 512].rearrange("p (h q) -> p h q", q=64),
                             bo.rearrange("p (h q) -> p h q", q=64),
                             mk(ecs[:, g * 8:(g + 1) * 8], [(1, 8), (0, 64)]), ALU.mult)
                else:
                    sample_states(i, t, L, da, ecs)
                def stA(hq):
                    q = hq % 2
                    rh, rl = L["rhsU"][q]
                    K.tt(rh[:, :, :], mk(Ub[kind][:, :], [(0, 4), (1, 128)]),
                         mk(dah[:, 0, i, hq * 4:hq * 4 + 4], [(1, 4), (0, 128)]), ALU.mult, eng=PENG)
                    K.tt(rl[:, :, :], mk(Ub[kind][:, :], [(0, 4), (1, 128)]),
                         mk(dah[:, 1, i, hq * 4:hq * 4 + 4], [(1, 4), (0, 128)]), ALU.mult, eng=PENG)
                    bR = bank(4 + q)
                    K.mm(bR, onesb[:, :], rh[:, :, :].rearrange("p a b -> p (a b)"), start=True, stop=False)
                    K.mm(bR, onesb[:, :], rl[:, :, :].rearrange("p a b -> p (a b)"), start=False, stop=False)
                    K.mm(bR, identb[:, :], mk(Mnb[kind][:, :], [(0, 4), (1, 128)]), start=False, stop=True)

                def stB(hq):
                    q = hq % 2
                    bR = bank(4 + q)
                    df = L["Eh"][q]
                    K.tt(df[:, :, :], bR.rearrange("p (a b) -> p a b", b=128),
                         mk(ncs[:, hq * 4:hq * 4 + 4], [(1, 4), (0, 128)]), ALU.add)
                    E4 = L["E4"][q]
                    K.act(E4[:, :, :].rearrange("p a b -> p (a b)"), df[:, :, :].rearrange("p a b -> p (a b)"), AF.Exp)

                def stC(hq):
                    q = hq % 2
                    g = hq // 2
                    E4 = L["E4"][q]
                    L4 = L["Lh"][q]
                    K.tt(L4[:, :, :], E4[:, :, :], mk(cbT[:, g * 128:(g + 1) * 128], [(0, 4), (1, 128)]), ALU.mult)
                    for hh in range(4):
                        h = hq * 4 + hh
                        byd = ps[:, 2 + (g % 2), (h % 8) * 64:(h % 8) * 64 + 64]
                        K.mm(byd, L4[:, hh, :], xdt[:, h * 64:(h + 1) * 64])
                    if hq % 2 == 1:
                        K.tt(ysb[:, g * 512:(g + 1) * 512], ysb[:, g * 512:(g + 1) * 512], bank(2 + (g % 2)), ALU.add)

                for k in range(10):
                    if k < 8:
                        stA(k)
                    if 0 <= k - 1 < 8:
                        stB(k - 1)
                    if 0 <= k - 2 < 8:
                        stC(k - 2)
            if True:
                if kind == "p":
                    K.tt(St[:, :].rearrange("p (h q) -> p h q", q=64), St[:, :].rearrange("p (h q) -> p h q", q=64),
                         b3(etot), ALU.mult)
                    for g in range(4):
                        bu = bank(g % 2)
                        K.mm(bu, L["Bt"][:, i, g * 128:(g + 1) * 128], xdtd[:, g * 512:(g + 1) * 512])
                        K.tt(St[:, g * 512:(g + 1) * 512], St[:, g * 512:(g + 1) * 512], bu, ALU.add)
                    K.cp(Sb[:, :], St[:, :], eng="act")

            if full:
                tmp = L["tmp"]
                K.tt(tmp[:, :].rearrange("p (h q) -> p h q", q=64), xs3, b3(dsk), ALU.mult, eng=PENG)
                K.tt(ysb[:, :], ysb[:, :], tmp[:, :], ALU.add, eng=PENG)
                K.tt(ysb[:, :], ysb[:, :], L["sz"][:, i, :], ALU.mult, eng=PENG)
                ssq4 = sm[:, 7, i, 0:4]
                for g in range(4):
                    K.act(tmp[:, g * 512:(g + 1) * 512], ysb[:, g * 512:(g + 1) * 512], AF.Square,
                          accum=ssq4[:, g:g + 1])
                r4 = sm[:, 7, i, 4:8]
                K.rsqrt(r4, ssq4, 1.0 / 512, EPS)
                ynb = L["ynb"]
                for g in range(4):
                    K.act(ynb[:, g * 512:(g + 1) * 512], ysb[:, g * 512:(g + 1) * 512], AF.Identity, scale=r4[:, g:g + 1])
                for half in range(2):
                    for c in range(8):
                        cc = half * 8 + c
                        K.tr(psb[:, half, c * 128:(c + 1) * 128], ynb[:, cc * 128:(cc + 1) * 128], identb[:, :])
                    for c in range(8):
                        cc = half * 8 + c
                        K.act(BIG["mixT"][:, 16 + cc, tc0:tc0 + 128], psb[:, half, c * 128:(c + 1) * 128], AF.Identity,
                              scale=snw[:, cc:cc + 1])
        def sample_states(i, t, L, da, ecs):
            tc0 = i * 128
            sm = L["sm"]
            ysb = L.get("ysb")
            xdtd = L["xdtd"]
            daj = L["daj"]
            K.tt(daj[:, :, :], mk(da, [(0, 16), (1, 32)]), mk(selj, [(1, 16), (0, 32)]), ALU.mult)
            bE = bank(1)
            K.mm(bE, onesf[:, :], daj[:, :, :].rearrange("p a b -> p (a b)"))
            Ej = L["Ej"]
            K.act(Ej[:, :], bE, AF.Exp)
            yoT = ps[:, 2:6, :].rearrange("p a b -> p (a b)")
            for j0 in range(2):
                K.dma(L["sj"][j0][:, :], sstT[j0, :, :], "sj%d" % j0)
            for j in range(16):
                sj = L["sj"][j % 2]
                sjb = L["sjb"][j % 2]
                K.cp(sjb[:, :], sj[:, :], eng="act")
                for b in range(16):
                    g = b // 4
                    K.mm(yoT[:, b * 128 + j * 8:b * 128 + j * 8 + 8], sjb[:, b * 128:(b + 1) * 128],
                         L["CT"][:, g, tc0 + j * 8:tc0 + j * 8 + 8])
                Btm = L["Btm"][j % 2]
                K.ts(Btm[:, :], L["Bt"][:, i, :], selj[:, j:j + 1], None, ALU.mult)
                sn = L["sn"][j % 2]
                K.tt(sn[:, :].rearrange("p (h q) -> p h q", q=64), sj[:, :].rearrange("p (h q) -> p h q", q=64),
                     mk(Ej[:, j * 32:(j + 1) * 32], [(1, 32), (0, 64)]), ALU.mult)
                for g in range(4):
                    bu = bank(g % 2)
                    K.mm(bu, Btm[:, g * 128:(g + 1) * 128], xdtd[:, g * 512:(g + 1) * 512])
                    K.tt(sn[:, g * 512:(g + 1) * 512], sn[:, g * 512:(g + 1) * 512], bu, ALU.add)
                K.dma(o_sst_s[j, :, :], sn[:, :], "sn%d" % (j % 2))
                if j + 2 < 16:
                    K.dma(sj[:, :], sstT[j + 2, :, :], "sj%d" % (j % 2))
            yoS = L["tmp"]
            K.cp(yoS[:, :], yoT, eng="act")
            for b in range(16):
                bk = ps[:, 2 + (b // 4), (b % 4) * 128:(b % 4) * 128 + 128]
                K.mm(bk, yoS[:, b * 128:(b + 1) * 128], identf)
            for g in range(4):
                bo = bank(2 + g)
                K.cp(ysb[:, g * 512:(g + 1) * 512], bo, eng="act")
                K.tt(ysb[:, g * 512:(g + 1) * 512].rearrange("p (h q) -> p h q", q=64),
                     ysb[:, g * 512:(g + 1) * 512].rearrange("p (h q) -> p h q", q=64),
                     mk(ecs[:, g * 8:(g + 1) * 8], [(1, 8), (0, 64)]), ALU.mult)

        def ssd_phase(tiles, full):
            T = len(tiles)
            NT = 128 * T
            has_s = any(t["kind"] == "s" for t in tiles)
            with contextlib.ExitStack() as ph:
                def pb(name, shape, dt=F32):
                    return ph.enter_context(nc.sbuf_tensor(un(name), list(shape), dt))

                L = {}
                L["xs"] = pb("xs", [128, T, 2048], BF16)
                L["BT"] = pb("BT", [128, 4, 512], BF16)
                L["CT"] = pb("CT", [128, 4, 512], BF16)
                L["Bt"] = pb("Bt", [128, 4, 512], BF16)
                L["dtr"] = pb("dtr", [128, 4, 32])
                L["raw"] = [[pb("raw%d%d" % (pp, q), [128, 3 + 512]) for q in range(2)] for pp in range(2)]
                L["raws"] = [pb("raws%d" % q, [128, 16, 11]) for q in range(2)]
                L["acc"] = [[pb("acc%d%d" % (pp, q), [128, 512]) for q in range(2)] for pp in range(2)]
                L["cvo"] = [[pb("cvo%d%d" % (pp, q), [128, 512], BF16) for q in range(2)] for pp in range(2)]
                L["sm"] = pb("sm", [128, 8, T, 32])
                L["dah"] = pb("dah", [128, 2, T, 32], BF16)
                L["xdt"] = pb("xdt", [128, 2048], BF16)
                L["xdtd"] = pb("xdtd", [128, 2048], BF16)
                if full:
                    L["sz"] = pb("sz", [128, T, 2048], BF16)
                    L["ysb"] = pb("ysb", [128, 2048])
                    L["tmp"] = pb("tmp", [128, 2048], F32 if has_s else BF16)
                    L["ynb"] = L["xdtd"]
                    L["cbT"] = pb("cbT", [128, 512], BF16)
                    L["rhsU"] = [(pb("rhsUh%d" % q, [128, 4, 128], BF16), pb("rhsUl%d" % q, [128, 4, 128], BF16)) for q in range(2)]
                    L["Eh"] = [pb("Eh%d" % q, [128, 4, 128]) for q in range(2)]
                    L["E4"] = [pb("E4%d" % q, [128, 4, 128], BF16) for q in range(2)]
                    L["Lh"] = [pb("Lh%d" % q, [128, 4, 128], BF16) for q in range(2)]
                if has_s:
                    L["sconv"] = pb("sconv", [128, 24, 48])
                    L["osc"] = L["sconv"]
                    L["daj"] = pb("daj", [128, 16, 32])
                    L["Ej"] = pb("Ej", [128, 512])
                    L["sj"] = [pb("sj%d" % q, [128, 2048]) for q in range(2)]
                    L["sjb"] = [pb("sjb0", [128, 2048], BF16)] * 2
                    L["sn"] = L["sj"]
                    L["Btm"] = [pb("Btm%d" % q, [128, 512], BF16) for q in range(2)]
                    K.dma(L["sconv"][:, :, :], sconvT[:, :, :], "m1")
                ws.begin_phase(alloc_extras(pb, 4))
                if full:
                    for b in range(4):
                        (W,) = ws.get(D_in(4096 + 512 * b, 512))
                        for i in range(T):
                            bk = bank(2 + (i % 4))
                            for k in range(16):
                                K.mm(bk, hT[:, k, i * 128:(i + 1) * 128], W[:, k, :], start=(k == 0), stop=(k == 15))
                            K.act(L["sz"][:, i, b * 512:(b + 1) * 512], bk, AF.Silu)
                S.mark("  s.z")
                ssd_inproj(tiles, full, L)
                S.mark("  s.inproj")
                ssd_scalars(tiles, L)
                for i, t in enumerate(tiles):
                    ssd_tile(i, t, full, L)
                    S.mark("  s.tile")
                    if dbg_stop < 10 ** 8:
                        K.dma(ysamp[:, i * 224:(i + 1) * 224].rearrange("p (a b) -> p a b", b=32), L["sm"][:, 0:7, i, :], "dbg2")
                    if t.get("last"):
                        K.dma(o_sconv_p[:, :, :], hist_s[:, :, :], "m2")
                        K.dma(o_sst_p[:, :], St[:, :], "m3")
                if has_s:
                    K.dma(o_sconv_s[:, :, :], L["osc"][:, :, :], "m4")
                if dbg_stop < 10 ** 8:
                    K.dma(o_sst_p[:, :], St[:, :], "dbg1")
                    K.dma(ysamp[:, 1024:1024 + 32 * T], L["dtr"][:, 0:T, :].rearrange("p a b -> p (a b)"), "dbg3")
                    K.dma(ysamp[:, 384:384 + 512], L["Bt"][:, 0, :], "dbg4") if False else None
                ws.end_phase()
                S.barrier("ssd")

        STG = [None]

        def nrmf_stage(pb):
            if STG[0] is None:
                STG[0] = pb("nrmf", [128, 2048])
            return STG[0]

        def gate_phase(tiles):
            STG[0] = None
            T = len(tiles)
            NT = 128 * T
            Tp = sum(1 for t in tiles if t["kind"] == "p")
            NTp = 128 * Tp
            has_s = Tp < T
            with contextlib.ExitStack() as ph:
                def pb(name, shape, dt=F32):
                    return ph.enter_context(nc.sbuf_tensor(un(name), list(shape), dt))

                gv = pb("gv", [128, T, 2048], BF16)
                nrm = gv
                gsum = pb("gsum", [128, 4, 8])
                wsT["s"] = None
                wsT["p"] = pb("wsTp", [128, 16, 128], BF16)
                K.dma(nrmf_stage(pb)[:, :], wsTp_d.rearrange("p a b -> p (a b)"), "mwstp")
                K.tt(wsT["p"][:, :, :], mk(STG[0][:, :], [(128, 16), (1, 128)]),
                     mk(Msk["p"], [(0, 16), (1, 128)]), ALU.mult)
                if has_s:
                    wsT["s"] = pb("wsTs", [128, 16, 128], BF16)
                    K.dma(nrmf_stage(pb)[:, :], wsTs_d.rearrange("p a b -> p (a b)"), "mwsts")
                    K.tt(wsT["s"][:, :, :], mk(STG[0][:, :], [(128, 16), (1, 128)]),
                         mk(Msk["s"], [(0, 16), (1, 128)]), ALU.mult)
                st = pb("gst", [128, 8])
                gu = pb("gu", [128, 512])
                t2 = pb("t2", [128, 512])
                Qs = {"p": pb("Qsp", [128, 128]), "s": pb("Qss", [128, 128])}
                bsr = {"p": pb("bsrp", [128, 2048])}
                K.dma(bsr["p"][:, :], bsp_d[:, :], "m5")
                if has_s:
                    bsr["s"] = pb("bsrs", [128, 2048])
                    K.dma(bsr["s"][:, :], bss_d[:, :], "m6")
                junk = pb("gjunk", [128, 512], BF16)
                nrmf = nrmf_stage(pb)
                cvT = pb("cvT", [128, 16, 128])
                ws.begin_phase(alloc_extras(pb, 3))
                for b in range(4):
                    (W,) = ws.get(D_in(2048 + 512 * b, 512))
                    for i in range(T):
                        bk = bank(i % 4)
                        for k in range(16):
                            K.mm(bk, hT[:, k, i * 128:(i + 1) * 128], W[:, k, :], start=(k == 0), stop=(k == 15))
                        K.act(gv[:, i, b * 512:(b + 1) * 512], bk, AF.Gelu, accum=gsum[:, i, b:b + 1])
                        K.act(junk[:, :], gv[:, i, b * 512:(b + 1) * 512], AF.Square, accum=gsum[:, i, 4 + b:5 + b])
                for i, t in enumerate(tiles):
                    K.tt(st[:, 5:6], gsum[:, i, 4:5], gsum[:, i, 5:6], ALU.add)
                    K.tt(st[:, 6:7], gsum[:, i, 6:7], gsum[:, i, 7:8], ALU.add)
                    K.tt(st[:, 0:1], st[:, 5:6], st[:, 6:7], ALU.add)
                    K.tt(st[:, 1:2], gsum[:, i, 0:1], gsum[:, i, 1:2], ALU.add)
                    K.tt(st[:, 2:3], gsum[:, i, 2:3], gsum[:, i, 3:4], ALU.add)
                    K.tt(st[:, 1:2], st[:, 1:2], st[:, 2:3], ALU.add)
                    K.ts(st[:, 1:2], st[:, 1:2], 1.0 / D, None, ALU.mult)
                    K.tt(st[:, 2:3], st[:, 1:2], st[:, 1:2], ALU.mult)
                    K.stt(st[:, 3:4], st[:, 0:1], 1.0 / D, st[:, 2:3], ALU.mult, ALU.subtract)
                    K.rsqrt(st[:, 3:4], st[:, 3:4], 1.0, EPS)
                    K.stt(st[:, 4:5], st[:, 1:2], -1.0, st[:, 3:4], ALU.mult, ALU.mult)
                    if t.get("last") or t["kind"] == "s":
                        K.act(nrmf[:, :], gv[:, i, :], AF.Identity, bias=st[:, 4:5], scale=st[:, 3:4])
                        for h in range(16):
                            bk = ps[:, 4 + (h % 2), 0:128]
                            K.mm(bk, nrmf[:, h * 128:(h + 1) * 128], identf)
                            K.act(cvT[:, h, :], bk, AF.Identity, bias=lnb[:, h:h + 1], scale=lng[:, h:h + 1])
                        K.dma((o_cv_p if t["kind"] == "p" else o_cv_s)[:, :, :], cvT[:, :, :], "m7")
                    K.act(nrm[:, i, :], gv[:, i, :], AF.Identity, bias=st[:, 4:5], scale=st[:, 3:4])
                for b in range(4):
                    (W,) = ws.get(D_in(512 * b, 512))
                    for hh in range(4):
                        h = b * 4 + hh
                        bu = bank(4, NT)
                        for k in range(16):
                            K.mm(bu, W[:, k, hh * 128:(hh + 1) * 128], hT[:, k, 0:NT], start=(k == 0), stop=(k == 15))
                        K.act(gu[:, 0:NT], bu, AF.Gelu)
                        bs_ = bank(5, NT)
                        for i, t in enumerate(tiles):
                            K.mm(bs_[:, i * 128:(i + 1) * 128], nrm[:, i, h * 128:(h + 1) * 128], wsT[t["kind"]][:, h, :])
                        kinds = ["p"] + (["s"] if has_s else [])
                        for qi, kd in enumerate(kinds):
                            bq = ps[:, 0, qi * 128:(qi + 1) * 128]
                            K.mm(bq, onesb[:, :], wsT[kd][:, h, :])
                            K.stt(Qs[kd][:, :], bq, lnb[:, h:h + 1], bsr[kd][:, h * 128:(h + 1) * 128], ALU.mult, ALU.add)
                        if Tp > 0:
                            K.stt(t2[:, 0:NTp].rearrange("p (i t) -> p i t", t=128),
                                  bs_[:, 0:NTp].rearrange("p (i t) -> p i t", t=128), lng[:, h:h + 1],
                                  mk(Qs["p"][:, :], [(0, Tp), (1, 128)]), ALU.mult, ALU.add)
                        if has_s:
                            K.stt(t2[:, NTp:NT], bs_[:, NTp:NT], lng[:, h:h + 1], Qs["s"][:, :], ALU.mult, ALU.add)
                        K.tt(BIG["mixT"][:, h, 0:NT], t2[:, 0:NT], gu[:, 0:NT], ALU.mult)
                ws.end_phase()
                S.barrier("gate")

        def wout_phase(tiles):
            T = len(tiles)
            with contextlib.ExitStack() as ph:
                def pb(name, shape, dt=F32):
                    return ph.enter_context(nc.sbuf_tensor(un(name), list(shape), dt))

                mo = pb("mo", [128, T, 2048])
                with contextlib.ExitStack() as ph2:
                    def pb2(name, shape, dt=F32):
                        return ph2.enter_context(nc.sbuf_tensor(un(name), list(shape), dt))
                    ws.begin_phase(alloc_extras(pb2, 4))
                    for b in range(8):
                        (W,) = ws.get([(w_out, 256 * b, 256, 32, 0)])
                        for i in range(T):
                            bk = bank(i % 4, 256)
                            for k in range(32):
                                K.mm(bk, BIG["mixT"][:, k, i * 128:(i + 1) * 128], W[:, k, :], start=(k == 0), stop=(k == 31))
                            K.cp(mo[:, i, b * 256:(b + 1) * 256], bk, eng="act")
                    ws.end_phase()
                    S.barrier("wout_mm")
                nmo = pb("nmo_sb", [128, 2048])
                xb = [pb("p3x%d" % q, [128, 2048]) for q in range(2)]
                xm = [pb("p3m%d" % q, [128, 2048]) for q in range(2)]
                xsb = pb("p3xs", [128, 2048], BF16)
                K.dma(nmo[:, :], nmo_d[:, :], "m8")
                for i0 in range(min(2, T)):
                    K.dma(xb[i0][:, :], tiles[i0]["src"], "x%d" % i0)
                for i, t in enumerate(tiles):
                    x = xb[i % 2]
                    ssq = small[:, 2:3]
                    K.act(xsb[:, :], mo[:, i, :], AF.Square, accum=ssq)
                    rstd = small[:, 3:4]
                    K.rsqrt(rstd, ssq, 1.0 / D, EPS)
                    m = xm[i % 2]
                    K.stt(m[:, :], mo[:, i, :], rstd, nmo[:, :], ALU.mult, ALU.mult)
                    K.tt(m[:, :], m[:, :], x[:, :], ALU.add)
                    K.dma(t["xm"], m[:, :], "xm%d" % (i % 2))
                    if i + 2 < T:
                        K.dma(x[:, :], tiles[i + 2]["src"], "x%d" % (i % 2))
                    rms_to_T(None, hT, i * 128, nfp, m, xsb)
                S.barrier("wout")

        def up_phase(tiles):
            T = len(tiles)
            NT = 128 * T
            Tp = sum(1 for t in tiles if t["kind"] == "p")
            NTp = 128 * Tp
            has_s = Tp < T
            is_last = any(t.get("last") for t in tiles)
            with contextlib.ExitStack() as ph:
                def pb(name, shape, dt=F32):
                    return ph.enter_context(nc.sbuf_tensor(un(name), list(shape), dt))

                raw = [[pb("fr%d%d" % (q, z), [128, 2 + 512]) for z in range(2)] for q in range(2)]
                acc = [[pb("fa%d%d" % (q, z), [128, 512]) for z in range(2)] for q in range(2)]
                gg = [pb("fgg%d" % q, [128, 512]) for q in range(2)]
                if has_s:
                    raws = [[pb("frs%d%d" % (q, z), [128, 16, 10]) for z in range(2)] for q in range(2)]
                    fcv = pb("fcv", [128, 88, 32])
                    ofs = fcv
                    K.dma(fcv[:, :, :], fconvT[:, :, :], "m9")
                ws.begin_phase(alloc_extras(pb, 4))
                for b in range(22):
                    (Wg, Wu) = ws.get([(w_up, 256 * b, 256, 16, 0), (w_up, DFF + 256 * b, 256, 16, 0)])
                    for cc in range(2):
                        c = b * 2 + cc
                        q = c % 2
                        for z, Wz in enumerate((Wg, Wu)):
                            bk = bank(2 * q + z, NT)
                            for k in range(16):
                                K.mm(bk, Wz[:, k, cc * 128:(cc + 1) * 128], hT[:, k, 0:NT], start=(k == 0), stop=(k == 15))
                        for z in range(2):
                            ch = c + 44 * z
                            if NTp > 0:
                                r = raw[q][z]
                                K.cp(r[:, 0:2], hist_f[:, ch, :])
                                K.cp(r[:, 2:2 + NTp], bank(2 * q + z, NTp), eng="act")
                                K.cp(hist_f[:, ch, :], r[:, NTp:NTp + 2])
                            if has_s:
                                rs = raws[q][z]
                                K.cp(rs[:, :, 0:2], fcv[:, ch, :].rearrange("p (j k) -> p j k", k=2))
                                K.cp(rs[:, :, 2:10], ps[:, 2 * q + z, NTp:NTp + 128].rearrange("p (j t) -> p j t", t=8),
                                     eng="act")
                                K.cp(ofs[:, ch, :].rearrange("p (j k) -> p j k", k=2), rs[:, :, 8:10])
                        for kk in range(3):
                            for z in range(2):
                                ch = c + 44 * z
                                if NTp > 0:
                                    r = raw[q][z]
                                    a = acc[q][z][:, 0:NTp]
                                    if kk == 0:
                                        K.ts(a, r[:, 0:NTp], fcw[:, ch, 0:1], fcw[:, ch, 3:4], ALU.mult, ALU.add)
                                    else:
                                        K.stt(a, r[:, kk:kk + NTp], fcw[:, ch, kk:kk + 1], a, ALU.mult, ALU.add)
                                if has_s:
                                    rs = raws[q][z]
                                    a3 = acc[q][z][:, NTp:NTp + 128].rearrange("p (j t) -> p j t", t=8)
                                    if kk == 0:
                                        K.ts(a3, rs[:, :, 0:8], fcw[:, ch, 0:1], fcw[:, ch, 3:4], ALU.mult, ALU.add)
                                    else:
                                        K.stt(a3, rs[:, :, kk:kk + 8], fcw[:, ch, kk:kk + 1], a3, ALU.mult, ALU.add)
                        K.act(gg[q][:, 0:NT], acc[q][0][:, 0:NT], AF.Gelu_apprx_tanh)
                        K.tt(BIG["actT"][:, c, 0:NT], gg[q][:, 0:NT], acc[q][1][:, 0:NT], ALU.mult)
                if is_last:
                    K.dma(o_fconv_p[:, :, :], hist_f[:, :, :], "m10")
                if has_s:
                    K.dma(o_fconv_s[:, :, :], ofs[:, :, :], "m11")
                ws.end_phase()
                S.barrier("up")

        def down_phase(tiles):
            T = len(tiles)

            def bankx(i):
                if i < 6:
                    return ps[:, i, 0:256]
                return psb[:, i - 6, :].bitcast(F32)[:, 0:256]

            with contextlib.ExitStack() as ph:
                def pb(name, shape, dt=F32):
                    return ph.enter_context(nc.sbuf_tensor(un(name), list(shape), dt))

                fo = pb("fo", [128, T, 2048])
                with contextlib.ExitStack() as ph2:
                    def pb2(name, shape, dt=F32):
                        return ph2.enter_context(nc.sbuf_tensor(un(name), list(shape), dt))
                    ws.begin_phase(alloc_extras(pb2, 4))
                    for b in range(8):
                        base = (b % 2) * 4 if T <= 4 else 0
                        for kh in range(2):
                            (W,) = ws.get([(w_down, 256 * b, 256, 22, kh * 22 * 128)])
                            for i in range(T):
                                bk = bankx(base + i)
                                for k in range(22):
                                    K.mm(bk, BIG["actT"][:, kh * 22 + k, i * 128:(i + 1) * 128], W[:, k, :],
                                         start=(kh == 0 and k == 0), stop=(kh == 1 and k == 21))
                        for i in range(T):
                            K.cp(fo[:, i, b * 256:(b + 1) * 256], bankx(base + i), eng="act")
                    ws.end_phase()
                    S.barrier("down_mm")
                nfo = pb("nfo_sb", [128, 2048])
                xm = [pb("p5m%d" % q, [128, 2048]) for q in range(2)]
                yo = [pb("p5y%d" % q, [128, 2048]) for q in range(2)]
                K.dma(nfo[:, :], nfo_d[:, :], "m12")
                for i0 in range(min(2, T)):
                    K.dma(xm[i0][:, :], tiles[i0]["xm"], "xr%d" % i0)
                for i, t in enumerate(tiles):
                    m = xm[i % 2]
                    ssq = small[:, 4:5]
                    y = yo[i % 2]
                    K.act(y[:, :], fo[:, i, :], AF.Square, accum=ssq)
                    rstd = small[:, 5:6]
                    K.rsqrt(rstd, ssq, 1.0 / D, EPS)
                    K.stt(y[:, :], fo[:, i, :], rstd, nfo[:, :], ALU.mult, ALU.mult)
                    K.tt(y[:, :], y[:, :], m[:, :], ALU.add)
                    K.dma(t["dst"], y[:, :], "yo%d" % (i % 2))
                    if i + 2 < T:
                        K.dma(m[:, :], tiles[i + 2]["xm"], "xr%d" % (i % 2))
                S.barrier("down")

        prog = []
        prog.append(lambda: S.barrier())
        for g in lite_groups:
            prog.append(lambda g=g: phase0(g))
            prog.append(lambda g=g: ssd_phase(g, full=False))

        def _maskstate():
            K.ts(St[:, :], St[:, :], flg[:, 0:1], None, ALU.mult)
            K.cp(Sb[:, :], St[:, :], eng="act")
        prog.append(_maskstate)
        for g in full_groups:
            Tg = len(g)
            prog.append(lambda g=g: phase0(g))

            def _mix(g=g, Tg=Tg):
                with nc.sbuf_tensor(un("mixT"), [128, 32, 128 * Tg], BF16) as mixT:
                    BIG["mixT"] = mixT
                    for fn in (gate_phase, lambda gg: ssd_phase(gg, full=True), wout_phase):
                        if DBG["n"] <= 0:
                            break
                        DBG["n"] -= 1
                        fn(g)

            def _ffn(g=g, Tg=Tg):
                with nc.sbuf_tensor(un("actT"), [128, 44, 128 * Tg], BF16) as actT:
                    BIG["actT"] = actT
                    for fn in (up_phase, down_phase):
                        if DBG["n"] <= 0:
                            break
                        DBG["n"] -= 1
                        fn(g)
            prog.append(_mix)
            prog.append(_ffn)
        DBG = {"n": dbg_stop}
        for fn in prog:
            if DBG["n"] <= 0:
                break
            if fn.__name__ in ("_mix", "_ffn"):
                fn()
            else:
                DBG["n"] -= 1
                fn()
        S.barrier()

        sem_names = ["eng_pe", "eng_act", "eng_dve", "eng_pool"] + ["ch_" + c for c in S.chan_count]
        sems = {}
        for n in sem_names:
            sems[n] = es.enter_context(nc.semaphore(n))
        with nc.Block() as block:
            S.emit(block, sems)
    return nc


_NC_CACHE = {}


def _consts():
    t = np.arange(128)
    ident = np.eye(128, dtype=np.float32)
    Up = (t[:, None] <= t[None, :]).astype(np.float32)
    same = (t[:, None] // 8) == (t[None, :] // 8)
    Us = (Up.astype(bool) & same).astype(np.float32)
    Tp = np.ones((128, 128), np.float32)
    Ts = same.astype(np.float32)
    mskp = Up.copy()
    msks = Us.copy()
    Mnp = np.where(mskp > 0, 0.0, NEG).astype(np.float32)
    Mns = np.where(msks > 0, 0.0, NEG).astype(np.float32)
    selj = np.zeros((128, 128), np.float32)
    selj[t, t // 8] = 1.0
    return np.stack([ident, Up, Us, Tp, Ts, Mnp, Mns, mskp, msks, selj], axis=1).astype(np.float32)


def kernel(x_prompt, x_sample, state_ssm, state_ssm_conv, state_ffn_conv,
           norm_mix_pre, w_in, gate_ln_g, gate_ln_b, gate_w_s, gate_b_s,
           ssm_conv_w, ssm_conv_b, ssm_dt_bias, ssm_a_log, ssm_d, ssm_norm_w,
           w_out, norm_mix_post, norm_ffn_pre, ffn_w_up, ffn_conv_w, ffn_conv_b,
           ffn_w_down, norm_ffn_post, _runner=None):
    f = np.float32
    A = lambda a: np.ascontiguousarray(np.asarray(a, dtype=f))
    x_prompt = A(x_prompt); x_sample = A(x_sample)

    def fm(v, n):
        return A(np.asarray(v, f).reshape(n, 128).T)

    vecs = np.stack([fm(norm_mix_pre[0], 16), fm(norm_ffn_pre[0], 16), fm(gate_ln_g[0], 16),
                     fm(gate_ln_b[0], 16), fm(ssm_norm_w[0], 16)], axis=1)
    rep = lambda v: A(np.broadcast_to(np.asarray(v, f).reshape(1, -1), (128, np.asarray(v).size)))
    ws = np.asarray(gate_w_s[0], f)
    wsTp = A(ws.transpose(2, 0, 1))
    wsTs = np.zeros((128, 16, 128), f)
    blk = ws[:, :8, :8].transpose(2, 0, 1)
    for j in range(16):
        wsTs[8 * j:8 * j + 8, :, 8 * j:8 * j + 8] = blk
    bs = np.asarray(gate_b_s[0], f)
    bsp = rep(bs.reshape(-1))
    bss = rep(np.tile(bs[:, :8], (1, 16)).reshape(-1))
    scw = np.concatenate([np.asarray(ssm_conv_w[0], f), np.asarray(ssm_conv_b[0], f)[None]], axis=0)
    scw = A(scw.reshape(5, 24, 128).transpose(2, 1, 0))
    fcw = np.concatenate([np.asarray(ffn_conv_w[0], f), np.asarray(ffn_conv_b[0], f)[None]], axis=0)
    fcw = A(fcw.reshape(4, 88, 128).transpose(2, 1, 0))
    h32 = np.stack([rep(ssm_dt_bias[0]), rep(ssm_a_log[0]), rep(ssm_d[0])], axis=1)
    cmat = _consts()
    common = dict(w_in=A(w_in[0]), w_out=A(w_out[0]), w_up=A(ffn_w_up[0]), w_down=A(ffn_w_down[0]),
                  vecs=A(vecs), nmo=rep(norm_mix_post[0]), nfo=rep(norm_ffn_post[0]), wsTp=wsTp, wsTs=wsTs,
                  bsp=bsp, bss=bss, scw=scw, fcw=fcw, h32=A(h32), cmat=cmat)
    in_maps = []
    for core in range(8):
        b = core // 2
        second = core % 2
        m = dict(common)
        if second == 0:
            m["xpre"] = np.zeros((NLITE * 128, D), f)
            m["xmain"] = A(x_prompt[b, 0:NMAIN * 128])
            m["flag"] = np.zeros((128, 1), f)
        else:
            m["xpre"] = A(x_prompt[b, 0:NLITE * 128])
            m["xmain"] = A(x_prompt[b, NLITE * 128:2048])
            m["flag"] = np.ones((128, 1), f)
        js = slice(16 * core, 16 * core + 16)
        m["xsamp"] = A(x_sample[js].reshape(128, D))
        m["sstT"] = A(np.asarray(state_ssm[0, js], f).reshape(16, 2048, 128).transpose(0, 2, 1))
        m["sconvT"] = A(np.asarray(state_ssm_conv[0, js], f).reshape(48, 24, 128).transpose(2, 1, 0))
        m["fconvT"] = A(np.asarray(state_ffn_conv[0, js], f).reshape(32, 88, 128).transpose(2, 1, 0))
        in_maps.append(m)
    if _runner is not None:
        R = _runner(in_maps)
    else:
        if "nc" not in _NC_CACHE:
            _NC_CACHE["nc"] = build()
        nc = _NC_CACHE["nc"]
        res = run_bass_kernel_spmd(nc, in_maps, core_ids=list(range(8)))
        R = res.results
    y_p = np.zeros((4, 2048, D), f)
    y_s = np.zeros((128, 8, D), f)
    sst_p = np.zeros((1, 4, 32, 64, 128), f)
    sconv_p = np.zeros((1, 4, 3, 3072), f)
    fconv_p = np.zeros((1, 4, 2, 11264), f)
    cv_p = np.zeros((1, 4, 128, 16, 128), f)
    sst_s = np.zeros((1, 128, 32, 64, 128), f)
    sconv_s = np.zeros((1, 128, 3, 3072), f)
    fconv_s = np.zeros((1, 128, 2, 11264), f)
    cv_s = np.zeros((1, 128, 8, 16, 128), f)
    for core in range(8):
        b = core // 2
        second = core % 2
        r = R[core]
        ym = np.asarray(r["ymain"]).reshape(NMAIN * 128, D)
        if second == 0:
            y_p[b, 0:NMAIN * 128] = ym
        else:
            y_p[b, NMAIN * 128:2048] = ym[(NMAIN * 128 - (2048 - NMAIN * 128)):]
            sst_p[0, b] = np.asarray(r["o_sst_p"]).reshape(128, 2048).T.reshape(32, 64, 128)
            sconv_p[0, b] = np.asarray(r["o_sconv_p"]).reshape(128, 24, 3).transpose(2, 1, 0).reshape(3, 3072)
            fconv_p[0, b] = np.asarray(r["o_fconv_p"]).reshape(128, 88, 2).transpose(2, 1, 0).reshape(2, 11264)
            cv_p[0, b] = np.asarray(r["o_cv_p"]).reshape(128, 16, 128).transpose(2, 1, 0)
        js = slice(16 * core, 16 * core + 16)
        y_s[js] = np.asarray(r["ysamp"]).reshape(16, 8, D)
        sst_s[0, js] = np.asarray(r["o_sst_s"]).reshape(16, 128, 2048).transpose(0, 2, 1).reshape(16, 32, 64, 128)
        sconv_s[0, js] = np.asarray(r["o_sconv_s"]).reshape(128, 24, 16, 3).transpose(2, 3, 1, 0).reshape(16, 3, 3072)
        fconv_s[0, js] = np.asarray(r["o_fconv_s"]).reshape(128, 88, 16, 2).transpose(2, 3, 1, 0).reshape(16, 2, 11264)
        cv_s[0, js] = np.asarray(r["o_cv_s"]).reshape(128, 16, 16, 8).transpose(2, 3, 1, 0)
    return (y_p, y_s, sst_p, sconv_p, fconv_p, cv_p, sst_s, sconv_s, fconv_s, cv_s)
```
